# Optimizing a Trainium2 kernel written in Bass

```python
import jax
import jax.numpy as jnp
from jax import lax
import numpy as np

D_MODEL = 1024
BATCH = 32
SEQ = 2048
DEPTH = 1
DEC_BATCH = 16
DEC_SEQ = 2048
PAST_LEN = 128

N_HEADS_GLA = 4
DV_GLA = D_MODEL // N_HEADS_GLA
DK_GLA = DV_GLA // 2
GLA_LOWRANK = 16
GLA_TAU = 16.0
N_HEADS_ML = 4
DV_ML = D_MODEL // N_HEADS_ML
DK_ML = DV_ML // 2
CONV_W = 3
D_FF = 2816
CHUNK = 64
EPS = 1e-6

GLA_QK = N_HEADS_GLA * DK_GLA
GLA_V = N_HEADS_GLA * DV_GLA
ML_QK = N_HEADS_ML * DK_ML
ML_V = N_HEADS_ML * DV_ML
IN_SIZES = (GLA_QK, GLA_QK, GLA_V, GLA_V, 2 * GLA_LOWRANK, 2 * ML_QK, ML_V, ML_V, 4 * N_HEADS_ML, D_MODEL, D_MODEL)
IN_OFFSETS = tuple(int(o) for o in np.cumsum(IN_SIZES)[:-1])
D_IN = int(sum(IN_SIZES))

kernel_name = "hybrid_gla_mlstm_macaron_encoder"


def rmsnorm(x, g):
    xf = x.astype(jnp.float32)
    y = xf * lax.rsqrt(jnp.mean(xf * xf, axis=-1, keepdims=True) + EPS)
    return (y * g.astype(jnp.float32)).astype(x.dtype)


def head_rmsnorm(o, g):
    B, L, H, Dv = o.shape
    y = o * lax.rsqrt(jnp.mean(o * o, axis=-1, keepdims=True) + EPS)
    return y.reshape(B, L, H * Dv) * g.astype(jnp.float32)


def swiglu(h, w_in, w_out):
    a, g = jnp.split(h @ w_in, [D_FF], axis=-1)
    return (jax.nn.silu(a) * g) @ w_out


def to_chunks(t):
    B, L, H = t.shape[:3]
    t = t.reshape((B, L // CHUNK, CHUNK, H) + t.shape[3:])
    return jnp.moveaxis(t, (1, 3), (0, 2))


def from_chunks(t):
    t = jnp.moveaxis(t, (0, 2), (1, 3))
    B, nc, C, H = t.shape[:4]
    return t.reshape((B, nc * C, H) + t.shape[4:])


def flip(t):
    return jnp.flip(t, axis=1)


def gla_scan(q, k, v, log_a):
    qc, kc, vc, gc = (to_chunks(t) for t in (q, k, v, log_a))
    b = jnp.cumsum(gc, axis=-2)
    b_last = b[..., -1, :]
    q_in = qc * jnp.exp(b)
    k_in = kc * jnp.exp(-b)
    k_st = kc * jnp.exp(b_last[..., None, :] - b)
    mask = jnp.tril(jnp.ones((CHUNK, CHUNK), dtype=bool))

    def step(S, xs):
        qi, ki, ks, vi, bl = xs
        att = jnp.where(mask, jnp.einsum('bhtd,bhsd->bhts', qi, ki), 0.0)
        o = jnp.einsum('bhts,bhsv->bhtv', att, vi) + jnp.einsum('bhtd,bhdv->bhtv', qi, S)
        S = S * jnp.exp(bl)[..., None] + jnp.einsum('bhsd,bhsv->bhdv', ks, vi)
        return S, o

    B, _, H, DK = q.shape
    S0 = jnp.zeros((B, H, DK, v.shape[-1]), jnp.float32)
    _, o = lax.scan(step, S0, (q_in, k_in, k_st, vc, b_last))
    return from_chunks(o)


def gla_mixer(q, k, v, r, lr, w2_f, b2_f, w2_b, b2_b, g_norm):
    B, L, _ = q.shape
    f32 = jnp.float32
    qh = q.astype(f32).reshape(B, L, N_HEADS_GLA, DK_GLA) * (DK_GLA ** -0.5)
    kh = k.astype(f32).reshape(B, L, N_HEADS_GLA, DK_GLA)
    vh = v.astype(f32).reshape(B, L, N_HEADS_GLA, DV_GLA)
    lr_f, lr_b = jnp.split(lr, [GLA_LOWRANK], axis=-1)
    la_f = jax.nn.log_sigmoid((lr_f @ w2_f + b2_f).astype(f32)).reshape(B, L, N_HEADS_GLA, DK_GLA) / GLA_TAU
    la_b = jax.nn.log_sigmoid((lr_b @ w2_b + b2_b).astype(f32)).reshape(B, L, N_HEADS_GLA, DK_GLA) / GLA_TAU
    o = gla_scan(qh, kh, vh, la_f) + flip(gla_scan(flip(qh), flip(kh), flip(vh), flip(la_b)))
    return head_rmsnorm(o, g_norm) * jax.nn.silu(r.astype(f32))


def mlstm_scan(q, k, v, ig, lf):
    qc, kc, vc, ic, fc = (to_chunks(t) for t in (q, k, v, ig, lf))
    b = jnp.cumsum(fc, axis=-1)
    mask = jnp.tril(jnp.ones((CHUNK, CHUNK), dtype=bool))

    def step(carry, xs):
        Cm, n, m = carry
        qi, ki, vi, ii, bi = xs
        logD = jnp.where(mask, bi[..., :, None] - bi[..., None, :] + ii[..., None, :], -jnp.inf)
        m_inter = bi + m[..., None]
        m_t = jnp.maximum(m_inter, jnp.max(logD, axis=-1))
        Dm = jnp.exp(logD - m_t[..., None])
        w_inter = jnp.exp(m_inter - m_t)
        s = jnp.einsum('bhtd,bhsd->bhts', qi, ki) * Dm
        num = jnp.einsum('bhts,bhsv->bhtv', s, vi) + w_inter[..., None] * jnp.einsum('bhtd,bhdv->bhtv', qi, Cm)
        den = jnp.sum(s, axis=-1) + w_inter * jnp.einsum('bhtd,bhd->bht', qi, n)
        h = num / jnp.maximum(jnp.abs(den), jnp.exp(-m_t))[..., None]
        b_L = bi[..., -1]
        log_w_st = b_L[..., None] - bi + ii
        m_new = jnp.maximum(b_L + m, jnp.max(log_w_st, axis=-1))
        w_st = jnp.exp(log_w_st - m_new[..., None])
        decay = jnp.exp(b_L + m - m_new)
        Cm = decay[..., None, None] * Cm + jnp.einsum('bhs,bhsd,bhsv->bhdv', w_st, ki, vi)
        n = decay[..., None] * n + jnp.einsum('bhs,bhsd->bhd', w_st, ki)
        return (Cm, n, m_new), h

    B, _, H, DK = q.shape
    init = (jnp.zeros((B, H, DK, v.shape[-1]), jnp.float32),
            jnp.zeros((B, H, DK), jnp.float32),
            jnp.full((B, H), -jnp.inf, jnp.float32))
    _, h = lax.scan(step, init, (qc, kc, vc, ic, b))
    return from_chunks(h)


def mlstm_mixer(qk, v, o, if_pre, conv_w, conv_b, b_if, g_norm):
    B, L, _ = qk.shape
    f32 = jnp.float32
    pad = CONV_W // 2
    qkp = jnp.pad(qk, ((0, 0), (pad, pad), (0, 0)))
    conv = conv_b
    for j in range(CONV_W):
        conv = conv + qkp[:, j:j + L] * conv_w[j]
    qk_c = jax.nn.silu(conv.astype(f32))
    q, k = jnp.split(qk_c, [ML_QK], axis=-1)
    qh = q.reshape(B, L, N_HEADS_ML, DK_ML) * (DK_ML ** -0.5)
    kh = k.reshape(B, L, N_HEADS_ML, DK_ML)
    vh = v.astype(f32).reshape(B, L, N_HEADS_ML, DV_ML)
    gates = (if_pre + b_if).astype(f32).reshape(B, L, 4, N_HEADS_ML)
    ig_f, lf_f = gates[:, :, 0], jax.nn.log_sigmoid(gates[:, :, 1])
    ig_b, lf_b = gates[:, :, 2], jax.nn.log_sigmoid(gates[:, :, 3])
    h = mlstm_scan(qh, kh, vh, ig_f, lf_f) + flip(mlstm_scan(flip(qh), flip(kh), flip(vh), flip(ig_b), flip(lf_b)))
    return head_rmsnorm(h, g_norm) * jax.nn.sigmoid(o.astype(f32))


def encoder_layer(x, g_ffn1, w_ffn1_in, w_ffn1_out, g_mix, w_in, gla_w2_fwd, gla_b2_fwd, gla_w2_bwd, gla_b2_bwd,
                  gla_norm, ml_conv_w, ml_conv_b, ml_b_if, ml_norm, w_out, g_ffn2, w_ffn2_in, w_ffn2_out):
    f32 = jnp.float32
    x = x + 0.5 * swiglu(rmsnorm(x, g_ffn1), w_ffn1_in, w_ffn1_out)
    h = rmsnorm(x, g_mix)
    (gla_q, gla_k, gla_v, gla_r, gla_lr, ml_qk, ml_v, ml_o, ml_if, gate_a, gate_b) = jnp.split(h @ w_in, IN_OFFSETS, axis=-1)
    o_a = gla_mixer(gla_q, gla_k, gla_v, gla_r, gla_lr, gla_w2_fwd, gla_b2_fwd, gla_w2_bwd, gla_b2_bwd, gla_norm)
    o_b = mlstm_mixer(ml_qk, ml_v, ml_o, ml_if, ml_conv_w, ml_conv_b, ml_b_if, ml_norm)
    merged = jax.nn.sigmoid(gate_a.astype(f32)) * o_a + jax.nn.sigmoid(gate_b.astype(f32)) * o_b
    x = x + merged.astype(x.dtype) @ w_out
    x = x + 0.5 * swiglu(rmsnorm(x, g_ffn2), w_ffn2_in, w_ffn2_out)
    return x


def setup_inputs(seed: int = 0) -> dict:
    key = jax.random.key(seed)
    ks = jax.random.split(key, 24)
    f32 = jnp.float32

    def nrm(k, shape, scale):
        return jax.random.normal(k, shape, f32) * scale

    def gain(k, shape):
        return 1.0 + 0.02 * jax.random.normal(k, shape, f32)

    forget_bias = jnp.concatenate([jnp.zeros((N_HEADS_ML,), f32), jnp.linspace(3.0, 6.0, N_HEADS_ML, dtype=f32)])
    if_base = jnp.tile(forget_bias, 2)
    return {
        "x_prompt": jax.random.normal(ks[0], (BATCH, SEQ, D_MODEL), f32),
        "x_sample": jax.random.normal(ks[1], (DEC_BATCH, DEC_SEQ, D_MODEL), f32),
        "g_ffn1": gain(ks[2], (DEPTH, D_MODEL)),
        "w_ffn1_in": nrm(ks[3], (DEPTH, D_MODEL, 2 * D_FF), D_MODEL ** -0.5),
        "w_ffn1_out": nrm(ks[4], (DEPTH, D_FF, D_MODEL), D_FF ** -0.5),
        "g_mix": gain(ks[5], (DEPTH, D_MODEL)),
        "w_in": nrm(ks[6], (DEPTH, D_MODEL, D_IN), D_MODEL ** -0.5),
        "gla_w2_fwd": nrm(ks[7], (DEPTH, GLA_LOWRANK, GLA_QK), GLA_LOWRANK ** -0.5),
        "gla_b2_fwd": nrm(ks[8], (DEPTH, GLA_QK), 0.1),
        "gla_w2_bwd": nrm(ks[9], (DEPTH, GLA_LOWRANK, GLA_QK), GLA_LOWRANK ** -0.5),
        "gla_b2_bwd": nrm(ks[10], (DEPTH, GLA_QK), 0.1),
        "gla_norm": gain(ks[11], (DEPTH, GLA_V)),
        "ml_conv_w": nrm(ks[12], (DEPTH, CONV_W, 2 * ML_QK), CONV_W ** -0.5),
        "ml_conv_b": nrm(ks[13], (DEPTH, 2 * ML_QK), 0.02),
        "ml_b_if": if_base + nrm(ks[14], (DEPTH, 4 * N_HEADS_ML), 0.1),
        "ml_norm": gain(ks[15], (DEPTH, ML_V)),
        "w_out": nrm(ks[16], (DEPTH, D_MODEL, D_MODEL), D_MODEL ** -0.5),
        "g_ffn2": gain(ks[17], (DEPTH, D_MODEL)),
        "w_ffn2_in": nrm(ks[18], (DEPTH, D_MODEL, 2 * D_FF), D_MODEL ** -0.5),
        "w_ffn2_out": nrm(ks[19], (DEPTH, D_FF, D_MODEL), D_FF ** -0.5),
        "g_final": gain(ks[20], (D_MODEL,)),
    }


def reference(x_prompt, x_sample, g_ffn1, w_ffn1_in, w_ffn1_out, g_mix, w_in, gla_w2_fwd, gla_b2_fwd,
              gla_w2_bwd, gla_b2_bwd, gla_norm, ml_conv_w, ml_conv_b, ml_b_if, ml_norm, w_out, g_ffn2,
              w_ffn2_in, w_ffn2_out, g_final):
    def trunk(x):
        for l in range(DEPTH):
            x = encoder_layer(x, g_ffn1[l], w_ffn1_in[l], w_ffn1_out[l], g_mix[l], w_in[l],
                              gla_w2_fwd[l], gla_b2_fwd[l], gla_w2_bwd[l], gla_b2_bwd[l], gla_norm[l],
                              ml_conv_w[l], ml_conv_b[l], ml_b_if[l], ml_norm[l], w_out[l],
                              g_ffn2[l], w_ffn2_in[l], w_ffn2_out[l])
        return rmsnorm(x, g_final)

    y_prompt = trunk(x_prompt)
    y_sample = trunk(x_sample)
    return (y_prompt, y_sample)
```

```python
import math
import types
import numpy as np
from contextlib import ExitStack
import concourse.bass as bass
import concourse.mybir as mybir
from concourse.bass_utils import run_bass_kernel_spmd

F32 = mybir.dt.float32
BF16 = mybir.dt.bfloat16
I32 = mybir.dt.int32
AF = mybir.ActivationFunctionType
ALU = mybir.AluOpType

ENGS = ("pe", "act", "dve", "pool", "sp")
SEM_EPOCH = 30000
DMA_RR = 6

D = 1024
L = 2048
NT = 16
DFF = 2816
NJ = 22
DIN = 8240
EPS = 1e-6
TAU = 16.0
O_GQ, O_GK, O_GV, O_GR, O_LR, O_MQK, O_MV, O_MO, O_MIF, O_GA, O_GB = 0, 512, 1024, 2048, 3072, 3104, 4128, 5152, 6176, 6192, 7216


class TT:
    __slots__ = ("ap", "w", "r")

    def __init__(self, ap):
        self.ap = ap
        self.w = None
        self.r = []

    def __getitem__(self, k):
        return self.ap[k]


class Ins:
    __slots__ = ("eng", "idx", "fn", "deps", "dma", "dma_idx", "signal", "cnt")

    def __init__(self, eng, idx, fn, deps, dma, dma_idx):
        self.eng, self.idx, self.fn, self.deps = eng, idx, fn, deps
        self.dma, self.dma_idx = dma, dma_idx
        self.signal = False
        self.cnt = 0


def _snap(fn):
    if fn is None or fn.__closure__ is None:
        return fn
    cells = []
    for c in fn.__closure__:
        try:
            cells.append(types.CellType(c.cell_contents))
        except ValueError:
            cells.append(c)
    g = types.FunctionType(fn.__code__, fn.__globals__, fn.__name__, fn.__defaults__, tuple(cells))
    g.__kwdefaults__ = fn.__kwdefaults__
    return g


class Prog:
    def __init__(self, nc):
        self.nc = nc
        self.q = {e: [] for e in ENGS}
        self.ndma = {e: 0 for e in ENGS}

    def add(self, eng, fn, rd=(), wr=(), dma=False):
        q = self.q[eng]
        idx = len(q)
        deps = set()
        for t in rd:
            if t.w is not None:
                deps.add(t.w)
        for t in wr:
            if t.w is not None:
                deps.add(t.w)
            deps.update(t.r)
        dma_idx = -1
        if dma:
            dma_idx = self.ndma[eng]
            self.ndma[eng] += 1
        ins = Ins(eng, idx, _snap(fn), deps, dma, dma_idx)
        q.append(ins)
        me = (eng, idx)
        for t in rd:
            t.r.append(me)
        for t in wr:
            t.w = me
            t.r = []
        return ins

    def barrier(self):
        last = []
        for e in ENGS:
            if self.q[e]:
                last.append((e, len(self.q[e]) - 1))
        dmas = []
        for e in ENGS:
            if self.ndma[e]:
                cnt = 0
                for ins in reversed(self.q[e]):
                    if ins.dma:
                        dmas.append((e, ins.idx))
                        cnt += 1
                        if cnt >= DMA_RR:
                            break
        for e in ENGS:
            ins = self.add(e, None)
            ins.deps.update(last)
            ins.deps.update(dmas)
            ins.deps.discard((e, ins.idx))

    def emit(self):
        with ExitStack() as st:
            self._emit(st)

    def _emit(self, st):
        nc = self.nc
        q = self.q
        for e in ENGS:
            for ins in q[e]:
                for (e2, i2) in ins.deps:
                    d = q[e2][i2]
                    if d.dma:
                        continue
                    if e2 == e and e == "pe":
                        continue
                    d.signal = True
        nsig = {}
        for e in ENGS:
            c = 0
            for ins in q[e]:
                if ins.signal and not ins.dma:
                    c += 1
                    ins.cnt = c
            nsig[e] = c
        sems = {}
        for e in ENGS:
            nep = nsig[e] // SEM_EPOCH + 1
            sems[e] = [st.enter_context(nc.semaphore(f"s_{e}_{k}")) for k in range(nep)]
        dsems = {}
        for e in ENGS:
            dsems[e] = [st.enter_context(nc.semaphore(f"d_{e}_{k}")) for k in range(DMA_RR)] if self.ndma[e] else []

        def target(d):
            if d.dma:
                return dsems[d.eng][d.dma_idx % DMA_RR], 16 * (d.dma_idx // DMA_RR + 1)
            c = d.cnt
            ep = (c - 1) // SEM_EPOCH
            return sems[d.eng][ep], c - ep * SEM_EPOCH

        stats = {e: [0, 0] for e in ENGS}

        def body(e):
            def run(eng):
                waited = {}
                for ins in q[e]:
                    need = {}
                    for (e2, i2) in ins.deps:
                        d = q[e2][i2]
                        if e2 == e and e == "pe" and not d.dma:
                            continue
                        s, v = target(d)
                        key = id(s)
                        if need.get(key, (None, 0))[1] < v:
                            need[key] = (s, v)
                    if ins.dma and ins.dma_idx >= DMA_RR:
                        s = dsems[e][ins.dma_idx % DMA_RR]
                        v = 16 * (ins.dma_idx // DMA_RR)
                        key = id(s)
                        if need.get(key, (None, 0))[1] < v:
                            need[key] = (s, v)
                    for key, (s, v) in need.items():
                        if waited.get(key, 0) >= v:
                            continue
                        eng.wait_ge(s, v)
                        waited[key] = v
                        stats[e][1] += 1
                    if ins.fn is None:
                        if ins.signal:
                            r = eng.nop()
                        else:
                            continue
                    else:
                        r = ins.fn(eng)
                    stats[e][0] += 1
                    if ins.dma:
                        s, _ = target(ins)
                        r.then_inc(s, 16)
                    elif ins.signal:
                        s, _ = target(ins)
                        r.then_inc(s, 1)
            return run

        block = st.enter_context(nc.Block())
        reg = {"pe": block.tensor, "act": block.scalar, "dve": block.vector,
               "pool": block.gpsimd, "sp": block.sync}
        for e in ENGS:
            if q[e]:
                reg[e](body(e))
        self.stats = stats


class Builder:
    def __init__(self, nseq, stages=("ffn1", "mix", "ffn2"), units=None):
        self.nseq = nseq
        self.stages = stages
        self.units = units if units is not None else [("gla", h) for h in range(4)] + [("ml", h) for h in range(4)]
        self.nc = bass.Bass("TRN2", target_bir_lowering=False)
        self.st = ExitStack()

    def dram(self, name, shape, kind="ExternalInput"):
        return self.nc.dram_tensor(name, shape, F32, kind=kind).ap()

    def sb(self, name, shape, dt):
        return self.st.enter_context(self.nc.sbuf_tensor(name, shape, dt))

    def carve(self, nelem, dt):
        n16 = nelem * (2 if dt == F32 else 1)
        n16 = (n16 + 15) // 16 * 16
        a = self.scr[:, self.sp:self.sp + n16]
        self.sp += n16
        assert self.sp <= self.SCR, (self.sp, self.SCR)
        if dt == F32:
            a = a.bitcast(F32)
        return a

    def build(self):
        nc = self.nc
        P = self.P = Prog(nc)
        ns = self.nseq
        self.x_d = self.dram("x", [ns * L, D])
        self.y_d = self.dram("y", [ns * L, D], kind="ExternalOutput")
        self.w1i = self.dram("w_ffn1_in", [D, 2 * DFF])
        self.w1o = self.dram("w_ffn1_out", [DFF, D])
        self.w2i = self.dram("w_ffn2_in", [D, 2 * DFF])
        self.w2o = self.dram("w_ffn2_out", [DFF, D])
        self.win = self.dram("w_in", [D, DIN])
        self.wout = self.dram("w_out", [D, D])
        self.pcol_d = self.dram("pcol", [128, 72])
        self.w2g_d = self.dram("w2g", [16, 1024])
        self.gvec_d = self.dram("gvec", [6, 1024])

        with self.st:
            self.alloc()
            self.setup_consts()
            for s in range(ns):
                self.sequence(s)
            P.barrier()
            P.emit()
        return nc

    def alloc(self):
        sb = self.sb
        self.xbuf = sb("xbuf", [128, NT, D], F32)
        self.X = [TT(self.xbuf[:, i, :]) for i in range(NT)]
        self.identf = TT(sb("identf", [128, 128], F32))
        self.ident = TT(sb("ident", [128, 128], BF16))
        self.maskF = TT(sb("maskF", [128, 128], F32))
        self.maskB = TT(sb("maskB", [128, 128], F32))
        self.pcol = TT(sb("pcol_s", [128, 72], F32))
        self.nb2 = TT(sb("nb2", [128, 8], F32))
        self.nbif = TT(sb("nbif", [4, 4], F32))
        self.w2g = TT(sb("w2g_s", [16, 1024], BF16))
        self.wsm = TT(sb("wsm", [128, 8, 48], BF16))
        self.rm = TT(sb("rm", [128, 512], F32))
        self.ones1 = TT(sb("ones1", [128, 1], F32))
        self.sel = [TT(sb(f"sel{h}", [4, 128], F32)) for h in range(4)]
        self.ss = [TT(sb(f"ss{k}", [128, 16], F32)) for k in range(2)]
        self.rs = [TT(sb(f"rs{k}", [128, 16], F32)) for k in range(2)]
        self.rtmp = TT(sb("rtmp", [128, 48], F32))
        self.ssk = 0
        self.SCR = 68032
        self.scr = sb("scr", [128, self.SCR], BF16)
        ps = lambda n, shape, dt: TT(self.st.enter_context(self.nc.psum_tensor(n, shape, dt)))
        self.PS = [ps(f"ps{k}", [128, 512], F32) for k in range(6)]
        self.PT = [ps(f"pt{k}", [128, 1024], BF16) for k in range(2)]
        self.layout_ffn()
        self.layout_mix()

    def layout_ffn(self):
        self.sp = 0
        c = self.carve
        self.f_gb = TT(c(1024, F32))
        self.f_gfin = TT(c(1024, F32))
        hT = c(8 * 1024, BF16).rearrange("p (k t) -> p k t", k=8)
        self.f_hTbuf = hT
        self.f_HT = [TT(hT[:, :, g * 512:(g + 1) * 512]) for g in range(2)]
        self.f_hb = [TT(c(1024, BF16)), TT(c(1024, BF16))]
        self.f_junk = TT(c(1024, BF16))
        act = c(11 * 1024, BF16).rearrange("p (j t) -> p j t", j=11)
        self.f_actbuf = act
        self.f_ACT = [TT(act[:, :, g * 512:(g + 1) * 512]) for g in range(2)]
        self.f_wo = [TT(c(11 * 1024, BF16).rearrange("p (j d) -> p j d", j=11)) for _ in range(2)]
        self.f_wi = [TT(c(8 * 2 * 128, BF16).rearrange("p (k a c) -> p k a c", k=8, a=2)) for _ in range(4)]
        self.f_sa = [TT(c(512, BF16)) for _ in range(2)]
        self.f_yst = [TT(c(1024, F32)) for _ in range(2)]
        self.f_end = self.sp
        print('ffn scratch units', self.sp)

    def layout_mix(self):
        self.sp = 0
        c = self.carve
        self.m_gb = TT(c(1024, F32))
        hT = c(8 * L, BF16).rearrange("p (k t) -> p k t", k=8)
        self.m_hTbuf = hT
        self.m_HT = [TT(hT[:, :, g * 512:(g + 1) * 512]) for g in range(4)]
        _hb = TT(c(1024, BF16))
        self.m_hb = [_hb, _hb]
        self.m_junk = TT(c(1024, BF16))
        w12 = c(2 * 8 * 256, BF16).rearrange("p (w k c) -> p w k c", w=2, k=8)
        self.m_W12 = w12
        self.m_W1 = TT(w12[:, 0])
        self.m_W2 = TT(w12[:, 1])
        self.m_wo = TT(c(2 * 1024, BF16).rearrange("p (k d) -> p k d", k=2))
        self.m_gn = TT(c(256, F32))
        self.m_lrg = [TT(c(512, BF16)[0:16, :]) for _ in range(2)]
        self.m_wst = [TT(c(64, F32)) for _ in range(2)]
        self.m_en2 = [TT(c(64, F32)) for _ in range(2)]
        self.m_decb = TT(c(128, F32))
        B = [TT(c(512, F32)) for _ in range(5)]
        self.m_B = B
        self.m_th = [B[0], B[1]]
        self.m_t12 = [B[2], B[3]]
        self.m_mb = [TT(c(256, BF16)) for _ in range(2)]
        self.m_mT = [TT(c(256, BF16)) for _ in range(2)]
        self.m_sq = TT(c(256, BF16))
        base = self.sp
        self.g_qin = [[TT(a[:, g * 512:(g + 1) * 512]) for g in range(4)] for a in (c(L, BF16), c(L, BF16))]
        self.g_kin = [[TT(a[:, g * 512:(g + 1) * 512]) for g in range(4)] for a in (c(L, BF16), c(L, BF16))]
        ksb = [c(NT * 128, BF16).rearrange("p (t d) -> p t d", t=NT) for _ in range(2)]
        self.g_ksbuf = ksb
        self.g_ks = [[TT(a[:, g * 4:(g + 1) * 4, :]) for g in range(4)] for a in ksb]
        vb = c(NT * 256, BF16).rearrange("p (t d) -> p t d", t=NT)
        self.g_vbuf = vb
        self.g_v = [TT(vb[:, g * 4:(g + 1) * 4, :]) for g in range(4)]
        ob = c(NT * 256, F32).rearrange("p (t d) -> p t d", t=NT)
        self.g_obuf = ob
        self.g_o = [TT(ob[:, i, :]) for i in range(NT)]
        self.g_Lin = [TT(c(512, F32)), self.m_B[0]]
        self.g_Lc = [TT(c(512, F32)), self.m_B[1]]
        self.g_E = [[TT(c(512, F32)) for _ in range(3)], [self.m_B[2], self.m_B[3], self.m_B[4]]]
        self.g_ksT = [TT(c(512, BF16)) for _ in range(2)]
        self.g_nl = [TT(c(4, F32)) for _ in range(2)]
        self.g_dec = TT(c(32, F32))
        self.g_att = [TT(c(128, BF16)) for _ in range(2)]
        self.g_S32 = [TT(c(256, F32)) for _ in range(2)]
        self.g_Sb = [TT(c(256, BF16)) for _ in range(2)]
        gla_end = self.sp
        self.sp = base
        self.l_pre = TT(c(L + 2, F32))
        self.l_cv = TT(c(L, F32))
        qa, ka = c(L, BF16), c(L, BF16)
        self.l_qTa = TT(qa)
        self.l_kTa = TT(ka)
        kt = c(NT * 128, BF16).rearrange("p (t d) -> p t d", t=NT)
        self.l_ktok = TT(kt)
        vw = [c(NT * 258, BF16).rearrange("p (t d) -> p t d", t=NT) for _ in range(2)]
        self.l_vwbuf = vw
        self.l_vw = [TT(a) for a in vw]
        hb_ = c(NT * 256, F32).rearrange("p (t d) -> p t d", t=NT)
        self.l_h = [TT(hb_[:, i, :]) for i in range(NT)]
        self.l_sT = [TT(c(128, BF16)) for _ in range(2)]
        self.l_C32 = [TT(c(258, F32)) for _ in range(2)]
        self.l_Cb = [TT(c(258, BF16)) for _ in range(2)]
        self.l_dn = [TT(c(4, F32)) for _ in range(2)]
        ml_end = self.sp
        self.sp = base
        self.r_rows = [(TT(c(L, F32)[0:4, :]), TT(c(L, F32)[0:4, :]), TT(c(L, F32)[0:4, :]), TT(c(64, F32)[0:4, :])) for _ in range(2)]
        self.sp = max(gla_end, ml_end, self.sp)
        self.m_end = self.sp
        print('mixer scratch units', self.sp, 'gla_end', gla_end, 'ml_end', ml_end)

    def rsqrt(self, src, dst, n, addc):
        P = self.P
        t = self.rtmp
        a, y0 = t[:, 0:n], t[:, 16:16 + n]
        w = t[:, 32:32 + n]
        P.add("dve", lambda e: e.tensor_scalar(out=a, in0=src[:, 0:n], scalar1=addc, scalar2=None, op0=ALU.add), rd=[src], wr=[t])
        P.add("dve", lambda e: e.tensor_single_scalar(out=w.bitcast(I32), in_=a.bitcast(I32), scalar=1, op=ALU.arith_shift_right), rd=[t], wr=[t])
        P.add("dve", lambda e: e.tensor_scalar(out=y0.bitcast(I32), in0=w.bitcast(I32), scalar1=-1, scalar2=1597463007, op0=ALU.mult, op1=ALU.add), rd=[t], wr=[t])
        for it in range(3):
            P.add("dve", lambda e: e.tensor_tensor(out=w, in0=a, in1=y0, op=ALU.mult), rd=[t], wr=[t])
            P.add("dve", lambda e: e.tensor_tensor(out=w, in0=w, in1=y0, op=ALU.mult), rd=[t], wr=[t])
            P.add("dve", lambda e: e.tensor_scalar(out=w, in0=w, scalar1=-0.5, scalar2=1.5, op0=ALU.mult, op1=ALU.add), rd=[t], wr=[t])
            if it < 2:
                P.add("dve", lambda e: e.tensor_tensor(out=y0, in0=y0, in1=w, op=ALU.mult), rd=[t], wr=[t])
            else:
                P.add("dve", lambda e: e.tensor_tensor(out=dst[:, 0:n], in0=y0, in1=w, op=ALU.mult), rd=[t], wr=[dst])

    def setup_consts(self):
        P = self.P
        idf, idb = self.identf, self.ident
        P.add("pool", lambda e: e.memset(idf[:], 0.0), wr=[idf])
        P.add("pool", lambda e: e.affine_select(out=idf[:], in_=idf[:], pattern=[[-1, 128]], compare_op=ALU.not_equal, fill=1.0, base=0, channel_multiplier=1), rd=[idf], wr=[idf])
        P.add("dve", lambda e: e.tensor_copy(out=idb[:], in_=idf[:]), rd=[idf], wr=[idb])
        mF, mB = self.maskF, self.maskB
        P.add("pool", lambda e: e.memset(mF[:], 1.0), wr=[mF])
        P.add("pool", lambda e: e.affine_select(out=mF[:], in_=mF[:], pattern=[[1, 128]], compare_op=ALU.is_ge, fill=0.0, base=0, channel_multiplier=-1), rd=[mF], wr=[mF])
        P.add("pool", lambda e: e.memset(mB[:], 1.0), wr=[mB])
        P.add("pool", lambda e: e.affine_select(out=mB[:], in_=mB[:], pattern=[[-1, 128]], compare_op=ALU.is_ge, fill=0.0, base=0, channel_multiplier=1), rd=[mB], wr=[mB])
        P.add("sp", lambda e: e.dma_start(out=self.pcol[:], in_=self.pcol_d[:, :]), wr=[self.pcol], dma=True)
        P.add("pool", lambda e: e.dma_start(out=self.w2g[:], in_=self.w2g_d[:, :]), wr=[self.w2g], dma=True)
        wsrc = self.win.rearrange("(k p) c -> p k c", p=128)
        P.add("pool", lambda e: e.dma_start(out=self.wsm[:, :, 0:32], in_=wsrc[:, :, O_LR:O_LR + 32]), wr=[self.wsm], dma=True)
        P.add("pool", lambda e: e.dma_start(out=self.wsm[:, :, 32:48], in_=wsrc[:, :, O_MIF:O_MIF + 16]), wr=[self.wsm], dma=True)
        P.add("dve", lambda e: e.tensor_scalar(out=self.nb2[:], in0=self.pcol[:, 24:32], scalar1=-1.0, scalar2=None, op0=ALU.mult), rd=[self.pcol], wr=[self.nb2])
        P.add("dve", lambda e: e.tensor_scalar(out=self.nbif[:], in0=self.pcol[0:4, 64:68], scalar1=-1.0, scalar2=None, op0=ALU.mult), rd=[self.pcol], wr=[self.nbif])
        rm = self.rm
        P.add("dve", lambda e: e.memset(rm[:], 1.0), wr=[rm])
        P.add("dve", lambda e: e.memset(rm[:].rearrange("p (c t) -> p c t", c=4)[:, :, 0:1], 0.0), rd=[rm], wr=[rm])
        P.add("dve", lambda e: e.memset(self.ones1[:], 1.0), wr=[self.ones1])
        for h in range(4):
            P.add("dve", lambda e, h=h: e.tensor_copy(out=self.sel[h][:], in_=idf[0:4, h:h + 1].to_broadcast([4, 128])), rd=[idf], wr=[self.sel[h]])

    def sequence(self, s):
        P = self.P
        for i in range(NT):
            r0 = s * L + i * 128
            P.add("sp", lambda e, i=i, r0=r0: e.dma_start(out=self.X[i][:], in_=self.x_d[r0:r0 + 128, :]), wr=[self.X[i]], dma=True)
        if "ffn1" in self.stages:
            self.ffn(s, self.w1i, self.w1o, 0, final=False)
        if "mix" in self.stages:
            P.barrier()
            self.mixer(s)
            P.barrier()
        self.ffn(s, self.w2i, self.w2o, 2, final=True, skip_ffn=("ffn2" not in self.stages))

    def load_gvec(self, dst, row):
        self.P.add("sp", lambda e: e.dma_start(out=dst[:], in_=self.gvec_d[row:row + 1, :].partition_broadcast(128)[:, 0, :]), wr=[dst], dma=True)

    def norm_T(self, tiles, gb, hb2, junk, dst_fn, dst_tt_fn):
        k = self.ssk
        self.ssk ^= 1
        for j in range(len(tiles)):
            self.norm_sq(tiles, j, junk, k)
        self.rsqrt(self.ss[k], self.rs[k], len(tiles), D * EPS)
        self.norm_B(tiles, gb, hb2, dst_fn, dst_tt_fn, k)

    def norm_sq(self, tiles, j, junk, k):
        ss = self.ss[k]
        i = tiles[j]
        self.P.add("act", lambda e: e.activation(out=junk[:], in_=self.X[i][:], func=AF.Square, accum_out=ss[:, j:j + 1]), rd=[self.X[i]], wr=[junk, ss])

    def norm_B(self, tiles, gb, hb2, dst_fn, dst_tt_fn, k):
        P = self.P
        rs = self.rs[k]
        for j, i in enumerate(tiles):
            hb = hb2[j % 2]
            pt = self.PT[j % 2]
            P.add("dve", lambda e, i=i, j=j, hb=hb: e.scalar_tensor_tensor(out=hb[:], in0=self.X[i][:], scalar=rs[:, j:j + 1], in1=gb[:], op0=ALU.mult, op1=ALU.mult), rd=[self.X[i], rs, gb], wr=[hb])
            for kc in range(8):
                P.add("pe", lambda e, kc=kc, hb=hb, pt=pt: e.transpose(out=pt[:, kc * 128:(kc + 1) * 128], in_=hb[:, kc * 128:(kc + 1) * 128], identity=self.ident[:]), rd=[hb, self.ident], wr=[pt])
            dst = dst_fn(j)
            P.add("act", lambda e, dst=dst, pt=pt: e.copy(out=dst, in_=pt[:].rearrange("p (k t) -> p k t", k=8)), rd=[pt], wr=[dst_tt_fn(j)])

    def ffn(self, s, wi_d, wo_d, grow, final, skip_ffn=False):
        P = self.P
        gb = self.f_gb
        if not skip_ffn:
            self.load_gvec(gb, grow)
            P.add("dve", lambda e: e.tensor_scalar(out=gb[:], in0=gb[:], scalar1=float(math.sqrt(D)), scalar2=None, op0=ALU.mult), rd=[gb], wr=[gb])
        if final:
            self.load_gvec(self.f_gfin, 3)
            P.add("dve", lambda e: e.tensor_scalar(out=self.f_gfin[:], in0=self.f_gfin[:], scalar1=float(math.sqrt(D)), scalar2=None, op0=ALU.mult), rd=[self.f_gfin], wr=[self.f_gfin])
        wi_src = wi_d.rearrange("(k p) (a c) -> p k a c", p=128, a=2)
        wslot = 0
        hTb = self.f_hTbuf
        dstf, dsttt = (lambda j: hTb[:, :, j * 128:(j + 1) * 128]), (lambda j: self.f_HT[j // 4])
        for p in range(2):
            tiles = list(range(p * 8, p * 8 + 8))
            ntiles = list(range(8, 16))
            if not skip_ffn:
                if p == 0:
                    self.norm_T(tiles, gb, self.f_hb, self.f_junk, dstf, dsttt)
                for hf in range(2):
                    hoist = (p == 0 and hf == 1)
                    if hoist:
                        kn = self.ssk
                        self.ssk ^= 1
                    wo = self.f_wo[hf]
                    P.add("pool", lambda e, wo=wo, hf=hf: e.dma_start(out=wo[:], in_=wo_d[hf * 1408:(hf + 1) * 1408, :].rearrange("(j p) d -> p j d", p=128)), wr=[wo], dma=True)
                    for jj in range(11):
                        j = hf * 11 + jj
                        wi = self.f_wi[wslot % 4]
                        wslot += 1
                        for a in range(2):
                            P.add("pool", lambda e, wi=wi, j=j, a=a: e.dma_start(out=wi[:, :, a, :], in_=wi_src[:, :, a, j * 128:(j + 1) * 128]), wr=[wi], dma=True)
                        for g in range(2):
                            pa, pg = self.PS[2 * g], self.PS[2 * g + 1]
                            ht = self.f_HT[g]
                            for a, pp in ((0, pa), (1, pg)):
                                for kc in range(8):
                                    P.add("pe", lambda e, a=a, pp=pp, kc=kc, wi=wi, ht=ht: e.matmul(pp[:], lhsT=wi[:, kc, a, :], rhs=ht[:, kc, :], start=(kc == 0), stop=(kc == 7)), rd=[wi, ht], wr=[pp])
                            sa = self.f_sa[g]
                            P.add("act", lambda e, sa=sa, pa=pa: e.activation(out=sa[:], in_=pa[:], func=AF.Silu), rd=[pa], wr=[sa])
                            at = self.f_ACT[g]
                            P.add("dve", lambda e, at=at, sa=sa, pg=pg, jj=jj: e.tensor_tensor(out=at[:, jj, :], in0=sa[:], in1=pg[:], op=ALU.mult), rd=[sa, pg], wr=[at])
                        if hoist and jj < 8:
                            self.norm_sq(ntiles, jj, self.f_junk, kn)
                        if hoist and jj == 8:
                            self.rsqrt(self.ss[kn], self.rs[kn], 8, D * EPS)
                    if hoist:
                        self.norm_B(ntiles, gb, self.f_hb, dstf, dsttt, kn)
                    for jt, i in enumerate(tiles):
                        at = self.f_ACT[jt // 4]
                        t0 = (jt % 4) * 128
                        for ch in range(2):
                            py = self.PS[4 + (jt * 2 + ch) % 2]
                            for jj in range(11):
                                P.add("pe", lambda e, py=py, at=at, t0=t0, jj=jj, ch=ch, wo=wo: e.matmul(py[:], lhsT=at[:, jj, t0:t0 + 128], rhs=wo[:, jj, ch * 512:(ch + 1) * 512], start=(jj == 0), stop=(jj == 10)), rd=[at, wo], wr=[py])
                            P.add("dve", lambda e, py=py, i=i, ch=ch: e.scalar_tensor_tensor(out=self.X[i][:, ch * 512:(ch + 1) * 512], in0=py[:], scalar=0.5, in1=self.X[i][:, ch * 512:(ch + 1) * 512], op0=ALU.mult, op1=ALU.add), rd=[py, self.X[i]], wr=[self.X[i]])
            if final:
                k = self.ssk
                self.ssk ^= 1
                ss, rs = self.ss[k], self.rs[k]
                for j, i in enumerate(tiles):
                    P.add("act", lambda e, i=i, j=j: e.activation(out=self.f_junk[:], in_=self.X[i][:], func=AF.Square, accum_out=ss[:, j:j + 1]), rd=[self.X[i]], wr=[self.f_junk, ss])
                self.rsqrt(ss, rs, 8, D * EPS)
                for j, i in enumerate(tiles):
                    yst = self.f_yst[j % 2]
                    r0 = s * L + i * 128
                    P.add("dve", lambda e, i=i, j=j, yst=yst: e.scalar_tensor_tensor(out=yst[:], in0=self.X[i][:], scalar=rs[:, j:j + 1], in1=self.f_gfin[:], op0=ALU.mult, op1=ALU.mult), rd=[self.X[i], rs, self.f_gfin], wr=[yst])
                    P.add("sp", lambda e, yst=yst, r0=r0: e.dma_start(out=self.y_d[r0:r0 + 128, :], in_=yst[:]), rd=[yst], dma=True)

    def mixer(self, s):
        P = self.P
        gb = self.m_gb
        self.load_gvec(gb, 1)
        P.add("dve", lambda e: e.tensor_scalar(out=gb[:], in0=gb[:], scalar1=float(math.sqrt(D)), scalar2=None, op0=ALU.mult), rd=[gb], wr=[gb])
        hTb = self.m_hTbuf
        need_ml = any(u[0] == "ml" for u in self.units)
        for half in range(2):
            tiles = list(range(half * 8, half * 8 + 8))
            self.norm_T(tiles, gb, self.m_hb, self.m_junk,
                        lambda j, half=half: hTb[:, :, (half * 8 + j) * 128:(half * 8 + j + 1) * 128],
                        lambda j, half=half: self.m_HT[(half * 8 + j) // 4])
            if need_ml:
                self.ml_gates_proj([2 * half, 2 * half + 1])
        if need_ml:
            self.ml_gates()
        P.barrier()
        prev = None
        for (kind, h) in self.units:
            if prev is not None and prev != kind:
                P.barrier()
            prev = kind
            if kind == "gla":
                self.gla_unit(h)
            else:
                self.ml_unit(h)

    def proj_feat(self, pp, lhs_fn, g, M=128):
        P = self.P
        ht = self.m_HT[g]
        for kc in range(8):
            lhs, rd = lhs_fn(kc)
            P.add("pe", lambda e, kc=kc, lhs=lhs: e.matmul(pp[0:M, :], lhsT=lhs, rhs=ht[:, kc, :], start=(kc == 0), stop=(kc == 7)), rd=[rd, ht], wr=[pp])

    def proj_tok(self, out_ap, pp, w, i, c0, n):
        P = self.P
        ht = self.m_HT[i // 4]
        t0 = (i % 4) * 128
        for kc in range(8):
            P.add("pe", lambda e, kc=kc: e.matmul(out_ap, lhsT=ht[:, kc, t0:t0 + 128], rhs=w[:, kc, c0:c0 + n], start=(kc == 0), stop=(kc == 7)), rd=[ht, w], wr=[pp])

    def lr_proj(self):
        P = self.P
        for d in range(2):
            for g in range(4):
                pp = self.PS[(d * 4 + g) % 2]
                self.proj_feat(pp, lambda kc, d=d: (self.wsm[:, kc, d * 16:(d + 1) * 16], self.wsm), g, M=16)
                lr = self.m_lrT[d]
                P.add("act", lambda e, pp=pp, lr=lr, g=g: e.copy(out=lr[:, g * 512:(g + 1) * 512], in_=pp[0:16, :]), rd=[pp], wr=[lr])

    def load_unit_weights(self, cq, ck, cv, cr, cg, h, gnrow):
        P = self.P
        wsrc = self.win.rearrange("(k p) c -> p k c", p=128)
        W1, W2 = self.m_W1, self.m_W2
        P.add("pool", lambda e: e.dma_start(out=W1[:, :, 0:128], in_=wsrc[:, :, cq:cq + 128]), wr=[W1], dma=True)
        P.add("pool", lambda e: e.dma_start(out=W1[:, :, 128:256], in_=wsrc[:, :, ck:ck + 128]), wr=[W1], dma=True)
        P.add("pool", lambda e: e.dma_start(out=W2[:], in_=wsrc[:, :, cv:cv + 256]), wr=[W2], dma=True)
        self._late = (cr, cg)
        P.add("pool", lambda e: e.dma_start(out=self.m_wo[:], in_=self.wout[h * 256:(h + 1) * 256, :].rearrange("(k p) d -> p k d", p=128)), wr=[self.m_wo], dma=True)
        gn = self.m_gn
        P.add("sp", lambda e: e.dma_start(out=gn[:], in_=self.gvec_d[gnrow:gnrow + 1, h * 256:(h + 1) * 256].partition_broadcast(128)[:, 0, :]), wr=[gn], dma=True)
        gsc = 8.0 if gnrow == 4 else 4.0
        P.add("dve", lambda e: e.tensor_scalar(out=gn[:], in0=gn[:], scalar1=gsc, scalar2=None, op0=ALU.mult), rd=[gn], wr=[gn])

    def load_late_weights(self):
        P = self.P
        wsrc = self.win.rearrange("(k p) c -> p k c", p=128)
        cr, cg = self._late
        W1, W2 = self.m_W1, self.m_W2
        P.add("pool", lambda e: e.dma_start(out=W1[:], in_=wsrc[:, :, cr:cr + 256]), wr=[W1], dma=True)
        P.add("pool", lambda e: e.dma_start(out=W2[:], in_=wsrc[:, :, cg:cg + 256]), wr=[W2], dma=True)

    def gla_unit(self, h):
        P = self.P
        self.load_unit_weights(O_GQ + h * 128, O_GK + h * 128, O_GV + h * 256, O_GR + h * 256, O_GA + h * 256, h, 4)
        lnsc = float(math.log(128.0 ** -0.5))
        dec = self.g_dec

        def stageApe(g):
            pq, pk = (self.PS[0], self.PS[1]) if g % 2 == 0 else (self.PS[4], self.PS[5])
            self.proj_feat(pq, lambda kc: (self.m_W1[:, kc, 0:128], self.m_W1), g)
            self.proj_feat(pk, lambda kc: (self.m_W1[:, kc, 128:256], self.m_W1), g)
            for d in range(2):
                pz = self.PS[2 + d]
                lr = self.m_lrg[d]
                self.proj_feat(pz, lambda kc, d=d: (self.wsm[:, kc, d * 16:(d + 1) * 16], self.wsm), g, M=16)
                P.add("dve", lambda e, pz=pz, lr=lr: e.tensor_copy(out=lr[:], in_=pz[0:16, :]), rd=[pz], wr=[lr])
                P.add("pe", lambda e, pz=pz, lr=lr, d=d: e.matmul(pz[:], lhsT=self.w2g[:, d * 512 + h * 128:d * 512 + (h + 1) * 128], rhs=lr[:], start=True, stop=True), rd=[self.w2g, lr], wr=[pz])

        def stageAch(g):
            pq, pk = (self.PS[0], self.PS[1]) if g % 2 == 0 else (self.PS[4], self.PS[5])
            for d in range(2):
                pz = self.PS[2 + d]
                Lin, Lc = self.g_Lin[d], self.g_Lc[d]
                col = d * 4 + h
                P.add("act", lambda e, pz=pz, Lin=Lin, col=col: e.activation(out=Lin[:], in_=pz[:], func=AF.Exp, scale=-1.0, bias=self.nb2[:, col:col + 1]), rd=[pz, self.nb2], wr=[Lin])
                P.add("act", lambda e, Lin=Lin: e.activation(out=Lin[:], in_=Lin[:], func=AF.Ln, bias=1.0), rd=[Lin], wr=[Lin])
                if d == 0:
                    P.add("dve", lambda e, Lin=Lin, Lc=Lc: e.tensor_tensor_scan(out=Lc[:], data0=self.rm[:], data1=Lin[:], initial=0.0, op0=ALU.mult, op1=ALU.add), rd=[self.rm, Lin], wr=[Lc])
                    last = Lc[:, 127::128]
                else:
                    P.add("dve", lambda e, Lin=Lin, Lc=Lc: e.tensor_tensor_scan(out=Lc[:][:, ::-1], data0=self.rm[:], data1=Lin[:][:, ::-1], initial=0.0, op0=ALU.mult, op1=ALU.add), rd=[self.rm, Lin], wr=[Lc])
                    last = Lc[:, 0::128]
                dsl = slice(d * 16 + g * 4, d * 16 + g * 4 + 4)
                P.add("act", lambda e, last=last, dsl=dsl: e.activation(out=dec[:, dsl], in_=last, func=AF.Exp, scale=-1.0 / TAU), rd=[Lc], wr=[dec])
                E1, E2, E3 = self.g_E[d]
                P.add("act", lambda e, E1=E1, Lc=Lc: e.activation(out=E1[:], in_=Lc[:], func=AF.Exp, scale=-1.0 / TAU, bias=lnsc), rd=[Lc], wr=[E1])
                P.add("act", lambda e, E2=E2, Lc=Lc: e.activation(out=E2[:], in_=Lc[:], func=AF.Exp, scale=1.0 / TAU), rd=[Lc], wr=[E2])
                qin, kin = self.g_qin[d][g], self.g_kin[d][g]
                ksT = self.g_ksT[d]
                P.add("dve", lambda e, qin=qin, E1=E1: e.tensor_tensor(out=qin[:], in0=pq[:], in1=E1[:], op=ALU.mult), rd=[pq, E1], wr=[qin])
                P.add("dve", lambda e, kin=kin, E2=E2: e.tensor_tensor(out=kin[:], in0=pk[:], in1=E2[:], op=ALU.mult), rd=[pk, E2], wr=[kin])
                P.add("dve", lambda e, E2=E2, E3=E3, dsl=dsl: e.tensor_tensor(out=E3[:].rearrange("p (c t) -> p c t", c=4), in0=E2[:].rearrange("p (c t) -> p c t", c=4), in1=dec[:, dsl].unsqueeze(2).to_broadcast([128, 4, 128]), op=ALU.mult), rd=[E2, dec], wr=[E3])
                P.add("dve", lambda e, ksT=ksT, E3=E3: e.tensor_tensor(out=ksT[:], in0=pk[:], in1=E3[:], op=ALU.mult), rd=[pk, E3], wr=[ksT])

        def stageT(g):
            for d in range(2):
                pt = self.PT[d]
                ksT = self.g_ksT[d]
                for c in range(4):
                    P.add("pe", lambda e, pt=pt, ksT=ksT, c=c: e.transpose(out=pt[:, c * 128:(c + 1) * 128], in_=ksT[:, c * 128:(c + 1) * 128], identity=self.ident[:]), rd=[ksT, self.ident], wr=[pt])
                ks = self.g_ks[d][g]
                P.add("dve", lambda e, ks=ks, pt=pt: e.tensor_copy(out=ks[:], in_=pt[:, 0:512].rearrange("p (t d) -> p t d", t=4)), rd=[pt], wr=[ks])

        def stageV(g):
            for pr in range(2):
                pv = self.PS[2 + pr]
                for k2 in range(2):
                    i = g * 4 + pr * 2 + k2
                    self.proj_tok(pv[:, k2 * 256:(k2 + 1) * 256], pv, self.m_W2, i, 0, 256)
                vt = self.g_v[g]
                P.add("act", lambda e, vt=vt, pv=pv, pr=pr: e.copy(out=vt[:, pr * 2:pr * 2 + 2, :], in_=pv[:].rearrange("p (t d) -> p t d", t=2)), rd=[pv], wr=[vt])

        stageApe(0)
        for g in range(4):
            stageAch(g)
            stageV(g)
            if g + 1 < 4:
                stageApe(g + 1)
            stageT(g)
        self.load_late_weights()
        uss, urs = self.ss[self.ssk], self.rs[self.ssk]
        self.ssk ^= 1
        for d in range(2):
            P.add("dve", lambda e, d=d: e.memset(self.g_S32[d][:], 0.0), wr=[self.g_S32[d]])
            P.add("dve", lambda e, d=d: e.memset(self.g_Sb[d][:], 0.0), wr=[self.g_Sb[d]])
        written = [False] * NT
        for step in range(NT):
            ctx = []
            for d in range(2):
                c = step if d == 0 else NT - 1 - step
                g, cc = c // 4, c % 4
                ctx.append(dict(d=d, c=c, cc=cc, pa=self.PS[3 * d], po=self.PS[3 * d + 1], pu=self.PS[3 * d + 2],
                                qin=self.g_qin[d][g], kin=self.g_kin[d][g], ks=self.g_ks[d][g], vt=self.g_v[g],
                                att=self.g_att[d], S32=self.g_S32[d], Sb=self.g_Sb[d],
                                mask=self.maskF if d == 0 else self.maskB, sl=slice(cc * 128, (cc + 1) * 128)))
            for k in ctx:
                pa, kin, qin, sl, pu, ks, vt, cc = k["pa"], k["kin"], k["qin"], k["sl"], k["pu"], k["ks"], k["vt"], k["cc"]
                P.add("pe", lambda e, pa=pa, kin=kin, qin=qin, sl=sl: e.matmul(pa[:, 0:128], lhsT=kin[:, sl], rhs=qin[:, sl], start=True, stop=True), rd=[kin, qin], wr=[pa])
                P.add("pe", lambda e, pu=pu, ks=ks, vt=vt, cc=cc: e.matmul(pu[:, 0:256], lhsT=ks[:, cc, :], rhs=vt[:, cc, :], start=True, stop=True), rd=[ks, vt], wr=[pu])
            for k in ctx:
                att, pa, mask = k["att"], k["pa"], k["mask"]
                P.add("dve", lambda e, att=att, pa=pa, mask=mask: e.tensor_tensor(out=att[:], in0=pa[:, 0:128], in1=mask[:], op=ALU.mult), rd=[pa, mask], wr=[att])
            for k in ctx:
                po, att, vt, cc, qin, Sb, sl = k["po"], k["att"], k["vt"], k["cc"], k["qin"], k["Sb"], k["sl"]
                P.add("pe", lambda e, po=po, att=att, vt=vt, cc=cc: e.matmul(po[:, 0:256], lhsT=att[:], rhs=vt[:, cc, :], start=True, stop=False), rd=[att, vt], wr=[po])
                P.add("pe", lambda e, po=po, qin=qin, Sb=Sb, sl=sl: e.matmul(po[:, 0:256], lhsT=qin[:, sl], rhs=Sb[:], start=False, stop=True), rd=[qin, Sb], wr=[po])
            for k in ctx:
                S32, pu = k["S32"], k["pu"]
                dcol = k["d"] * 16 + k["c"]
                P.add("dve", lambda e, S32=S32, pu=pu, dcol=dcol: e.scalar_tensor_tensor(out=S32[:], in0=S32[:], scalar=dec[:, dcol:dcol + 1], in1=pu[:, 0:256], op0=ALU.mult, op1=ALU.add), rd=[S32, pu, dec], wr=[S32])
            for k in ctx:
                S32, Sb = k["S32"], k["Sb"]
                P.add("act", lambda e, S32=S32, Sb=Sb: e.copy(out=Sb[:], in_=S32[:]), rd=[S32], wr=[Sb])
            for k in ctx:
                c, po = k["c"], k["po"]
                oc = self.g_o[c]
                if not written[c]:
                    written[c] = True
                    P.add("act", lambda e, oc=oc, po=po: e.copy(out=oc[:], in_=po[:, 0:256]), rd=[po], wr=[oc])
                else:
                    P.add("dve", lambda e, oc=oc, po=po: e.tensor_tensor(out=oc[:], in0=po[:, 0:256], in1=oc[:], op=ALU.add), rd=[po, oc], wr=[oc])
                    P.add("act", lambda e, oc=oc, c=c: e.activation(out=self.m_sq[:], in_=oc[:], func=AF.Square, accum_out=uss[:, c:c + 1]), rd=[oc], wr=[self.m_sq, uss])
        self.gating(self.g_o, silu_first=True, ssrs=(uss, urs))

    def gating(self, acc, silu_first, ssrs):
        P = self.P
        ss, rs = ssrs
        self.rsqrt(ss, rs, NT, 256 * EPS)
        gn = self.m_gn
        W12 = self.m_W12

        def head(i):
            b = i % 2
            pr = self.PS[3 * b]
            th, t12, mb = self.m_th[b], self.m_t12[b], self.m_mb[b]
            t1, t2 = t12[:, 0:256], t12[:, 256:512]
            ht = self.m_HT[i // 4]
            t0 = (i % 4) * 128
            for kc in range(8):
                P.add("pe", lambda e, kc=kc: e.matmul(pr[:], lhsT=ht[:, kc, t0:t0 + 128], rhs=W12[:, :, kc, :], start=(kc == 0), stop=(kc == 7)), rd=[ht, self.m_W1, self.m_W2], wr=[pr])
            if silu_first:
                P.add("act", lambda e: e.activation(out=th[:, 0:256], in_=pr[:, 0:256], func=AF.Silu), rd=[pr], wr=[th])
                P.add("act", lambda e: e.activation(out=th[:, 256:512], in_=pr[:, 256:512], func=AF.Tanh, scale=0.5), rd=[pr], wr=[th])
            else:
                P.add("act", lambda e: e.activation(out=th[:], in_=pr[:], func=AF.Tanh, scale=0.5), rd=[pr], wr=[th])
            P.add("dve", lambda e: e.scalar_tensor_tensor(out=t1, in0=acc[i][:], scalar=rs[:, i:i + 1], in1=gn[:], op0=ALU.mult, op1=ALU.mult), rd=[acc[i], rs, gn], wr=[t12])
            if silu_first:
                P.add("dve", lambda e: e.scalar_tensor_tensor(out=t2, in0=th[:, 256:512], scalar=1.0, in1=th[:, 0:256], op0=ALU.add, op1=ALU.mult), rd=[th, t12], wr=[t12])
                P.add("dve", lambda e: e.tensor_tensor(out=mb[:], in0=t1, in1=t2, op=ALU.mult), rd=[t12], wr=[mb])
            else:
                P.add("dve", lambda e: e.scalar_tensor_tensor(out=t2, in0=th[:, 0:256], scalar=1.0, in1=t1, op0=ALU.add, op1=ALU.mult), rd=[th, t12], wr=[t12])
                P.add("dve", lambda e: e.scalar_tensor_tensor(out=mb[:], in0=th[:, 256:512], scalar=1.0, in1=t2, op0=ALU.add, op1=ALU.mult), rd=[th, t12], wr=[mb])

        def tail(i):
            b = i % 2
            py = [self.PS[3 * b + 1], self.PS[3 * b + 2]]
            pt = self.PT[b]
            mb, mT = self.m_mb[b], self.m_mT[b]
            for k2 in range(2):
                P.add("pe", lambda e, k2=k2: e.transpose(out=pt[:, k2 * 128:(k2 + 1) * 128], in_=mb[:, k2 * 128:(k2 + 1) * 128], identity=self.ident[:]), rd=[mb, self.ident], wr=[pt])
            P.add("act", lambda e: e.copy(out=mT[:], in_=pt[:, 0:256]), rd=[pt], wr=[mT])
            for ch in range(2):
                pyc = py[ch]
                for k2 in range(2):
                    P.add("pe", lambda e, pyc=pyc, k2=k2, ch=ch: e.matmul(pyc[:], lhsT=mT[:, k2 * 128:(k2 + 1) * 128], rhs=self.m_wo[:, k2, ch * 512:(ch + 1) * 512], start=(k2 == 0), stop=(k2 == 1)), rd=[mT, self.m_wo], wr=[pyc])
                P.add("dve", lambda e, pyc=pyc, ch=ch: e.tensor_tensor(out=self.X[i][:, ch * 512:(ch + 1) * 512], in0=pyc[:], in1=self.X[i][:, ch * 512:(ch + 1) * 512], op=ALU.add), rd=[pyc, self.X[i]], wr=[self.X[i]])

        for i in range(NT + 1):
            if i < NT:
                head(i)
            if i >= 1:
                tail(i - 1)

    def ml_gates_proj(self, groups):
        P = self.P
        for d in range(2):
            A, B, G, sm = self.r_rows[d]
            for g in groups:
                pi, pf = self.PS[2 * d], self.PS[2 * d + 1]
                self.proj_feat(pi, lambda kc, d=d: (self.wsm[:, kc, 32 + d * 8:32 + d * 8 + 4], self.wsm), g, M=4)
                self.proj_feat(pf, lambda kc, d=d: (self.wsm[:, kc, 32 + d * 8 + 4:32 + d * 8 + 8], self.wsm), g, M=4)
                sl = slice(g * 512, (g + 1) * 512)
                P.add("act", lambda e, pi=pi, sl=sl, d=d: e.activation(out=A[:, sl], in_=pi[0:4, :], func=AF.Identity, bias=self.pcol[0:4, 64 + 2 * d:65 + 2 * d]), rd=[pi, self.pcol], wr=[A])
                P.add("act", lambda e, pf=pf, sl=sl, d=d: e.activation(out=B[:, sl], in_=pf[0:4, :], func=AF.Exp, scale=-1.0, bias=self.nbif[:, 2 * d + 1:2 * d + 2]), rd=[pf, self.nbif], wr=[B])

    def ml_gates(self):
        P = self.P
        lnf = float(math.log(math.sqrt(128.0)))
        for d in range(2):
            A, B, G, sm = self.r_rows[d]
            P.add("act", lambda e: e.activation(out=B[:], in_=B[:], func=AF.Ln, bias=1.0), rd=[B], wr=[B])
            rv = (lambda a: a) if d == 0 else (lambda a: a[:, ::-1])
            P.add("dve", lambda e, rv=rv: e.tensor_tensor_scan(out=rv(B[:]), data0=self.ones1[0:4, 0:1].to_broadcast([4, L]), data1=rv(B[:]), initial=0.0, op0=ALU.mult, op1=ALU.add), rd=[B, self.ones1], wr=[B])
            P.add("dve", lambda e: e.tensor_tensor(out=A[:], in0=A[:], in1=B[:], op=ALU.add), rd=[A, B], wr=[A])
            P.add("dve", lambda e, rv=rv: e.tensor_tensor_scan(out=rv(G[:]), data0=self.ones1[0:4, 0:1].to_broadcast([4, L]), data1=rv(A[:]), initial=-1e30, op0=ALU.mult, op1=ALU.max), rd=[A, self.ones1], wr=[G])
            gend = G[:, 127::128] if d == 0 else G[:, 0::128]
            P.add("dve", lambda e, gend=gend: e.tensor_copy(out=sm[:, 0:16], in_=gend), rd=[G], wr=[sm])
            P.add("dve", lambda e: e.memset(sm[:, 16:32], -1e30), rd=[sm], wr=[sm])
            if d == 0:
                P.add("dve", lambda e: e.tensor_copy(out=sm[:, 17:32], in_=sm[:, 0:15]), rd=[sm], wr=[sm])
            else:
                P.add("dve", lambda e: e.tensor_copy(out=sm[:, 16:31], in_=sm[:, 1:16]), rd=[sm], wr=[sm])
            P.add("dve", lambda e: e.tensor_tensor(out=sm[:, 32:48], in0=sm[:, 16:32], in1=sm[:, 0:16], op=ALU.subtract), rd=[sm], wr=[sm])
            P.add("act", lambda e: e.activation(out=sm[:, 32:48], in_=sm[:, 32:48], func=AF.Exp), rd=[sm], wr=[sm])
            gb_ = sm[:, 0:16].unsqueeze(2).to_broadcast([4, 16, 128])
            v3 = lambda t: t[:].rearrange("p (c t) -> p c t", c=16)
            P.add("dve", lambda e, gb_=gb_: e.tensor_tensor(out=v3(A), in0=v3(A), in1=gb_, op=ALU.subtract), rd=[A, sm], wr=[A])
            P.add("act", lambda e: e.activation(out=A[:], in_=A[:], func=AF.Exp), rd=[A], wr=[A])
            P.add("dve", lambda e, gb_=gb_: e.tensor_tensor(out=v3(B), in0=v3(B), in1=gb_, op=ALU.subtract), rd=[B, sm], wr=[B])
            P.add("act", lambda e: e.activation(out=B[:], in_=B[:], func=AF.Exp, bias=lnf), rd=[B], wr=[B])
        for d in range(2):
            A, B, G, sm = self.r_rows[d]
            pw, pe_ = self.PS[4], self.PS[5]
            for i in range(NT):
                P.add("pe", lambda e, i=i, pw=pw: e.matmul(pw[:, i * 4:(i + 1) * 4], lhsT=A[:, i * 128:(i + 1) * 128], rhs=self.identf[0:4, 0:4], start=True, stop=True), rd=[A, self.identf], wr=[pw])
                P.add("pe", lambda e, i=i, pe_=pe_: e.matmul(pe_[:, i * 4:(i + 1) * 4], lhsT=B[:, i * 128:(i + 1) * 128], rhs=self.identf[0:4, 0:4], start=True, stop=True), rd=[B, self.identf], wr=[pe_])
            P.add("dve", lambda e, d=d, pw=pw: e.tensor_copy(out=self.m_wst[d][:], in_=pw[:, 0:64]), rd=[pw], wr=[self.m_wst[d]])
            P.add("dve", lambda e, d=d, pe_=pe_: e.tensor_copy(out=self.m_en2[d][:], in_=pe_[:, 0:64]), rd=[pe_], wr=[self.m_en2[d]])
            pd = self.PS[d]
            for h in range(4):
                P.add("pe", lambda e, h=h, pd=pd: e.matmul(pd[:, h * 16:(h + 1) * 16], lhsT=self.sel[h][:], rhs=sm[:, 32:48], start=True, stop=True), rd=[self.sel[h], sm], wr=[pd])
            P.add("dve", lambda e, d=d, pd=pd: e.tensor_copy(out=self.m_decb[:, d * 64:(d + 1) * 64], in_=pd[:, 0:64]), rd=[pd], wr=[self.m_decb])

    def ml_unit(self, h):
        P = self.P
        self.load_unit_weights(O_MQK + h * 128, O_MQK + 512 + h * 128, O_MV + h * 256, O_MO + h * 256, O_GB + h * 256, h, 5)
        pre, cv = self.l_pre, self.l_cv
        for qk, dstT in enumerate((self.l_qTa, self.l_kTa)):
            P.add("dve", lambda e: e.memset(pre[:, 0:1], 0.0), rd=[pre], wr=[pre])
            P.add("dve", lambda e: e.memset(pre[:, L + 1:L + 2], 0.0), rd=[pre], wr=[pre])
            for g in range(4):
                pp = self.PS[g % 2]
                self.proj_feat(pp, lambda kc, qk=qk: (self.m_W1[:, kc, qk * 128:(qk + 1) * 128], self.m_W1), g)
                P.add("act", lambda e, pp=pp, g=g: e.copy(out=pre[:, 1 + g * 512:1 + (g + 1) * 512], in_=pp[:]), rd=[pp], wr=[pre])
            cb = 32 + (h * 2 + qk) * 4
            pc = self.pcol
            P.add("dve", lambda e, cb=cb: e.tensor_scalar(out=cv[:], in0=pre[:, 1:L + 1], scalar1=pc[:, cb + 1:cb + 2], scalar2=pc[:, cb + 3:cb + 4], op0=ALU.mult, op1=ALU.add), rd=[pre, pc], wr=[cv])
            P.add("dve", lambda e, cb=cb: e.scalar_tensor_tensor(out=cv[:], in0=pre[:, 0:L], scalar=pc[:, cb:cb + 1], in1=cv[:], op0=ALU.mult, op1=ALU.add), rd=[pre, pc, cv], wr=[cv])
            P.add("dve", lambda e, cb=cb: e.scalar_tensor_tensor(out=cv[:], in0=pre[:, 2:L + 2], scalar=pc[:, cb + 2:cb + 3], in1=cv[:], op0=ALU.mult, op1=ALU.add), rd=[pre, pc, cv], wr=[cv])
            P.add("act", lambda e, dstT=dstT: e.activation(out=dstT[:], in_=cv[:], func=AF.Silu), rd=[cv], wr=[dstT])
        for g in range(4):
            pt = self.PT[g % 2]
            for c in range(4):
                sl = slice((g * 4 + c) * 128, (g * 4 + c + 1) * 128)
                P.add("pe", lambda e, pt=pt, sl=sl, c=c: e.transpose(out=pt[:, c * 128:(c + 1) * 128], in_=self.l_kTa[:, sl], identity=self.ident[:]), rd=[self.l_kTa, self.ident], wr=[pt])
            P.add("act", lambda e, pt=pt, g=g: e.copy(out=self.l_ktok[:, g * 4:(g + 1) * 4, :], in_=pt[:, 0:512].rearrange("p (t d) -> p t d", t=4)), rd=[pt], wr=[self.l_ktok])
        for i in range(NT):
            pv = self.PS[2 + i % 2]
            self.proj_tok(pv[:, 0:256], pv, self.m_W2, i, 0, 256)
            for d in range(2):
                vw = self.l_vw[d]
                P.add("act", lambda e, vw=vw, pv=pv, i=i, d=d: e.activation(out=vw[:, i, 0:256], in_=pv[:, 0:256], func=AF.Copy, scale=self.m_wst[d][:, i * 4 + h:i * 4 + h + 1]), rd=[pv, self.m_wst[d]], wr=[vw])
        for d in range(2):
            vw = self.l_vw[d]
            wcol = self.m_wst[d][:].rearrange("p (t h) -> p t h", h=4)[:, :, h:h + 1]
            P.add("dve", lambda e, vw=vw, wcol=wcol: e.tensor_copy(out=vw[:, :, 256:257], in_=wcol), rd=[self.m_wst[d]], wr=[vw])
        self.load_late_weights()
        uss, urs = self.ss[self.ssk], self.rs[self.ssk]
        self.ssk ^= 1
        for d in range(2):
            P.add("dve", lambda e, d=d: e.memset(self.l_C32[d][:], 0.0), wr=[self.l_C32[d]])
        written = [False] * NT

        def mkctx(step):
            ctx = []
            for d in range(2):
                c = step if d == 0 else NT - 1 - step
                ctx.append(dict(d=d, c=c, sl=slice(c * 128, (c + 1) * 128), pa=self.PS[3 * d], pn=self.PS[3 * d + 1], pu=self.PS[3 * d + 2],
                                sT=self.l_sT[d], C32=self.l_C32[d], Cb=self.l_Cb[d], dn=self.l_dn[d], vw=self.l_vw[d],
                                mask=self.maskF if d == 0 else self.maskB, dcol=d * 64 + h * 16 + c, ecol=c * 4 + h))
            return ctx

        def front_pe(ctx):
            for k in ctx:
                pa, sl, pu, vw, c = k["pa"], k["sl"], k["pu"], k["vw"], k["c"]
                P.add("pe", lambda e, pa=pa, sl=sl: e.matmul(pa[:, 0:128], lhsT=self.l_kTa[:, sl], rhs=self.l_qTa[:, sl], start=True, stop=True), rd=[self.l_kTa, self.l_qTa], wr=[pa])
                P.add("pe", lambda e, pu=pu, vw=vw, c=c: e.matmul(pu[:, 0:257], lhsT=self.l_ktok[:, c, :], rhs=vw[:, c, 0:257], start=True, stop=True), rd=[self.l_ktok, vw], wr=[pu])

        def front_ev(ctx):
            for k in ctx:
                sT, pa, mask, Cb, C32, dcol = k["sT"], k["pa"], k["mask"], k["Cb"], k["C32"], k["dcol"]
                P.add("dve", lambda e, sT=sT, pa=pa, mask=mask: e.tensor_tensor(out=sT[:], in0=pa[:, 0:128], in1=mask[:], op=ALU.mult), rd=[pa, mask], wr=[sT])
                P.add("act", lambda e, Cb=Cb, C32=C32, dcol=dcol: e.activation(out=Cb[:, 0:257], in_=C32[:, 0:257], func=AF.Copy, scale=self.m_decb[:, dcol:dcol + 1]), rd=[C32, self.m_decb], wr=[Cb])

        cur = mkctx(0)
        front_pe(cur)
        front_ev(cur)
        for step in range(NT):
            ctx = cur
            for k in ctx:
                pn, sT, vw, c, Cb, sl = k["pn"], k["sT"], k["vw"], k["c"], k["Cb"], k["sl"]
                P.add("pe", lambda e, pn=pn, sT=sT, vw=vw, c=c: e.matmul(pn[:, 0:257], lhsT=sT[:], rhs=vw[:, c, 0:257], start=True, stop=False), rd=[sT, vw], wr=[pn])
                P.add("pe", lambda e, pn=pn, Cb=Cb, sl=sl: e.matmul(pn[:, 0:257], lhsT=self.l_qTa[:, sl], rhs=Cb[:, 0:257], start=False, stop=True), rd=[self.l_qTa, Cb], wr=[pn])
            for k in ctx:
                C32, pu, dcol = k["C32"], k["pu"], k["dcol"]
                P.add("dve", lambda e, C32=C32, pu=pu, dcol=dcol: e.scalar_tensor_tensor(out=C32[:, 0:257], in0=C32[:, 0:257], scalar=self.m_decb[:, dcol:dcol + 1], in1=pu[:, 0:257], op0=ALU.mult, op1=ALU.add), rd=[C32, pu, self.m_decb], wr=[C32])
            if step + 1 < NT:
                cur = mkctx(step + 1)
                front_pe(cur)
                front_ev(cur)
            for k in ctx:
                dn, pn, d, ecol, c = k["dn"], k["pn"], k["d"], k["ecol"], k["c"]
                P.add("dve", lambda e, dn=dn, pn=pn, d=d, ecol=ecol: e.tensor_scalar(out=dn[:, 2:3], in0=pn[:, 256:257], scalar1=self.m_en2[d][:, ecol:ecol + 1], scalar2=None, op0=ALU.max), rd=[pn, self.m_en2[d]], wr=[dn])
                P.add("dve", lambda e, dn=dn, pn=pn: e.scalar_tensor_tensor(out=dn[:, 0:1], in0=pn[:, 256:257], scalar=-1.0, in1=dn[:, 2:3], op0=ALU.mult, op1=ALU.max), rd=[pn, dn], wr=[dn])
                P.add("dve", lambda e, dn=dn: e.reciprocal(out=dn[:, 1:2], in_=dn[:, 0:1]), rd=[dn], wr=[dn])
                hc = self.l_h[c]
                if not written[c]:
                    written[c] = True
                    P.add("act", lambda e, hc=hc, pn=pn, dn=dn: e.activation(out=hc[:], in_=pn[:, 0:256], func=AF.Copy, scale=dn[:, 1:2]), rd=[pn, dn], wr=[hc])
                else:
                    P.add("dve", lambda e, hc=hc, pn=pn, dn=dn: e.scalar_tensor_tensor(out=hc[:], in0=pn[:, 0:256], scalar=dn[:, 1:2], in1=hc[:], op0=ALU.mult, op1=ALU.add), rd=[pn, dn, hc], wr=[hc])
                    P.add("act", lambda e, hc=hc, c=c: e.activation(out=self.m_sq[:], in_=hc[:], func=AF.Square, accum_out=uss[:, c:c + 1]), rd=[hc], wr=[self.m_sq, uss])
        self.gating(self.l_h, silu_first=False, ssrs=(uss, urs))


_CACHE = {}


def pack_small(inp):
    f = np.float32
    pcol = np.zeros((128, 72), f)
    pcol[:, 0:8] = np.asarray(inp["g_ffn1"], f).reshape(8, 128).T
    pcol[:, 8:16] = np.asarray(inp["g_mix"], f).reshape(8, 128).T
    pcol[:, 16:24] = np.asarray(inp["g_ffn2"], f).reshape(8, 128).T
    pcol[:, 24:28] = np.asarray(inp["gla_b2_fwd"], f).reshape(4, 128).T
    pcol[:, 28:32] = np.asarray(inp["gla_b2_bwd"], f).reshape(4, 128).T
    cw = np.asarray(inp["ml_conv_w"], f).reshape(3, 2, 4, 128)
    cb = np.asarray(inp["ml_conv_b"], f).reshape(2, 4, 128)
    for h in range(4):
        for qk in range(2):
            b = 32 + (h * 2 + qk) * 4
            for j in range(3):
                pcol[:, b + j] = cw[j, qk, h]
            pcol[:, b + 3] = cb[qk, h]
    pcol[0:4, 64:68] = np.asarray(inp["ml_b_if"], f).reshape(4, 4).T
    w2g = np.concatenate([np.asarray(inp["gla_w2_fwd"], f).reshape(16, 512), np.asarray(inp["gla_w2_bwd"], f).reshape(16, 512)], axis=1)
    gvec = np.stack([np.asarray(inp[k], f).reshape(1024) for k in ("g_ffn1", "g_mix", "g_ffn2", "g_final", "gla_norm", "ml_norm")], 0)
    return pcol, np.ascontiguousarray(w2g), np.ascontiguousarray(gvec)


def run(inputs, xs_per_core, nseq, stages=("ffn1", "mix", "ffn2"), units=None, ncores=8):
    key = (nseq, tuple(stages), None if units is None else tuple(units))
    if key not in _CACHE:
        _CACHE[key] = Builder(nseq, stages, units).build()
    nc = _CACHE[key]
    pcol, w2g, gvec = pack_small(inputs)
    f = np.float32
    shared = {
        "w_ffn1_in": np.ascontiguousarray(np.asarray(inputs["w_ffn1_in"], f).reshape(D, 2 * DFF)),
        "w_ffn1_out": np.ascontiguousarray(np.asarray(inputs["w_ffn1_out"], f).reshape(DFF, D)),
        "w_ffn2_in": np.ascontiguousarray(np.asarray(inputs["w_ffn2_in"], f).reshape(D, 2 * DFF)),
        "w_ffn2_out": np.ascontiguousarray(np.asarray(inputs["w_ffn2_out"], f).reshape(DFF, D)),
        "w_in": np.ascontiguousarray(np.asarray(inputs["w_in"], f).reshape(D, DIN)),
        "w_out": np.ascontiguousarray(np.asarray(inputs["w_out"], f).reshape(D, D)),
        "pcol": pcol, "w2g": w2g, "gvec": gvec,
    }
    in_maps = [dict(shared, x=np.ascontiguousarray(xs_per_core[c])) for c in range(ncores)]
    res = run_bass_kernel_spmd(nc, in_maps, core_ids=list(range(ncores)))
    return [res.results[c]["y"] for c in range(ncores)]


def kernel(**inputs):
    xp = np.asarray(inputs["x_prompt"], np.float32)
    xs = np.asarray(inputs["x_sample"], np.float32)
    per_core = []
    for c in range(8):
        per_core.append(np.concatenate([xp[4 * c:4 * c + 4].reshape(4 * L, D), xs[2 * c:2 * c + 2].reshape(2 * L, D)], axis=0))
    ys = run(inputs, per_core, 6)
    yp = np.stack([ys[c][0:4 * L].reshape(4, L, D) for c in range(8)], 0).reshape(32, L, D)
    ysm = np.stack([ys[c][4 * L:6 * L].reshape(2, L, D) for c in range(8)], 0).reshape(16, L, D)
    return (np.ascontiguousarray(yp, dtype=np.float32), np.ascontiguousarray(ysm, dtype=np.float32))
```

```python
import math
import types
import numpy as np
from contextlib import ExitStack
import concourse.bass as bass
import concourse.mybir as mybir
from concourse.bass_utils import run_bass_kernel_spmd

F32 = mybir.dt.float32
BF16 = mybir.dt.bfloat16
I32 = mybir.dt.int32
AF = mybir.ActivationFunctionType
ALU = mybir.AluOpType

ENGS = ("pe", "act", "dve", "pool", "sp")
SEM_EPOCH = 30000
DMA_RR = 6

D = 1024
L = 2048
NT = 16
DFF = 2816
NJ = 22
DIN = 8240
EPS = 1e-6
TAU = 16.0
O_GQ, O_GK, O_GV, O_GR, O_LR, O_MQK, O_MV, O_MO, O_MIF, O_GA, O_GB = 0, 512, 1024, 2048, 3072, 3104, 4128, 5152, 6176, 6192, 7216


class TT:
    __slots__ = ("ap", "w", "r")

    def __init__(self, ap):
        self.ap = ap
        self.w = None
        self.r = []

    def __getitem__(self, k):
        return self.ap[k]


class Ins:
    __slots__ = ("eng", "idx", "fn", "deps", "dma", "dma_idx", "signal", "cnt")

    def __init__(self, eng, idx, fn, deps, dma, dma_idx):
        self.eng, self.idx, self.fn, self.deps = eng, idx, fn, deps
        self.dma, self.dma_idx = dma, dma_idx
        self.signal = False
        self.cnt = 0


def _snap(fn):
    if fn is None or fn.__closure__ is None:
        return fn
    cells = []
    for c in fn.__closure__:
        try:
            cells.append(types.CellType(c.cell_contents))
        except ValueError:
            cells.append(c)
    g = types.FunctionType(fn.__code__, fn.__globals__, fn.__name__, fn.__defaults__, tuple(cells))
    g.__kwdefaults__ = fn.__kwdefaults__
    return g


class Prog:
    def __init__(self, nc):
        self.nc = nc
        self.q = {e: [] for e in ENGS}
        self.ndma = {e: 0 for e in ENGS}

    def add(self, eng, fn, rd=(), wr=(), dma=False):
        q = self.q[eng]
        idx = len(q)
        deps = set()
        for t in rd:
            if t.w is not None:
                deps.add(t.w)
        for t in wr:
            if t.w is not None:
                deps.add(t.w)
            deps.update(t.r)
        dma_idx = -1
        if dma:
            dma_idx = self.ndma[eng]
            self.ndma[eng] += 1
        ins = Ins(eng, idx, _snap(fn), deps, dma, dma_idx)
        q.append(ins)
        me = (eng, idx)
        for t in rd:
            t.r.append(me)
        for t in wr:
            t.w = me
            t.r = []
        return ins

    def barrier(self):
        last = []
        for e in ENGS:
            if self.q[e]:
                last.append((e, len(self.q[e]) - 1))
        dmas = []
        for e in ENGS:
            if self.ndma[e]:
                cnt = 0
                for ins in reversed(self.q[e]):
                    if ins.dma:
                        dmas.append((e, ins.idx))
                        cnt += 1
                        if cnt >= DMA_RR:
                            break
        for e in ENGS:
            ins = self.add(e, None)
            ins.deps.update(last)
            ins.deps.update(dmas)
            ins.deps.discard((e, ins.idx))

    def emit(self):
        with ExitStack() as st:
            self._emit(st)

    def _emit(self, st):
        nc = self.nc
        q = self.q
        for e in ENGS:
            for ins in q[e]:
                for (e2, i2) in ins.deps:
                    d = q[e2][i2]
                    if d.dma:
                        continue
                    if e2 == e and e == "pe":
                        continue
                    d.signal = True
        nsig = {}
        for e in ENGS:
            c = 0
            for ins in q[e]:
                if ins.signal and not ins.dma:
                    c += 1
                    ins.cnt = c
            nsig[e] = c
        sems = {}
        for e in ENGS:
            nep = nsig[e] // SEM_EPOCH + 1
            sems[e] = [st.enter_context(nc.semaphore(f"s_{e}_{k}")) for k in range(nep)]
        dsems = {}
        for e in ENGS:
            dsems[e] = [st.enter_context(nc.semaphore(f"d_{e}_{k}")) for k in range(DMA_RR)] if self.ndma[e] else []

        def target(d):
            if d.dma:
                return dsems[d.eng][d.dma_idx % DMA_RR], 16 * (d.dma_idx // DMA_RR + 1)
            c = d.cnt
            ep = (c - 1) // SEM_EPOCH
            return sems[d.eng][ep], c - ep * SEM_EPOCH

        stats = {e: [0, 0] for e in ENGS}

        def body(e):
            def run(eng):
                waited = {}
                for ins in q[e]:
                    need = {}
                    for (e2, i2) in ins.deps:
                        d = q[e2][i2]
                        if e2 == e and e == "pe" and not d.dma:
                            continue
                        s, v = target(d)
                        key = id(s)
                        if need.get(key, (None, 0))[1] < v:
                            need[key] = (s, v)
                    if ins.dma and ins.dma_idx >= DMA_RR:
                        s = dsems[e][ins.dma_idx % DMA_RR]
                        v = 16 * (ins.dma_idx // DMA_RR)
                        key = id(s)
                        if need.get(key, (None, 0))[1] < v:
                            need[key] = (s, v)
                    for key, (s, v) in need.items():
                        if waited.get(key, 0) >= v:
                            continue
                        eng.wait_ge(s, v)
                        waited[key] = v
                        stats[e][1] += 1
                    if ins.fn is None:
                        if ins.signal:
                            r = eng.nop()
                        else:
                            continue
                    else:
                        r = ins.fn(eng)
                    stats[e][0] += 1
                    if ins.dma:
                        s, _ = target(ins)
                        r.then_inc(s, 16)
                    elif ins.signal:
                        s, _ = target(ins)
                        r.then_inc(s, 1)
            return run

        block = st.enter_context(nc.Block())
        reg = {"pe": block.tensor, "act": block.scalar, "dve": block.vector,
               "pool": block.gpsimd, "sp": block.sync}
        for e in ENGS:
            if q[e]:
                reg[e](body(e))
        self.stats = stats


class Builder:
    def __init__(self, nseq, stages=("ffn1", "mix", "ffn2"), units=None):
        self.nseq = nseq
        self.stages = stages
        self.units = units if units is not None else [("gla", h) for h in range(4)] + [("ml", h) for h in range(4)]
        self.nc = bass.Bass("TRN2", target_bir_lowering=False)
        self.st = ExitStack()

    def dram(self, name, shape, kind="ExternalInput"):
        return self.nc.dram_tensor(name, shape, F32, kind=kind).ap()

    def sb(self, name, shape, dt):
        return self.st.enter_context(self.nc.sbuf_tensor(name, shape, dt))

    def carve(self, nelem, dt):
        n16 = nelem * (2 if dt == F32 else 1)
        n16 = (n16 + 15) // 16 * 16
        a = self.scr[:, self.sp:self.sp + n16]
        self.sp += n16
        assert self.sp <= self.SCR, (self.sp, self.SCR)
        if dt == F32:
            a = a.bitcast(F32)
        return a

    def build(self):
        nc = self.nc
        P = self.P = Prog(nc)
        ns = self.nseq
        self.x_d = self.dram("x", [ns * L, D])
        self.y_d = self.dram("y", [ns * L, D], kind="ExternalOutput")
        self.w1i = self.dram("w_ffn1_in", [D, 2 * DFF])
        self.w1o = self.dram("w_ffn1_out", [DFF, D])
        self.w2i = self.dram("w_ffn2_in", [D, 2 * DFF])
        self.w2o = self.dram("w_ffn2_out", [DFF, D])
        self.win = self.dram("w_in", [D, DIN])
        self.wout = self.dram("w_out", [D, D])
        self.pcol_d = self.dram("pcol", [128, 72])
        self.w2g_d = self.dram("w2g", [16, 1024])
        self.gvec_d = self.dram("gvec", [6, 1024])

        with self.st:
            self.alloc()
            self.setup_consts()
            for s in range(ns):
                self.sequence(s)
            P.barrier()
            P.emit()
        return nc

    def alloc(self):
        sb = self.sb
        self.xbuf = sb("xbuf", [128, NT, D], F32)
        self.X = [TT(self.xbuf[:, i, :]) for i in range(NT)]
        self.identf = TT(sb("identf", [128, 128], F32))
        self.ident = TT(sb("ident", [128, 128], BF16))
        self.maskF = TT(sb("maskF", [128, 128], F32))
        self.maskB = TT(sb("maskB", [128, 128], F32))
        self.pcol = TT(sb("pcol_s", [128, 72], F32))
        self.nb2 = TT(sb("nb2", [128, 8], F32))
        self.nbif = TT(sb("nbif", [4, 4], F32))
        self.w2g = TT(sb("w2g_s", [16, 1024], BF16))
        self.wsm = TT(sb("wsm", [128, 8, 48], BF16))
        self.rm = TT(sb("rm", [128, 512], F32))
        self.ones1 = TT(sb("ones1", [128, 1], F32))
        self.sel = [TT(sb(f"sel{h}", [4, 128], F32)) for h in range(4)]
        self.ss = [TT(sb(f"ss{k}", [128, 16], F32)) for k in range(2)]
        self.rs = [TT(sb(f"rs{k}", [128, 16], F32)) for k in range(2)]
        self.rtmp = TT(sb("rtmp", [128, 48], F32))
        self.ssk = 0
        self.SCR = 68032
        self.scr = sb("scr", [128, self.SCR], BF16)
        ps = lambda n, shape, dt: TT(self.st.enter_context(self.nc.psum_tensor(n, shape, dt)))
        self.PS = [ps(f"ps{k}", [128, 512], F32) for k in range(6)]
        self.PT = [ps(f"pt{k}", [128, 1024], BF16) for k in range(2)]
        self.layout_ffn()
        self.layout_mix()

    def layout_ffn(self):
        self.sp = 0
        c = self.carve
        self.f_gb = TT(c(1024, F32))
        self.f_gfin = TT(c(1024, F32))
        hT = c(8 * 1024, BF16).rearrange("p (k t) -> p k t", k=8)
        self.f_hTbuf = hT
        self.f_HT = [TT(hT[:, :, g * 512:(g + 1) * 512]) for g in range(2)]
        self.f_hb = [TT(c(1024, BF16)), TT(c(1024, BF16))]
        self.f_junk = TT(c(1024, BF16))
        act = c(11 * 1024, BF16).rearrange("p (j t) -> p j t", j=11)
        self.f_actbuf = act
        self.f_ACT = [TT(act[:, :, g * 512:(g + 1) * 512]) for g in range(2)]
        self.f_wo = [TT(c(11 * 1024, BF16).rearrange("p (j d) -> p j d", j=11)) for _ in range(2)]
        self.f_wi = [TT(c(8 * 2 * 128, BF16).rearrange("p (k a c) -> p k a c", k=8, a=2)) for _ in range(4)]
        self.f_sa = [TT(c(512, BF16)) for _ in range(2)]
        self.f_yst = [TT(c(1024, F32)) for _ in range(2)]
        self.f_end = self.sp
        print('ffn scratch units', self.sp)

    def layout_mix(self):
        self.sp = 0
        c = self.carve
        alt = c(4096, BF16)
        self.m_gb = TT(alt[:, 0:2048].bitcast(F32))
        hT = c(8 * L, BF16).rearrange("p (k t) -> p k t", k=8)
        self.m_hTbuf = hT
        self.m_HT = [TT(hT[:, :, g * 512:(g + 1) * 512]) for g in range(4)]
        _hb = TT(alt[:, 2048:3072])
        self.m_hb = [_hb, _hb]
        self.m_junk = TT(alt[:, 3072:4096])
        w12 = c(2 * 8 * 256, BF16).rearrange("p (w k c) -> p w k c", w=2, k=8)
        w12b = alt.rearrange("p (w k c) -> p w k c", w=2, k=8)
        self.m_W12s = [w12, w12b]
        self.m_W1s = [TT(w12[:, 0]), TT(w12b[:, 0])]
        self.m_W2s = [TT(w12[:, 1]), TT(w12b[:, 1])]
        self.set_w(0)
        self.m_wo = TT(c(2 * 1024, BF16).rearrange("p (k d) -> p k d", k=2))
        self.m_gn = TT(c(256, F32))
        self.m_lrg = [TT(c(512, BF16)[0:16, :]) for _ in range(2)]
        self.m_wst = [TT(c(64, F32)) for _ in range(2)]
        self.m_en2 = [TT(c(64, F32)) for _ in range(2)]
        self.m_decb = TT(c(128, F32))
        B = [TT(c(512, F32)) for _ in range(5)]
        self.m_B = B
        self.m_th = [B[0], B[1]]
        self.m_t12 = [B[2], B[3]]
        self.m_mb = [TT(c(256, BF16)) for _ in range(2)]
        self.m_mT = [TT(c(256, BF16)) for _ in range(2)]
        self.m_sq = TT(c(256, BF16))
        base = self.sp
        self.g_qin = [[TT(a[:, g * 512:(g + 1) * 512]) for g in range(4)] for a in (c(L, BF16), c(L, BF16))]
        self.g_kin = [[TT(a[:, g * 512:(g + 1) * 512]) for g in range(4)] for a in (c(L, BF16), c(L, BF16))]
        ksb = [c(NT * 128, BF16).rearrange("p (t d) -> p t d", t=NT) for _ in range(2)]
        self.g_ksbuf = ksb
        self.g_ks = [[TT(a[:, g * 4:(g + 1) * 4, :]) for g in range(4)] for a in ksb]
        vb = c(NT * 256, BF16).rearrange("p (t d) -> p t d", t=NT)
        self.g_vbuf = vb
        self.g_v = [TT(vb[:, g * 4:(g + 1) * 4, :]) for g in range(4)]
        ob = c(NT * 256, F32).rearrange("p (t d) -> p t d", t=NT)
        self.g_obuf = ob
        self.g_o = [TT(ob[:, i, :]) for i in range(NT)]
        self.g_Lin = [TT(c(512, F32)), self.m_B[0]]
        self.g_Lc = [TT(c(512, F32)), self.m_B[1]]
        self.g_E = [[TT(c(512, F32)) for _ in range(3)], [self.m_B[2], self.m_B[3], self.m_B[4]]]
        self.g_ksT = [TT(c(512, BF16)) for _ in range(2)]
        self.g_nl = [TT(c(4, F32)) for _ in range(2)]
        self.g_dec = TT(c(32, F32))
        self.g_att = [TT(c(128, BF16)) for _ in range(2)]
        self.g_S32 = [TT(c(256, F32)) for _ in range(2)]
        self.g_Sb = [TT(c(256, BF16)) for _ in range(2)]
        gla_end = self.sp
        self.sp = base
        self.l_pre = TT(c(L + 2, F32))
        self.l_cv = TT(c(L, F32))
        qa, ka = c(L, BF16), c(L, BF16)
        self.l_qTa = TT(qa)
        self.l_kTa = TT(ka)
        kt = c(NT * 128, BF16).rearrange("p (t d) -> p t d", t=NT)
        self.l_ktok = TT(kt)
        vw = [c(NT * 258, BF16).rearrange("p (t d) -> p t d", t=NT) for _ in range(2)]
        self.l_vwbuf = vw
        self.l_vw = [TT(a) for a in vw]
        hb_ = c(NT * 256, F32).rearrange("p (t d) -> p t d", t=NT)
        self.l_h = [TT(hb_[:, i, :]) for i in range(NT)]
        self.l_sT = [TT(c(128, BF16)) for _ in range(2)]
        self.l_C32 = [TT(c(258, F32)) for _ in range(2)]
        self.l_Cb = [TT(c(258, BF16)) for _ in range(2)]
        self.l_dn = [TT(c(4, F32)) for _ in range(2)]
        ml_end = self.sp
        self.sp = base
        self.r_rows = [(TT(c(L, F32)[0:4, :]), TT(c(L, F32)[0:4, :]), TT(c(L, F32)[0:4, :]), TT(c(64, F32)[0:4, :])) for _ in range(2)]
        self.sp = max(gla_end, ml_end, self.sp)
        self.m_end = self.sp
        print('mixer scratch units', self.sp, 'gla_end', gla_end, 'ml_end', ml_end)

    def rsqrt(self, src, dst, n, addc):
        P = self.P
        t = self.rtmp
        a, y0 = t[:, 0:n], t[:, 16:16 + n]
        w = t[:, 32:32 + n]
        P.add("dve", lambda e: e.tensor_scalar(out=a, in0=src[:, 0:n], scalar1=addc, scalar2=None, op0=ALU.add), rd=[src], wr=[t])
        P.add("dve", lambda e: e.tensor_single_scalar(out=w.bitcast(I32), in_=a.bitcast(I32), scalar=1, op=ALU.arith_shift_right), rd=[t], wr=[t])
        P.add("dve", lambda e: e.tensor_scalar(out=y0.bitcast(I32), in0=w.bitcast(I32), scalar1=-1, scalar2=1597463007, op0=ALU.mult, op1=ALU.add), rd=[t], wr=[t])
        for it in range(3):
            P.add("dve", lambda e: e.tensor_tensor(out=w, in0=a, in1=y0, op=ALU.mult), rd=[t], wr=[t])
            P.add("dve", lambda e: e.tensor_tensor(out=w, in0=w, in1=y0, op=ALU.mult), rd=[t], wr=[t])
            P.add("dve", lambda e: e.tensor_scalar(out=w, in0=w, scalar1=-0.5, scalar2=1.5, op0=ALU.mult, op1=ALU.add), rd=[t], wr=[t])
            if it < 2:
                P.add("dve", lambda e: e.tensor_tensor(out=y0, in0=y0, in1=w, op=ALU.mult), rd=[t], wr=[t])
            else:
                P.add("dve", lambda e: e.tensor_tensor(out=dst[:, 0:n], in0=y0, in1=w, op=ALU.mult), rd=[t], wr=[dst])

    def setup_consts(self):
        P = self.P
        idf, idb = self.identf, self.ident
        P.add("pool", lambda e: e.memset(idf[:], 0.0), wr=[idf])
        P.add("pool", lambda e: e.affine_select(out=idf[:], in_=idf[:], pattern=[[-1, 128]], compare_op=ALU.not_equal, fill=1.0, base=0, channel_multiplier=1), rd=[idf], wr=[idf])
        P.add("dve", lambda e: e.tensor_copy(out=idb[:], in_=idf[:]), rd=[idf], wr=[idb])
        mF, mB = self.maskF, self.maskB
        P.add("pool", lambda e: e.memset(mF[:], 1.0), wr=[mF])
        P.add("pool", lambda e: e.affine_select(out=mF[:], in_=mF[:], pattern=[[1, 128]], compare_op=ALU.is_ge, fill=0.0, base=0, channel_multiplier=-1), rd=[mF], wr=[mF])
        P.add("pool", lambda e: e.memset(mB[:], 1.0), wr=[mB])
        P.add("pool", lambda e: e.affine_select(out=mB[:], in_=mB[:], pattern=[[-1, 128]], compare_op=ALU.is_ge, fill=0.0, base=0, channel_multiplier=1), rd=[mB], wr=[mB])
        P.add("sp", lambda e: e.dma_start(out=self.pcol[:], in_=self.pcol_d[:, :]), wr=[self.pcol], dma=True)
        P.add("pool", lambda e: e.dma_start(out=self.w2g[:], in_=self.w2g_d[:, :]), wr=[self.w2g], dma=True)
        wsrc = self.win.rearrange("(k p) c -> p k c", p=128)
        P.add("pool", lambda e: e.dma_start(out=self.wsm[:, :, 0:32], in_=wsrc[:, :, O_LR:O_LR + 32]), wr=[self.wsm], dma=True)
        P.add("pool", lambda e: e.dma_start(out=self.wsm[:, :, 32:48], in_=wsrc[:, :, O_MIF:O_MIF + 16]), wr=[self.wsm], dma=True)
        P.add("dve", lambda e: e.tensor_scalar(out=self.nb2[:], in0=self.pcol[:, 24:32], scalar1=-1.0, scalar2=None, op0=ALU.mult), rd=[self.pcol], wr=[self.nb2])
        P.add("dve", lambda e: e.tensor_scalar(out=self.nbif[:], in0=self.pcol[0:4, 64:68], scalar1=-1.0, scalar2=None, op0=ALU.mult), rd=[self.pcol], wr=[self.nbif])
        rm = self.rm
        P.add("dve", lambda e: e.memset(rm[:], 1.0), wr=[rm])
        P.add("dve", lambda e: e.memset(rm[:].rearrange("p (c t) -> p c t", c=4)[:, :, 0:1], 0.0), rd=[rm], wr=[rm])
        P.add("dve", lambda e: e.memset(self.ones1[:], 1.0), wr=[self.ones1])
        for h in range(4):
            P.add("dve", lambda e, h=h: e.tensor_copy(out=self.sel[h][:], in_=idf[0:4, h:h + 1].to_broadcast([4, 128])), rd=[idf], wr=[self.sel[h]])

    def sequence(self, s):
        P = self.P
        for i in range(NT):
            r0 = s * L + i * 128
            P.add("sp", lambda e, i=i, r0=r0: e.dma_start(out=self.X[i][:], in_=self.x_d[r0:r0 + 128, :]), wr=[self.X[i]], dma=True)
        if "ffn1" in self.stages:
            self.ffn(s, self.w1i, self.w1o, 0, final=False)
        if "mix" in self.stages:
            P.barrier()
            self.mixer(s)
            P.barrier()
        self.ffn(s, self.w2i, self.w2o, 2, final=True, skip_ffn=("ffn2" not in self.stages))

    def load_gvec(self, dst, row):
        self.P.add("sp", lambda e: e.dma_start(out=dst[:], in_=self.gvec_d[row:row + 1, :].partition_broadcast(128)[:, 0, :]), wr=[dst], dma=True)

    def norm_T(self, tiles, gb, hb2, junk, dst_fn, dst_tt_fn):
        k = self.ssk
        self.ssk ^= 1
        for j in range(len(tiles)):
            self.norm_sq(tiles, j, junk, k)
        self.rsqrt(self.ss[k], self.rs[k], len(tiles), D * EPS)
        self.norm_B(tiles, gb, hb2, dst_fn, dst_tt_fn, k)

    def norm_sq(self, tiles, j, junk, k):
        ss = self.ss[k]
        i = tiles[j]
        self.P.add("act", lambda e: e.activation(out=junk[:], in_=self.X[i][:], func=AF.Square, accum_out=ss[:, j:j + 1]), rd=[self.X[i]], wr=[junk, ss])

    def norm_B(self, tiles, gb, hb2, dst_fn, dst_tt_fn, k):
        P = self.P
        rs = self.rs[k]
        for j, i in enumerate(tiles):
            hb = hb2[j % 2]
            pt = self.PT[j % 2]
            P.add("dve", lambda e, i=i, j=j, hb=hb: e.scalar_tensor_tensor(out=hb[:], in0=self.X[i][:], scalar=rs[:, j:j + 1], in1=gb[:], op0=ALU.mult, op1=ALU.mult), rd=[self.X[i], rs, gb], wr=[hb])
            for kc in range(8):
                P.add("pe", lambda e, kc=kc, hb=hb, pt=pt: e.transpose(out=pt[:, kc * 128:(kc + 1) * 128], in_=hb[:, kc * 128:(kc + 1) * 128], identity=self.ident[:]), rd=[hb, self.ident], wr=[pt])
            dst = dst_fn(j)
            P.add("act", lambda e, dst=dst, pt=pt: e.copy(out=dst, in_=pt[:].rearrange("p (k t) -> p k t", k=8)), rd=[pt], wr=[dst_tt_fn(j)])

    def ffn(self, s, wi_d, wo_d, grow, final, skip_ffn=False):
        P = self.P
        gb = self.f_gb
        if not skip_ffn:
            self.load_gvec(gb, grow)
            P.add("dve", lambda e: e.tensor_scalar(out=gb[:], in0=gb[:], scalar1=float(math.sqrt(D)), scalar2=None, op0=ALU.mult), rd=[gb], wr=[gb])
        if final:
            self.load_gvec(self.f_gfin, 3)
            P.add("dve", lambda e: e.tensor_scalar(out=self.f_gfin[:], in0=self.f_gfin[:], scalar1=float(math.sqrt(D)), scalar2=None, op0=ALU.mult), rd=[self.f_gfin], wr=[self.f_gfin])
        wi_src = wi_d.rearrange("(k p) (a c) -> p k a c", p=128, a=2)
        wslot = 0
        hTb = self.f_hTbuf
        dstf, dsttt = (lambda j: hTb[:, :, j * 128:(j + 1) * 128]), (lambda j: self.f_HT[j // 4])
        for p in range(2):
            tiles = list(range(p * 8, p * 8 + 8))
            ntiles = list(range(8, 16))
            if not skip_ffn:
                if p == 0:
                    self.norm_T(tiles, gb, self.f_hb, self.f_junk, dstf, dsttt)
                for hf in range(2):
                    hoist = (p == 0 and hf == 1)
                    if hoist:
                        kn = self.ssk
                        self.ssk ^= 1
                    wo = self.f_wo[hf]
                    P.add("pool", lambda e, wo=wo, hf=hf: e.dma_start(out=wo[:], in_=wo_d[hf * 1408:(hf + 1) * 1408, :].rearrange("(j p) d -> p j d", p=128)), wr=[wo], dma=True)
                    for jj in range(11):
                        j = hf * 11 + jj
                        wi = self.f_wi[wslot % 4]
                        wslot += 1
                        for a in range(2):
                            P.add("pool", lambda e, wi=wi, j=j, a=a: e.dma_start(out=wi[:, :, a, :], in_=wi_src[:, :, a, j * 128:(j + 1) * 128]), wr=[wi], dma=True)
                        for g in range(2):
                            pa, pg = self.PS[2 * g], self.PS[2 * g + 1]
                            ht = self.f_HT[g]
                            for a, pp in ((0, pa), (1, pg)):
                                for kc in range(8):
                                    P.add("pe", lambda e, a=a, pp=pp, kc=kc, wi=wi, ht=ht: e.matmul(pp[:], lhsT=wi[:, kc, a, :], rhs=ht[:, kc, :], start=(kc == 0), stop=(kc == 7)), rd=[wi, ht], wr=[pp])
                            sa = self.f_sa[g]
                            P.add("act", lambda e, sa=sa, pa=pa: e.activation(out=sa[:], in_=pa[:], func=AF.Silu), rd=[pa], wr=[sa])
                            at = self.f_ACT[g]
                            P.add("dve", lambda e, at=at, sa=sa, pg=pg, jj=jj: e.tensor_tensor(out=at[:, jj, :], in0=sa[:], in1=pg[:], op=ALU.mult), rd=[sa, pg], wr=[at])
                        if hoist and jj < 8:
                            self.norm_sq(ntiles, jj, self.f_junk, kn)
                        if hoist and jj == 8:
                            self.rsqrt(self.ss[kn], self.rs[kn], 8, D * EPS)
                    if hoist:
                        self.norm_B(ntiles, gb, self.f_hb, dstf, dsttt, kn)
                    for jt, i in enumerate(tiles):
                        at = self.f_ACT[jt // 4]
                        t0 = (jt % 4) * 128
                        for ch in range(2):
                            py = self.PS[4 + (jt * 2 + ch) % 2]
                            for jj in range(11):
                                P.add("pe", lambda e, py=py, at=at, t0=t0, jj=jj, ch=ch, wo=wo: e.matmul(py[:], lhsT=at[:, jj, t0:t0 + 128], rhs=wo[:, jj, ch * 512:(ch + 1) * 512], start=(jj == 0), stop=(jj == 10)), rd=[at, wo], wr=[py])
                            P.add("dve", lambda e, py=py, i=i, ch=ch: e.scalar_tensor_tensor(out=self.X[i][:, ch * 512:(ch + 1) * 512], in0=py[:], scalar=0.5, in1=self.X[i][:, ch * 512:(ch + 1) * 512], op0=ALU.mult, op1=ALU.add), rd=[py, self.X[i]], wr=[self.X[i]])
            if final:
                k = self.ssk
                self.ssk ^= 1
                ss, rs = self.ss[k], self.rs[k]
                for j, i in enumerate(tiles):
                    P.add("act", lambda e, i=i, j=j: e.activation(out=self.f_junk[:], in_=self.X[i][:], func=AF.Square, accum_out=ss[:, j:j + 1]), rd=[self.X[i]], wr=[self.f_junk, ss])
                self.rsqrt(ss, rs, 8, D * EPS)
                for j, i in enumerate(tiles):
                    yst = self.f_yst[j % 2]
                    r0 = s * L + i * 128
                    P.add("dve", lambda e, i=i, j=j, yst=yst: e.scalar_tensor_tensor(out=yst[:], in0=self.X[i][:], scalar=rs[:, j:j + 1], in1=self.f_gfin[:], op0=ALU.mult, op1=ALU.mult), rd=[self.X[i], rs, self.f_gfin], wr=[yst])
                    P.add("sp", lambda e, yst=yst, r0=r0: e.dma_start(out=self.y_d[r0:r0 + 128, :], in_=yst[:]), rd=[yst], dma=True)

    def mixer(self, s):
        P = self.P
        gb = self.m_gb
        self.load_gvec(gb, 1)
        P.add("dve", lambda e: e.tensor_scalar(out=gb[:], in0=gb[:], scalar1=float(math.sqrt(D)), scalar2=None, op0=ALU.mult), rd=[gb], wr=[gb])
        hTb = self.m_hTbuf
        need_ml = any(u[0] == "ml" for u in self.units)
        for half in range(2):
            tiles = list(range(half * 8, half * 8 + 8))
            self.norm_T(tiles, gb, self.m_hb, self.m_junk,
                        lambda j, half=half: hTb[:, :, (half * 8 + j) * 128:(half * 8 + j + 1) * 128],
                        lambda j, half=half: self.m_HT[(half * 8 + j) // 4])
            if need_ml:
                self.ml_gates_proj([2 * half, 2 * half + 1])
        if need_ml:
            self.ml_gates()
        P.barrier()
        prev = None
        self._preloaded = False
        for idx, (kind, h) in enumerate(self.units):
            if prev is not None and prev != kind:
                P.barrier()
            prev = kind
            self._unit, self._par = (kind, h), idx % 2
            self._next = self.units[idx + 1] if idx + 1 < len(self.units) else None
            self.set_w(idx % 2)
            if kind == "gla":
                self.gla_unit(h)
            else:
                self.ml_unit(h)
            self._preloaded = self._next is not None

    def proj_feat(self, pp, lhs_fn, g, M=128):
        P = self.P
        ht = self.m_HT[g]
        for kc in range(8):
            lhs, rd = lhs_fn(kc)
            P.add("pe", lambda e, kc=kc, lhs=lhs: e.matmul(pp[0:M, :], lhsT=lhs, rhs=ht[:, kc, :], start=(kc == 0), stop=(kc == 7)), rd=[rd, ht], wr=[pp])

    def proj_tok(self, out_ap, pp, w, i, c0, n):
        P = self.P
        ht = self.m_HT[i // 4]
        t0 = (i % 4) * 128
        for kc in range(8):
            P.add("pe", lambda e, kc=kc: e.matmul(out_ap, lhsT=ht[:, kc, t0:t0 + 128], rhs=w[:, kc, c0:c0 + n], start=(kc == 0), stop=(kc == 7)), rd=[ht, w], wr=[pp])

    def lr_proj(self):
        P = self.P
        for d in range(2):
            for g in range(4):
                pp = self.PS[(d * 4 + g) % 2]
                self.proj_feat(pp, lambda kc, d=d: (self.wsm[:, kc, d * 16:(d + 1) * 16], self.wsm), g, M=16)
                lr = self.m_lrT[d]
                P.add("act", lambda e, pp=pp, lr=lr, g=g: e.copy(out=lr[:, g * 512:(g + 1) * 512], in_=pp[0:16, :]), rd=[pp], wr=[lr])

    def set_w(self, par):
        self.m_W12, self.m_W1, self.m_W2 = self.m_W12s[par], self.m_W1s[par], self.m_W2s[par]

    def load_qkv(self, unit, par):
        P = self.P
        kind, h = unit
        if kind == "gla":
            cq, ck, cv = O_GQ + h * 128, O_GK + h * 128, O_GV + h * 256
        else:
            cq, ck, cv = O_MQK + h * 128, O_MQK + 512 + h * 128, O_MV + h * 256
        wsrc = self.win.rearrange("(k p) c -> p k c", p=128)
        W1, W2 = self.m_W1s[par], self.m_W2s[par]
        P.add("pool", lambda e: e.dma_start(out=W1[:, :, 0:128], in_=wsrc[:, :, cq:cq + 128]), wr=[W1], dma=True)
        P.add("pool", lambda e: e.dma_start(out=W1[:, :, 128:256], in_=wsrc[:, :, ck:ck + 128]), wr=[W1], dma=True)
        P.add("pool", lambda e: e.dma_start(out=W2[:], in_=wsrc[:, :, cv:cv + 256]), wr=[W2], dma=True)

    def load_unit_weights(self, cq, ck, cv, cr, cg, h, gnrow):
        P = self.P
        if not self._preloaded:
            self.load_qkv(self._unit, self._par)
        self._late = (cr, cg)
        P.add("pool", lambda e: e.dma_start(out=self.m_wo[:], in_=self.wout[h * 256:(h + 1) * 256, :].rearrange("(k p) d -> p k d", p=128)), wr=[self.m_wo], dma=True)
        gn = self.m_gn
        P.add("sp", lambda e: e.dma_start(out=gn[:], in_=self.gvec_d[gnrow:gnrow + 1, h * 256:(h + 1) * 256].partition_broadcast(128)[:, 0, :]), wr=[gn], dma=True)
        gsc = 8.0 if gnrow == 4 else 4.0
        P.add("dve", lambda e: e.tensor_scalar(out=gn[:], in0=gn[:], scalar1=gsc, scalar2=None, op0=ALU.mult), rd=[gn], wr=[gn])

    def load_late_weights(self):
        P = self.P
        wsrc = self.win.rearrange("(k p) c -> p k c", p=128)
        cr, cg = self._late
        W1, W2 = self.m_W1, self.m_W2
        P.add("pool", lambda e: e.dma_start(out=W1[:], in_=wsrc[:, :, cr:cr + 256]), wr=[W1], dma=True)
        P.add("pool", lambda e: e.dma_start(out=W2[:], in_=wsrc[:, :, cg:cg + 256]), wr=[W2], dma=True)
        if self._next is not None:
            self.load_qkv(self._next, 1 - self._par)

    def gla_unit(self, h):
        P = self.P
        self.load_unit_weights(O_GQ + h * 128, O_GK + h * 128, O_GV + h * 256, O_GR + h * 256, O_GA + h * 256, h, 4)
        lnsc = float(math.log(128.0 ** -0.5))
        dec = self.g_dec
        order = [0, 3, 1, 2]
        gpar = {g: n_ % 2 for n_, g in enumerate(order)}

        def stageApe(g):
            pq, pk = (self.PS[0], self.PS[1]) if gpar[g] == 0 else (self.PS[4], self.PS[5])
            self.proj_feat(pq, lambda kc: (self.m_W1[:, kc, 0:128], self.m_W1), g)
            self.proj_feat(pk, lambda kc: (self.m_W1[:, kc, 128:256], self.m_W1), g)
            for d in range(2):
                pz = self.PS[2 + d]
                lr = self.m_lrg[d]
                self.proj_feat(pz, lambda kc, d=d: (self.wsm[:, kc, d * 16:(d + 1) * 16], self.wsm), g, M=16)
                P.add("dve", lambda e, pz=pz, lr=lr: e.tensor_copy(out=lr[:], in_=pz[0:16, :]), rd=[pz], wr=[lr])
                P.add("pe", lambda e, pz=pz, lr=lr, d=d: e.matmul(pz[:], lhsT=self.w2g[:, d * 512 + h * 128:d * 512 + (h + 1) * 128], rhs=lr[:], start=True, stop=True), rd=[self.w2g, lr], wr=[pz])

        def stageAch(g):
            pq, pk = (self.PS[0], self.PS[1]) if gpar[g] == 0 else (self.PS[4], self.PS[5])
            for d in range(2):
                pz = self.PS[2 + d]
                Lin, Lc = self.g_Lin[d], self.g_Lc[d]
                col = d * 4 + h
                P.add("act", lambda e, pz=pz, Lin=Lin, col=col: e.activation(out=Lin[:], in_=pz[:], func=AF.Exp, scale=-1.0, bias=self.nb2[:, col:col + 1]), rd=[pz, self.nb2], wr=[Lin])
                P.add("act", lambda e, Lin=Lin: e.activation(out=Lin[:], in_=Lin[:], func=AF.Ln, bias=1.0), rd=[Lin], wr=[Lin])
                if d == 0:
                    P.add("dve", lambda e, Lin=Lin, Lc=Lc: e.tensor_tensor_scan(out=Lc[:], data0=self.rm[:], data1=Lin[:], initial=0.0, op0=ALU.mult, op1=ALU.add), rd=[self.rm, Lin], wr=[Lc])
                    last = Lc[:, 127::128]
                else:
                    P.add("dve", lambda e, Lin=Lin, Lc=Lc: e.tensor_tensor_scan(out=Lc[:][:, ::-1], data0=self.rm[:], data1=Lin[:][:, ::-1], initial=0.0, op0=ALU.mult, op1=ALU.add), rd=[self.rm, Lin], wr=[Lc])
                    last = Lc[:, 0::128]
                dsl = slice(d * 16 + g * 4, d * 16 + g * 4 + 4)
                P.add("act", lambda e, last=last, dsl=dsl: e.activation(out=dec[:, dsl], in_=last, func=AF.Exp, scale=-1.0 / TAU), rd=[Lc], wr=[dec])
                E1, E2, E3 = self.g_E[d]
                P.add("act", lambda e, E1=E1, Lc=Lc: e.activation(out=E1[:], in_=Lc[:], func=AF.Exp, scale=-1.0 / TAU, bias=lnsc), rd=[Lc], wr=[E1])
                P.add("act", lambda e, E2=E2, Lc=Lc: e.activation(out=E2[:], in_=Lc[:], func=AF.Exp, scale=1.0 / TAU), rd=[Lc], wr=[E2])
                qin, kin = self.g_qin[d][g], self.g_kin[d][g]
                ksT = self.g_ksT[d]
                P.add("dve", lambda e, qin=qin, E1=E1: e.tensor_tensor(out=qin[:], in0=pq[:], in1=E1[:], op=ALU.mult), rd=[pq, E1], wr=[qin])
                P.add("dve", lambda e, kin=kin, E2=E2: e.tensor_tensor(out=kin[:], in0=pk[:], in1=E2[:], op=ALU.mult), rd=[pk, E2], wr=[kin])
                P.add("dve", lambda e, E2=E2, E3=E3, dsl=dsl: e.tensor_tensor(out=E3[:].rearrange("p (c t) -> p c t", c=4), in0=E2[:].rearrange("p (c t) -> p c t", c=4), in1=dec[:, dsl].unsqueeze(2).to_broadcast([128, 4, 128]), op=ALU.mult), rd=[E2, dec], wr=[E3])
                P.add("dve", lambda e, ksT=ksT, E3=E3: e.tensor_tensor(out=ksT[:], in0=pk[:], in1=E3[:], op=ALU.mult), rd=[pk, E3], wr=[ksT])

        def stageT(g):
            for d in range(2):
                pt = self.PT[d]
                ksT = self.g_ksT[d]
                for c in range(4):
                    P.add("pe", lambda e, pt=pt, ksT=ksT, c=c: e.transpose(out=pt[:, c * 128:(c + 1) * 128], in_=ksT[:, c * 128:(c + 1) * 128], identity=self.ident[:]), rd=[ksT, self.ident], wr=[pt])
                ks = self.g_ks[d][g]
                P.add("dve", lambda e, ks=ks, pt=pt: e.tensor_copy(out=ks[:], in_=pt[:, 0:512].rearrange("p (t d) -> p t d", t=4)), rd=[pt], wr=[ks])

        def stageV(g):
            for pr in range(2):
                pv = self.PS[2 + pr]
                for k2 in range(2):
                    i = g * 4 + pr * 2 + k2
                    self.proj_tok(pv[:, k2 * 256:(k2 + 1) * 256], pv, self.m_W2, i, 0, 256)
                vt = self.g_v[g]
                P.add("act", lambda e, vt=vt, pv=pv, pr=pr: e.copy(out=vt[:, pr * 2:pr * 2 + 2, :], in_=pv[:].rearrange("p (t d) -> p t d", t=2)), rd=[pv], wr=[vt])

        stageApe(order[0])
        for n_, g in enumerate(order):
            stageAch(g)
            stageV(g)
            if n_ + 1 < 4:
                stageApe(order[n_ + 1])
            stageT(g)
        self.load_late_weights()
        uss, urs = self.ss[self.ssk], self.rs[self.ssk]
        self.ssk ^= 1
        for d in range(2):
            P.add("dve", lambda e, d=d: e.memset(self.g_S32[d][:], 0.0), wr=[self.g_S32[d]])
            P.add("dve", lambda e, d=d: e.memset(self.g_Sb[d][:], 0.0), wr=[self.g_Sb[d]])
        written = [False] * NT
        for step in range(NT):
            ctx = []
            for d in range(2):
                c = step if d == 0 else NT - 1 - step
                g, cc = c // 4, c % 4
                ctx.append(dict(d=d, c=c, cc=cc, pa=self.PS[3 * d], po=self.PS[3 * d + 1], pu=self.PS[3 * d + 2],
                                qin=self.g_qin[d][g], kin=self.g_kin[d][g], ks=self.g_ks[d][g], vt=self.g_v[g],
                                att=self.g_att[d], S32=self.g_S32[d], Sb=self.g_Sb[d],
                                mask=self.maskF if d == 0 else self.maskB, sl=slice(cc * 128, (cc + 1) * 128)))
            for k in ctx:
                pa, kin, qin, sl, pu, ks, vt, cc = k["pa"], k["kin"], k["qin"], k["sl"], k["pu"], k["ks"], k["vt"], k["cc"]
                P.add("pe", lambda e, pa=pa, kin=kin, qin=qin, sl=sl: e.matmul(pa[:, 0:128], lhsT=kin[:, sl], rhs=qin[:, sl], start=True, stop=True), rd=[kin, qin], wr=[pa])
                P.add("pe", lambda e, pu=pu, ks=ks, vt=vt, cc=cc: e.matmul(pu[:, 0:256], lhsT=ks[:, cc, :], rhs=vt[:, cc, :], start=True, stop=True), rd=[ks, vt], wr=[pu])
            for k in ctx:
                att, pa, mask = k["att"], k["pa"], k["mask"]
                P.add("dve", lambda e, att=att, pa=pa, mask=mask: e.tensor_tensor(out=att[:], in0=pa[:, 0:128], in1=mask[:], op=ALU.mult), rd=[pa, mask], wr=[att])
            for k in ctx:
                po, att, vt, cc, qin, Sb, sl = k["po"], k["att"], k["vt"], k["cc"], k["qin"], k["Sb"], k["sl"]
                P.add("pe", lambda e, po=po, att=att, vt=vt, cc=cc: e.matmul(po[:, 0:256], lhsT=att[:], rhs=vt[:, cc, :], start=True, stop=False), rd=[att, vt], wr=[po])
                P.add("pe", lambda e, po=po, qin=qin, Sb=Sb, sl=sl: e.matmul(po[:, 0:256], lhsT=qin[:, sl], rhs=Sb[:], start=False, stop=True), rd=[qin, Sb], wr=[po])
            for k in ctx:
                S32, pu = k["S32"], k["pu"]
                dcol = k["d"] * 16 + k["c"]
                P.add("dve", lambda e, S32=S32, pu=pu, dcol=dcol: e.scalar_tensor_tensor(out=S32[:], in0=S32[:], scalar=dec[:, dcol:dcol + 1], in1=pu[:, 0:256], op0=ALU.mult, op1=ALU.add), rd=[S32, pu, dec], wr=[S32])
            for k in ctx:
                S32, Sb = k["S32"], k["Sb"]
                P.add("act", lambda e, S32=S32, Sb=Sb: e.copy(out=Sb[:], in_=S32[:]), rd=[S32], wr=[Sb])
            for k in ctx:
                c, po = k["c"], k["po"]
                oc = self.g_o[c]
                if not written[c]:
                    written[c] = True
                    P.add("act", lambda e, oc=oc, po=po: e.copy(out=oc[:], in_=po[:, 0:256]), rd=[po], wr=[oc])
                else:
                    P.add("dve", lambda e, oc=oc, po=po: e.tensor_tensor(out=oc[:], in0=po[:, 0:256], in1=oc[:], op=ALU.add), rd=[po, oc], wr=[oc])
                    P.add("act", lambda e, oc=oc, c=c: e.activation(out=self.m_sq[:], in_=oc[:], func=AF.Square, accum_out=uss[:, c:c + 1]), rd=[oc], wr=[self.m_sq, uss])
        self.gating(self.g_o, silu_first=True, ssrs=(uss, urs))

    def gating(self, acc, silu_first, ssrs):
        P = self.P
        ss, rs = ssrs
        self.rsqrt(ss, rs, NT, 256 * EPS)
        gn = self.m_gn
        W12 = self.m_W12

        def head(i):
            b = i % 2
            pr = self.PS[3 * b]
            th, t12, mb = self.m_th[b], self.m_t12[b], self.m_mb[b]
            t1, t2 = t12[:, 0:256], t12[:, 256:512]
            ht = self.m_HT[i // 4]
            t0 = (i % 4) * 128
            for kc in range(8):
                P.add("pe", lambda e, kc=kc: e.matmul(pr[:], lhsT=ht[:, kc, t0:t0 + 128], rhs=W12[:, :, kc, :], start=(kc == 0), stop=(kc == 7)), rd=[ht, self.m_W1, self.m_W2], wr=[pr])
            if silu_first:
                P.add("act", lambda e: e.activation(out=th[:, 0:256], in_=pr[:, 0:256], func=AF.Silu), rd=[pr], wr=[th])
                P.add("act", lambda e: e.activation(out=th[:, 256:512], in_=pr[:, 256:512], func=AF.Tanh, scale=0.5), rd=[pr], wr=[th])
            else:
                P.add("act", lambda e: e.activation(out=th[:], in_=pr[:], func=AF.Tanh, scale=0.5), rd=[pr], wr=[th])
            P.add("dve", lambda e: e.scalar_tensor_tensor(out=t1, in0=acc[i][:], scalar=rs[:, i:i + 1], in1=gn[:], op0=ALU.mult, op1=ALU.mult), rd=[acc[i], rs, gn], wr=[t12])
            if silu_first:
                P.add("dve", lambda e: e.scalar_tensor_tensor(out=t2, in0=th[:, 256:512], scalar=1.0, in1=th[:, 0:256], op0=ALU.add, op1=ALU.mult), rd=[th, t12], wr=[t12])
                P.add("dve", lambda e: e.tensor_tensor(out=mb[:], in0=t1, in1=t2, op=ALU.mult), rd=[t12], wr=[mb])
            else:
                P.add("dve", lambda e: e.scalar_tensor_tensor(out=t2, in0=th[:, 0:256], scalar=1.0, in1=t1, op0=ALU.add, op1=ALU.mult), rd=[th, t12], wr=[t12])
                P.add("dve", lambda e: e.scalar_tensor_tensor(out=mb[:], in0=th[:, 256:512], scalar=1.0, in1=t2, op0=ALU.add, op1=ALU.mult), rd=[th, t12], wr=[mb])

        def tail(i):
            b = i % 2
            py = [self.PS[3 * b + 1], self.PS[3 * b + 2]]
            pt = self.PT[b]
            mb, mT = self.m_mb[b], self.m_mT[b]
            for k2 in range(2):
                P.add("pe", lambda e, k2=k2: e.transpose(out=pt[:, k2 * 128:(k2 + 1) * 128], in_=mb[:, k2 * 128:(k2 + 1) * 128], identity=self.ident[:]), rd=[mb, self.ident], wr=[pt])
            P.add("act", lambda e: e.copy(out=mT[:], in_=pt[:, 0:256]), rd=[pt], wr=[mT])
            for ch in range(2):
                pyc = py[ch]
                for k2 in range(2):
                    P.add("pe", lambda e, pyc=pyc, k2=k2, ch=ch: e.matmul(pyc[:], lhsT=mT[:, k2 * 128:(k2 + 1) * 128], rhs=self.m_wo[:, k2, ch * 512:(ch + 1) * 512], start=(k2 == 0), stop=(k2 == 1)), rd=[mT, self.m_wo], wr=[pyc])
                P.add("dve", lambda e, pyc=pyc, ch=ch: e.tensor_tensor(out=self.X[i][:, ch * 512:(ch + 1) * 512], in0=pyc[:], in1=self.X[i][:, ch * 512:(ch + 1) * 512], op=ALU.add), rd=[pyc, self.X[i]], wr=[self.X[i]])

        for i in range(NT + 1):
            if i < NT:
                head(i)
            if i >= 1:
                tail(i - 1)

    def ml_gates_proj(self, groups):
        P = self.P
        for d in range(2):
            A, B, G, sm = self.r_rows[d]
            for g in groups:
                pi, pf = self.PS[2 * d], self.PS[2 * d + 1]
                self.proj_feat(pi, lambda kc, d=d: (self.wsm[:, kc, 32 + d * 8:32 + d * 8 + 4], self.wsm), g, M=4)
                self.proj_feat(pf, lambda kc, d=d: (self.wsm[:, kc, 32 + d * 8 + 4:32 + d * 8 + 8], self.wsm), g, M=4)
                sl = slice(g * 512, (g + 1) * 512)
                P.add("act", lambda e, pi=pi, sl=sl, d=d: e.activation(out=A[:, sl], in_=pi[0:4, :], func=AF.Identity, bias=self.pcol[0:4, 64 + 2 * d:65 + 2 * d]), rd=[pi, self.pcol], wr=[A])
                P.add("act", lambda e, pf=pf, sl=sl, d=d: e.activation(out=B[:, sl], in_=pf[0:4, :], func=AF.Exp, scale=-1.0, bias=self.nbif[:, 2 * d + 1:2 * d + 2]), rd=[pf, self.nbif], wr=[B])

    def ml_gates(self):
        P = self.P
        lnf = float(math.log(math.sqrt(128.0)))
        for d in range(2):
            A, B, G, sm = self.r_rows[d]
            P.add("act", lambda e: e.activation(out=B[:], in_=B[:], func=AF.Ln, bias=1.0), rd=[B], wr=[B])
            rv = (lambda a: a) if d == 0 else (lambda a: a[:, ::-1])
            P.add("dve", lambda e, rv=rv: e.tensor_tensor_scan(out=rv(B[:]), data0=self.ones1[0:4, 0:1].to_broadcast([4, L]), data1=rv(B[:]), initial=0.0, op0=ALU.mult, op1=ALU.add), rd=[B, self.ones1], wr=[B])
            P.add("dve", lambda e: e.tensor_tensor(out=A[:], in0=A[:], in1=B[:], op=ALU.add), rd=[A, B], wr=[A])
            P.add("dve", lambda e, rv=rv: e.tensor_tensor_scan(out=rv(G[:]), data0=self.ones1[0:4, 0:1].to_broadcast([4, L]), data1=rv(A[:]), initial=-1e30, op0=ALU.mult, op1=ALU.max), rd=[A, self.ones1], wr=[G])
            gend = G[:, 127::128] if d == 0 else G[:, 0::128]
            P.add("dve", lambda e, gend=gend: e.tensor_copy(out=sm[:, 0:16], in_=gend), rd=[G], wr=[sm])
            P.add("dve", lambda e: e.memset(sm[:, 16:32], -1e30), rd=[sm], wr=[sm])
            if d == 0:
                P.add("dve", lambda e: e.tensor_copy(out=sm[:, 17:32], in_=sm[:, 0:15]), rd=[sm], wr=[sm])
            else:
                P.add("dve", lambda e: e.tensor_copy(out=sm[:, 16:31], in_=sm[:, 1:16]), rd=[sm], wr=[sm])
            P.add("dve", lambda e: e.tensor_tensor(out=sm[:, 32:48], in0=sm[:, 16:32], in1=sm[:, 0:16], op=ALU.subtract), rd=[sm], wr=[sm])
            P.add("act", lambda e: e.activation(out=sm[:, 32:48], in_=sm[:, 32:48], func=AF.Exp), rd=[sm], wr=[sm])
            gb_ = sm[:, 0:16].unsqueeze(2).to_broadcast([4, 16, 128])
            v3 = lambda t: t[:].rearrange("p (c t) -> p c t", c=16)
            P.add("dve", lambda e, gb_=gb_: e.tensor_tensor(out=v3(A), in0=v3(A), in1=gb_, op=ALU.subtract), rd=[A, sm], wr=[A])
            P.add("act", lambda e: e.activation(out=A[:], in_=A[:], func=AF.Exp), rd=[A], wr=[A])
            P.add("dve", lambda e, gb_=gb_: e.tensor_tensor(out=v3(B), in0=v3(B), in1=gb_, op=ALU.subtract), rd=[B, sm], wr=[B])
            P.add("act", lambda e: e.activation(out=B[:], in_=B[:], func=AF.Exp, bias=lnf), rd=[B], wr=[B])
        for d in range(2):
            A, B, G, sm = self.r_rows[d]
            pw, pe_ = self.PS[4], self.PS[5]
            for i in range(NT):
                P.add("pe", lambda e, i=i, pw=pw: e.matmul(pw[:, i * 4:(i + 1) * 4], lhsT=A[:, i * 128:(i + 1) * 128], rhs=self.identf[0:4, 0:4], start=True, stop=True), rd=[A, self.identf], wr=[pw])
                P.add("pe", lambda e, i=i, pe_=pe_: e.matmul(pe_[:, i * 4:(i + 1) * 4], lhsT=B[:, i * 128:(i + 1) * 128], rhs=self.identf[0:4, 0:4], start=True, stop=True), rd=[B, self.identf], wr=[pe_])
            P.add("dve", lambda e, d=d, pw=pw: e.tensor_copy(out=self.m_wst[d][:], in_=pw[:, 0:64]), rd=[pw], wr=[self.m_wst[d]])
            P.add("dve", lambda e, d=d, pe_=pe_: e.tensor_copy(out=self.m_en2[d][:], in_=pe_[:, 0:64]), rd=[pe_], wr=[self.m_en2[d]])
            pd = self.PS[d]
            for h in range(4):
                P.add("pe", lambda e, h=h, pd=pd: e.matmul(pd[:, h * 16:(h + 1) * 16], lhsT=self.sel[h][:], rhs=sm[:, 32:48], start=True, stop=True), rd=[self.sel[h], sm], wr=[pd])
            P.add("dve", lambda e, d=d, pd=pd: e.tensor_copy(out=self.m_decb[:, d * 64:(d + 1) * 64], in_=pd[:, 0:64]), rd=[pd], wr=[self.m_decb])

    def ml_unit(self, h):
        P = self.P
        self.load_unit_weights(O_MQK + h * 128, O_MQK + 512 + h * 128, O_MV + h * 256, O_MO + h * 256, O_GB + h * 256, h, 5)
        pre, cv = self.l_pre, self.l_cv
        for qk, dstT in enumerate((self.l_qTa, self.l_kTa)):
            P.add("dve", lambda e: e.memset(pre[:, 0:1], 0.0), rd=[pre], wr=[pre])
            P.add("dve", lambda e: e.memset(pre[:, L + 1:L + 2], 0.0), rd=[pre], wr=[pre])
            for g in range(4):
                pp = self.PS[g % 2]
                self.proj_feat(pp, lambda kc, qk=qk: (self.m_W1[:, kc, qk * 128:(qk + 1) * 128], self.m_W1), g)
                P.add("act", lambda e, pp=pp, g=g: e.copy(out=pre[:, 1 + g * 512:1 + (g + 1) * 512], in_=pp[:]), rd=[pp], wr=[pre])
            cb = 32 + (h * 2 + qk) * 4
            pc = self.pcol
            P.add("dve", lambda e, cb=cb: e.tensor_scalar(out=cv[:], in0=pre[:, 1:L + 1], scalar1=pc[:, cb + 1:cb + 2], scalar2=pc[:, cb + 3:cb + 4], op0=ALU.mult, op1=ALU.add), rd=[pre, pc], wr=[cv])
            P.add("dve", lambda e, cb=cb: e.scalar_tensor_tensor(out=cv[:], in0=pre[:, 0:L], scalar=pc[:, cb:cb + 1], in1=cv[:], op0=ALU.mult, op1=ALU.add), rd=[pre, pc, cv], wr=[cv])
            P.add("dve", lambda e, cb=cb: e.scalar_tensor_tensor(out=cv[:], in0=pre[:, 2:L + 2], scalar=pc[:, cb + 2:cb + 3], in1=cv[:], op0=ALU.mult, op1=ALU.add), rd=[pre, pc, cv], wr=[cv])
            P.add("act", lambda e, dstT=dstT: e.activation(out=dstT[:], in_=cv[:], func=AF.Silu), rd=[cv], wr=[dstT])
        for g in range(4):
            pt = self.PT[g % 2]
            for c in range(4):
                sl = slice((g * 4 + c) * 128, (g * 4 + c + 1) * 128)
                P.add("pe", lambda e, pt=pt, sl=sl, c=c: e.transpose(out=pt[:, c * 128:(c + 1) * 128], in_=self.l_kTa[:, sl], identity=self.ident[:]), rd=[self.l_kTa, self.ident], wr=[pt])
            P.add("act", lambda e, pt=pt, g=g: e.copy(out=self.l_ktok[:, g * 4:(g + 1) * 4, :], in_=pt[:, 0:512].rearrange("p (t d) -> p t d", t=4)), rd=[pt], wr=[self.l_ktok])
        for i in range(NT):
            pv = self.PS[2 + i % 2]
            self.proj_tok(pv[:, 0:256], pv, self.m_W2, i, 0, 256)
            for d in range(2):
                vw = self.l_vw[d]
                P.add("act", lambda e, vw=vw, pv=pv, i=i, d=d: e.activation(out=vw[:, i, 0:256], in_=pv[:, 0:256], func=AF.Copy, scale=self.m_wst[d][:, i * 4 + h:i * 4 + h + 1]), rd=[pv, self.m_wst[d]], wr=[vw])
        for d in range(2):
            vw = self.l_vw[d]
            wcol = self.m_wst[d][:].rearrange("p (t h) -> p t h", h=4)[:, :, h:h + 1]
            P.add("dve", lambda e, vw=vw, wcol=wcol: e.tensor_copy(out=vw[:, :, 256:257], in_=wcol), rd=[self.m_wst[d]], wr=[vw])
        self.load_late_weights()
        uss, urs = self.ss[self.ssk], self.rs[self.ssk]
        self.ssk ^= 1
        for d in range(2):
            P.add("dve", lambda e, d=d: e.memset(self.l_C32[d][:], 0.0), wr=[self.l_C32[d]])
        written = [False] * NT

        def mkctx(step):
            ctx = []
            for d in range(2):
                c = step if d == 0 else NT - 1 - step
                ctx.append(dict(d=d, c=c, sl=slice(c * 128, (c + 1) * 128), pa=self.PS[3 * d], pn=self.PS[3 * d + 1], pu=self.PS[3 * d + 2],
                                sT=self.l_sT[d], C32=self.l_C32[d], Cb=self.l_Cb[d], dn=self.l_dn[d], vw=self.l_vw[d],
                                mask=self.maskF if d == 0 else self.maskB, dcol=d * 64 + h * 16 + c, ecol=c * 4 + h))
            return ctx

        def front_pe(ctx):
            for k in ctx:
                pa, sl, pu, vw, c = k["pa"], k["sl"], k["pu"], k["vw"], k["c"]
                P.add("pe", lambda e, pa=pa, sl=sl: e.matmul(pa[:, 0:128], lhsT=self.l_kTa[:, sl], rhs=self.l_qTa[:, sl], start=True, stop=True), rd=[self.l_kTa, self.l_qTa], wr=[pa])
                P.add("pe", lambda e, pu=pu, vw=vw, c=c: e.matmul(pu[:, 0:257], lhsT=self.l_ktok[:, c, :], rhs=vw[:, c, 0:257], start=True, stop=True), rd=[self.l_ktok, vw], wr=[pu])

        def front_ev(ctx):
            for k in ctx:
                sT, pa, mask, Cb, C32, dcol = k["sT"], k["pa"], k["mask"], k["Cb"], k["C32"], k["dcol"]
                P.add("dve", lambda e, sT=sT, pa=pa, mask=mask: e.tensor_tensor(out=sT[:], in0=pa[:, 0:128], in1=mask[:], op=ALU.mult), rd=[pa, mask], wr=[sT])
                P.add("act", lambda e, Cb=Cb, C32=C32, dcol=dcol: e.activation(out=Cb[:, 0:257], in_=C32[:, 0:257], func=AF.Copy, scale=self.m_decb[:, dcol:dcol + 1]), rd=[C32, self.m_decb], wr=[Cb])

        cur = mkctx(0)
        front_pe(cur)
        front_ev(cur)
        for step in range(NT):
            ctx = cur
            for k in ctx:
                pn, sT, vw, c, Cb, sl = k["pn"], k["sT"], k["vw"], k["c"], k["Cb"], k["sl"]
                P.add("pe", lambda e, pn=pn, sT=sT, vw=vw, c=c: e.matmul(pn[:, 0:257], lhsT=sT[:], rhs=vw[:, c, 0:257], start=True, stop=False), rd=[sT, vw], wr=[pn])
                P.add("pe", lambda e, pn=pn, Cb=Cb, sl=sl: e.matmul(pn[:, 0:257], lhsT=self.l_qTa[:, sl], rhs=Cb[:, 0:257], start=False, stop=True), rd=[self.l_qTa, Cb], wr=[pn])
            for k in ctx:
                C32, pu, dcol = k["C32"], k["pu"], k["dcol"]
                P.add("dve", lambda e, C32=C32, pu=pu, dcol=dcol: e.scalar_tensor_tensor(out=C32[:, 0:257], in0=C32[:, 0:257], scalar=self.m_decb[:, dcol:dcol + 1], in1=pu[:, 0:257], op0=ALU.mult, op1=ALU.add), rd=[C32, pu, self.m_decb], wr=[C32])
            if step + 1 < NT:
                cur = mkctx(step + 1)
                front_pe(cur)
                front_ev(cur)
            for k in ctx:
                dn, pn, d, ecol, c = k["dn"], k["pn"], k["d"], k["ecol"], k["c"]
                P.add("dve", lambda e, dn=dn, pn=pn, d=d, ecol=ecol: e.tensor_scalar(out=dn[:, 2:3], in0=pn[:, 256:257], scalar1=self.m_en2[d][:, ecol:ecol + 1], scalar2=None, op0=ALU.max), rd=[pn, self.m_en2[d]], wr=[dn])
                P.add("dve", lambda e, dn=dn, pn=pn: e.scalar_tensor_tensor(out=dn[:, 0:1], in0=pn[:, 256:257], scalar=-1.0, in1=dn[:, 2:3], op0=ALU.mult, op1=ALU.max), rd=[pn, dn], wr=[dn])
                P.add("dve", lambda e, dn=dn: e.reciprocal(out=dn[:, 1:2], in_=dn[:, 0:1]), rd=[dn], wr=[dn])
                hc = self.l_h[c]
                if not written[c]:
                    written[c] = True
                    P.add("act", lambda e, hc=hc, pn=pn, dn=dn: e.activation(out=hc[:], in_=pn[:, 0:256], func=AF.Copy, scale=dn[:, 1:2]), rd=[pn, dn], wr=[hc])
                else:
                    P.add("dve", lambda e, hc=hc, pn=pn, dn=dn: e.scalar_tensor_tensor(out=hc[:], in0=pn[:, 0:256], scalar=dn[:, 1:2], in1=hc[:], op0=ALU.mult, op1=ALU.add), rd=[pn, dn, hc], wr=[hc])
                    P.add("act", lambda e, hc=hc, c=c: e.activation(out=self.m_sq[:], in_=hc[:], func=AF.Square, accum_out=uss[:, c:c + 1]), rd=[hc], wr=[self.m_sq, uss])
        self.gating(self.l_h, silu_first=False, ssrs=(uss, urs))


_CACHE = {}


def pack_small(inp):
    f = np.float32
    pcol = np.zeros((128, 72), f)
    pcol[:, 0:8] = np.asarray(inp["g_ffn1"], f).reshape(8, 128).T
    pcol[:, 8:16] = np.asarray(inp["g_mix"], f).reshape(8, 128).T
    pcol[:, 16:24] = np.asarray(inp["g_ffn2"], f).reshape(8, 128).T
    pcol[:, 24:28] = np.asarray(inp["gla_b2_fwd"], f).reshape(4, 128).T
    pcol[:, 28:32] = np.asarray(inp["gla_b2_bwd"], f).reshape(4, 128).T
    cw = np.asarray(inp["ml_conv_w"], f).reshape(3, 2, 4, 128)
    cb = np.asarray(inp["ml_conv_b"], f).reshape(2, 4, 128)
    for h in range(4):
        for qk in range(2):
            b = 32 + (h * 2 + qk) * 4
            for j in range(3):
                pcol[:, b + j] = cw[j, qk, h]
            pcol[:, b + 3] = cb[qk, h]
    pcol[0:4, 64:68] = np.asarray(inp["ml_b_if"], f).reshape(4, 4).T
    w2g = np.concatenate([np.asarray(inp["gla_w2_fwd"], f).reshape(16, 512), np.asarray(inp["gla_w2_bwd"], f).reshape(16, 512)], axis=1)
    gvec = np.stack([np.asarray(inp[k], f).reshape(1024) for k in ("g_ffn1", "g_mix", "g_ffn2", "g_final", "gla_norm", "ml_norm")], 0)
    return pcol, np.ascontiguousarray(w2g), np.ascontiguousarray(gvec)


def run(inputs, xs_per_core, nseq, stages=("ffn1", "mix", "ffn2"), units=None, ncores=8):
    key = (nseq, tuple(stages), None if units is None else tuple(units))
    if key not in _CACHE:
        _CACHE[key] = Builder(nseq, stages, units).build()
    nc = _CACHE[key]
    pcol, w2g, gvec = pack_small(inputs)
    f = np.float32
    shared = {
        "w_ffn1_in": np.ascontiguousarray(np.asarray(inputs["w_ffn1_in"], f).reshape(D, 2 * DFF)),
        "w_ffn1_out": np.ascontiguousarray(np.asarray(inputs["w_ffn1_out"], f).reshape(DFF, D)),
        "w_ffn2_in": np.ascontiguousarray(np.asarray(inputs["w_ffn2_in"], f).reshape(D, 2 * DFF)),
        "w_ffn2_out": np.ascontiguousarray(np.asarray(inputs["w_ffn2_out"], f).reshape(DFF, D)),
        "w_in": np.ascontiguousarray(np.asarray(inputs["w_in"], f).reshape(D, DIN)),
        "w_out": np.ascontiguousarray(np.asarray(inputs["w_out"], f).reshape(D, D)),
        "pcol": pcol, "w2g": w2g, "gvec": gvec,
    }
    in_maps = [dict(shared, x=np.ascontiguousarray(xs_per_core[c])) for c in range(ncores)]
    res = run_bass_kernel_spmd(nc, in_maps, core_ids=list(range(ncores)))
    return [res.results[c]["y"] for c in range(ncores)]


def kernel(**inputs):
    xp = np.asarray(inputs["x_prompt"], np.float32)
    xs = np.asarray(inputs["x_sample"], np.float32)
    per_core = []
    for c in range(8):
        per_core.append(np.concatenate([xp[4 * c:4 * c + 4].reshape(4 * L, D), xs[2 * c:2 * c + 2].reshape(2 * L, D)], axis=0))
    ys = run(inputs, per_core, 6)
    yp = np.stack([ys[c][0:4 * L].reshape(4, L, D) for c in range(8)], 0).reshape(32, L, D)
    ysm = np.stack([ys[c][4 * L:6 * L].reshape(2, L, D) for c in range(8)], 0).reshape(16, L, D)
    return (np.ascontiguousarray(yp, dtype=np.float32), np.ascontiguousarray(ysm, dtype=np.float32))
```

```python
import math
import types
import numpy as np
from contextlib import ExitStack
import concourse.bass as bass
import concourse.mybir as mybir
from concourse.bass_utils import run_bass_kernel_spmd

F32 = mybir.dt.float32
BF16 = mybir.dt.bfloat16
I32 = mybir.dt.int32
AF = mybir.ActivationFunctionType
ALU = mybir.AluOpType

ENGS = ("pe", "act", "dve", "pool", "sp")
SEM_EPOCH = 30000
DMA_RR = 6

D = 1024
L = 2048
NT = 16
DFF = 2816
NJ = 22
DIN = 8240
EPS = 1e-6
TAU = 16.0
O_GQ, O_GK, O_GV, O_GR, O_LR, O_MQK, O_MV, O_MO, O_MIF, O_GA, O_GB = 0, 512, 1024, 2048, 3072, 3104, 4128, 5152, 6176, 6192, 7216


class TT:
    __slots__ = ("ap", "w", "r")

    def __init__(self, ap):
        self.ap = ap
        self.w = None
        self.r = []

    def __getitem__(self, k):
        return self.ap[k]


class Ins:
    __slots__ = ("eng", "idx", "fn", "deps", "dma", "dma_idx", "signal", "cnt")

    def __init__(self, eng, idx, fn, deps, dma, dma_idx):
        self.eng, self.idx, self.fn, self.deps = eng, idx, fn, deps
        self.dma, self.dma_idx = dma, dma_idx
        self.signal = False
        self.cnt = 0


def _snap(fn):
    if fn is None or fn.__closure__ is None:
        return fn
    cells = []
    for c in fn.__closure__:
        try:
            cells.append(types.CellType(c.cell_contents))
        except ValueError:
            cells.append(c)
    g = types.FunctionType(fn.__code__, fn.__globals__, fn.__name__, fn.__defaults__, tuple(cells))
    g.__kwdefaults__ = fn.__kwdefaults__
    return g


class Prog:
    def __init__(self, nc):
        self.nc = nc
        self.q = {e: [] for e in ENGS}
        self.ndma = {e: 0 for e in ENGS}

    def add(self, eng, fn, rd=(), wr=(), dma=False):
        q = self.q[eng]
        idx = len(q)
        deps = set()
        for t in rd:
            if t.w is not None:
                deps.add(t.w)
        for t in wr:
            if t.w is not None:
                deps.add(t.w)
            deps.update(t.r)
        dma_idx = -1
        if dma:
            dma_idx = self.ndma[eng]
            self.ndma[eng] += 1
        ins = Ins(eng, idx, _snap(fn), deps, dma, dma_idx)
        q.append(ins)
        me = (eng, idx)
        for t in rd:
            t.r.append(me)
        for t in wr:
            t.w = me
            t.r = []
        return ins

    def barrier(self):
        last = []
        for e in ENGS:
            if self.q[e]:
                last.append((e, len(self.q[e]) - 1))
        dmas = []
        for e in ENGS:
            if self.ndma[e]:
                cnt = 0
                for ins in reversed(self.q[e]):
                    if ins.dma:
                        dmas.append((e, ins.idx))
                        cnt += 1
                        if cnt >= DMA_RR:
                            break
        for e in ENGS:
            ins = self.add(e, None)
            ins.deps.update(last)
            ins.deps.update(dmas)
            ins.deps.discard((e, ins.idx))

    def emit(self):
        with ExitStack() as st:
            self._emit(st)

    def _emit(self, st):
        nc = self.nc
        q = self.q
        for e in ENGS:
            for ins in q[e]:
                for (e2, i2) in ins.deps:
                    d = q[e2][i2]
                    if d.dma:
                        continue
                    if e2 == e and e == "pe":
                        continue
                    d.signal = True
        nsig = {}
        for e in ENGS:
            c = 0
            for ins in q[e]:
                if ins.signal and not ins.dma:
                    c += 1
                    ins.cnt = c
            nsig[e] = c
        sems = {}
        for e in ENGS:
            nep = nsig[e] // SEM_EPOCH + 1
            sems[e] = [st.enter_context(nc.semaphore(f"s_{e}_{k}")) for k in range(nep)]
        dsems = {}
        for e in ENGS:
            dsems[e] = [st.enter_context(nc.semaphore(f"d_{e}_{k}")) for k in range(DMA_RR)] if self.ndma[e] else []

        def target(d):
            if d.dma:
                return dsems[d.eng][d.dma_idx % DMA_RR], 16 * (d.dma_idx // DMA_RR + 1)
            c = d.cnt
            ep = (c - 1) // SEM_EPOCH
            return sems[d.eng][ep], c - ep * SEM_EPOCH

        stats = {e: [0, 0] for e in ENGS}

        def body(e):
            def run(eng):
                waited = {}
                for ins in q[e]:
                    need = {}
                    for (e2, i2) in ins.deps:
                        d = q[e2][i2]
                        if e2 == e and e == "pe" and not d.dma:
                            continue
                        s, v = target(d)
                        key = id(s)
                        if need.get(key, (None, 0))[1] < v:
                            need[key] = (s, v)
                    if ins.dma and ins.dma_idx >= DMA_RR:
                        s = dsems[e][ins.dma_idx % DMA_RR]
                        v = 16 * (ins.dma_idx // DMA_RR)
                        key = id(s)
                        if need.get(key, (None, 0))[1] < v:
                            need[key] = (s, v)
                    for key, (s, v) in need.items():
                        if waited.get(key, 0) >= v:
                            continue
                        eng.wait_ge(s, v)
                        waited[key] = v
                        stats[e][1] += 1
                    if ins.fn is None:
                        if ins.signal:
                            r = eng.nop()
                        else:
                            continue
                    else:
                        r = ins.fn(eng)
                    stats[e][0] += 1
                    if ins.dma:
                        s, _ = target(ins)
                        r.then_inc(s, 16)
                    elif ins.signal:
                        s, _ = target(ins)
                        r.then_inc(s, 1)
            return run

        block = st.enter_context(nc.Block())
        reg = {"pe": block.tensor, "act": block.scalar, "dve": block.vector,
               "pool": block.gpsimd, "sp": block.sync}
        for e in ENGS:
            if q[e]:
                reg[e](body(e))
        self.stats = stats


class Builder:
    def __init__(self, nseq, stages=("ffn1", "mix", "ffn2"), units=None):
        self.nseq = nseq
        self.stages = stages
        self.units = units if units is not None else [("gla", h) for h in range(4)] + [("ml", h) for h in range(4)]
        self.nc = bass.Bass("TRN2", target_bir_lowering=False)
        self.st = ExitStack()

    def dram(self, name, shape, kind="ExternalInput"):
        return self.nc.dram_tensor(name, shape, F32, kind=kind).ap()

    def sb(self, name, shape, dt):
        return self.st.enter_context(self.nc.sbuf_tensor(name, shape, dt))

    def carve(self, nelem, dt):
        n16 = nelem * (2 if dt == F32 else 1)
        n16 = (n16 + 15) // 16 * 16
        a = self.scr[:, self.sp:self.sp + n16]
        self.sp += n16
        assert self.sp <= self.SCR, (self.sp, self.SCR)
        if dt == F32:
            a = a.bitcast(F32)
        return a

    def build(self):
        nc = self.nc
        P = self.P = Prog(nc)
        ns = self.nseq
        self.x_d = self.dram("x", [ns * L, D])
        self.y_d = self.dram("y", [ns * L, D], kind="ExternalOutput")
        self.w1i = self.dram("w_ffn1_in", [D, 2 * DFF])
        self.w1o = self.dram("w_ffn1_out", [DFF, D])
        self.w2i = self.dram("w_ffn2_in", [D, 2 * DFF])
        self.w2o = self.dram("w_ffn2_out", [DFF, D])
        self.win = self.dram("w_in", [D, DIN])
        self.wout = self.dram("w_out", [D, D])
        self.pcol_d = self.dram("pcol", [128, 72])
        self.w2g_d = self.dram("w2g", [16, 1024])
        self.gvec_d = self.dram("gvec", [6, 1024])

        with self.st:
            self.alloc()
            self.setup_consts()
            for s in range(ns):
                self.sequence(s)
            P.barrier()
            P.emit()
        return nc

    def alloc(self):
        sb = self.sb
        self.xbuf = sb("xbuf", [128, NT, D], F32)
        self.X = [TT(self.xbuf[:, i, :]) for i in range(NT)]
        self.identf = TT(sb("identf", [128, 128], F32))
        self.ident = TT(sb("ident", [128, 128], BF16))
        self.maskF = TT(sb("maskF", [128, 128], F32))
        self.maskB = TT(sb("maskB", [128, 128], F32))
        self.pcol = TT(sb("pcol_s", [128, 72], F32))
        self.nb2 = TT(sb("nb2", [128, 8], F32))
        self.nbif = TT(sb("nbif", [4, 4], F32))
        self.w2g = TT(sb("w2g_s", [16, 1024], BF16))
        self.wsm = TT(sb("wsm", [128, 8, 48], BF16))
        self.rm = TT(sb("rm", [128, 512], F32))
        self.ones1 = TT(sb("ones1", [128, 1], F32))
        self.sel = [TT(sb(f"sel{h}", [4, 128], F32)) for h in range(4)]
        self.ss = [TT(sb(f"ss{k}", [128, 16], F32)) for k in range(2)]
        self.rs = [TT(sb(f"rs{k}", [128, 16], F32)) for k in range(2)]
        self.rtmp = TT(sb("rtmp", [128, 48], F32))
        self.ssk = 0
        self.SCR = 68032
        self.scr = sb("scr", [128, self.SCR], BF16)
        ps = lambda n, shape, dt: TT(self.st.enter_context(self.nc.psum_tensor(n, shape, dt)))
        self.PS = [ps(f"ps{k}", [128, 512], F32) for k in range(6)]
        self.PT = [ps(f"pt{k}", [128, 1024], BF16) for k in range(2)]
        self.layout_ffn()
        self.layout_mix()

    def layout_ffn(self):
        self.sp = 0
        c = self.carve
        self.f_gb = TT(c(1024, F32))
        self.f_gfin = TT(c(1024, F32))
        hT = c(8 * 1024, BF16).rearrange("p (k t) -> p k t", k=8)
        self.f_hTbuf = hT
        self.f_HT = [TT(hT[:, :, g * 512:(g + 1) * 512]) for g in range(2)]
        self.f_hb = [TT(c(1024, BF16)), TT(c(1024, BF16))]
        self.f_junk = TT(c(1024, BF16))
        act = c(11 * 1024, BF16).rearrange("p (j t) -> p j t", j=11)
        self.f_actbuf = act
        self.f_ACT = [TT(act[:, :, g * 512:(g + 1) * 512]) for g in range(2)]
        self.f_wo = [TT(c(11 * 1024, BF16).rearrange("p (j d) -> p j d", j=11)) for _ in range(2)]
        self.f_wi = [TT(c(8 * 2 * 128, BF16).rearrange("p (k a c) -> p k a c", k=8, a=2)) for _ in range(4)]
        self.f_sa = [TT(c(512, BF16)) for _ in range(2)]
        self.f_yst = [TT(c(1024, F32)) for _ in range(2)]
        self.f_end = self.sp
        print('ffn scratch units', self.sp)

    def layout_mix(self):
        self.sp = 0
        c = self.carve
        alt = c(4096, BF16)
        self.m_gb = TT(alt[:, 0:2048].bitcast(F32))
        hT = c(8 * L, BF16).rearrange("p (k t) -> p k t", k=8)
        self.m_hTbuf = hT
        self.m_HT = [TT(hT[:, :, g * 512:(g + 1) * 512]) for g in range(4)]
        _hb = TT(alt[:, 2048:3072])
        self.m_hb = [_hb, _hb]
        self.m_junk = TT(alt[:, 3072:4096])
        w12 = c(2 * 8 * 256, BF16).rearrange("p (w k c) -> p w k c", w=2, k=8)
        w12b = alt.rearrange("p (w k c) -> p w k c", w=2, k=8)
        self.m_W12s = [w12, w12b]
        self.m_W1s = [TT(w12[:, 0]), TT(w12b[:, 0])]
        self.m_W2s = [TT(w12[:, 1]), TT(w12b[:, 1])]
        self.set_w(0)
        self.m_wo = TT(c(2 * 1024, BF16).rearrange("p (k d) -> p k d", k=2))
        self.m_gn = TT(c(256, F32))
        self.m_lrg = [TT(c(512, BF16)[0:16, :]) for _ in range(2)]
        self.m_wst = [TT(c(64, F32)) for _ in range(2)]
        self.m_en2 = [TT(c(64, F32)) for _ in range(2)]
        self.m_decb = TT(c(128, F32))
        B = [TT(c(512, F32)) for _ in range(5)]
        self.m_B = B
        self.m_th = [B[0], B[1]]
        self.m_t12 = [B[2], B[3]]
        self.m_mb = [TT(c(256, BF16)) for _ in range(2)]
        self.m_mT = [TT(c(256, BF16)) for _ in range(2)]
        self.m_sq = TT(c(256, BF16))
        base = self.sp
        self.g_qin = [[TT(a[:, g * 512:(g + 1) * 512]) for g in range(4)] for a in (c(L, BF16), c(L, BF16))]
        self.g_kin = [[TT(a[:, g * 512:(g + 1) * 512]) for g in range(4)] for a in (c(L, BF16), c(L, BF16))]
        ksb = [c(NT * 128, BF16).rearrange("p (t d) -> p t d", t=NT) for _ in range(2)]
        self.g_ksbuf = ksb
        self.g_ks = [[TT(a[:, g * 4:(g + 1) * 4, :]) for g in range(4)] for a in ksb]
        vb = c(NT * 256, BF16).rearrange("p (t d) -> p t d", t=NT)
        self.g_vbuf = vb
        self.g_v = [TT(vb[:, g * 4:(g + 1) * 4, :]) for g in range(4)]
        ob = c(NT * 256, F32).rearrange("p (t d) -> p t d", t=NT)
        self.g_obuf = ob
        self.g_o = [TT(ob[:, i, :]) for i in range(NT)]
        self.g_Lin = [TT(c(512, F32)), self.m_B[0]]
        self.g_Lc = [TT(c(512, F32)), self.m_B[1]]
        self.g_E = [[TT(c(512, F32)) for _ in range(3)], [self.m_B[2], self.m_B[3], self.m_B[4]]]
        self.g_ksT = [TT(c(512, BF16)) for _ in range(2)]
        self.g_nl = [TT(c(4, F32)) for _ in range(2)]
        self.g_dec = TT(c(32, F32))
        self.g_att = [TT(c(128, BF16)) for _ in range(2)]
        self.g_S32 = [TT(c(256, F32)) for _ in range(2)]
        self.g_Sb = [TT(c(256, BF16)) for _ in range(2)]
        gla_end = self.sp
        self.sp = base
        self.l_pre = TT(c(L + 2, F32))
        self.l_cv = TT(c(L, F32))
        qa, ka = c(L, BF16), c(L, BF16)
        self.l_qTa = TT(qa)
        self.l_kTa = TT(ka)
        kt = c(NT * 128, BF16).rearrange("p (t d) -> p t d", t=NT)
        self.l_ktok = TT(kt)
        vw = [c(NT * 258, BF16).rearrange("p (t d) -> p t d", t=NT) for _ in range(2)]
        self.l_vwbuf = vw
        self.l_vw = [TT(a) for a in vw]
        hb_ = c(NT * 256, F32).rearrange("p (t d) -> p t d", t=NT)
        self.l_h = [TT(hb_[:, i, :]) for i in range(NT)]
        self.l_sT = [TT(c(128, BF16)) for _ in range(2)]
        self.l_C32 = [TT(c(258, F32)) for _ in range(2)]
        self.l_Cb = [TT(c(258, BF16)) for _ in range(2)]
        self.l_dn = [TT(c(4, F32)) for _ in range(2)]
        ml_end = self.sp
        self.sp = base
        self.r_rows = [(TT(c(L, F32)[0:4, :]), TT(c(L, F32)[0:4, :]), TT(c(L, F32)[0:4, :]), TT(c(64, F32)[0:4, :])) for _ in range(2)]
        self.sp = max(gla_end, ml_end, self.sp)
        self.m_end = self.sp
        print('mixer scratch units', self.sp, 'gla_end', gla_end, 'ml_end', ml_end)

    def rsqrt(self, src, dst, n, addc):
        P = self.P
        t = self.rtmp
        a, y0 = t[:, 0:n], t[:, 16:16 + n]
        w = t[:, 32:32 + n]
        P.add("dve", lambda e: e.tensor_scalar(out=a, in0=src[:, 0:n], scalar1=addc, scalar2=None, op0=ALU.add), rd=[src], wr=[t])
        P.add("dve", lambda e: e.tensor_single_scalar(out=w.bitcast(I32), in_=a.bitcast(I32), scalar=1, op=ALU.arith_shift_right), rd=[t], wr=[t])
        P.add("dve", lambda e: e.tensor_scalar(out=y0.bitcast(I32), in0=w.bitcast(I32), scalar1=-1, scalar2=1597463007, op0=ALU.mult, op1=ALU.add), rd=[t], wr=[t])
        for it in range(3):
            P.add("dve", lambda e: e.tensor_tensor(out=w, in0=a, in1=y0, op=ALU.mult), rd=[t], wr=[t])
            P.add("dve", lambda e: e.tensor_tensor(out=w, in0=w, in1=y0, op=ALU.mult), rd=[t], wr=[t])
            P.add("dve", lambda e: e.tensor_scalar(out=w, in0=w, scalar1=-0.5, scalar2=1.5, op0=ALU.mult, op1=ALU.add), rd=[t], wr=[t])
            if it < 2:
                P.add("dve", lambda e: e.tensor_tensor(out=y0, in0=y0, in1=w, op=ALU.mult), rd=[t], wr=[t])
            else:
                P.add("dve", lambda e: e.tensor_tensor(out=dst[:, 0:n], in0=y0, in1=w, op=ALU.mult), rd=[t], wr=[dst])

    def setup_consts(self):
        P = self.P
        idf, idb = self.identf, self.ident
        P.add("pool", lambda e: e.memset(idf[:], 0.0), wr=[idf])
        P.add("pool", lambda e: e.affine_select(out=idf[:], in_=idf[:], pattern=[[-1, 128]], compare_op=ALU.not_equal, fill=1.0, base=0, channel_multiplier=1), rd=[idf], wr=[idf])
        P.add("dve", lambda e: e.tensor_copy(out=idb[:], in_=idf[:]), rd=[idf], wr=[idb])
        mF, mB = self.maskF, self.maskB
        P.add("pool", lambda e: e.memset(mF[:], 1.0), wr=[mF])
        P.add("pool", lambda e: e.affine_select(out=mF[:], in_=mF[:], pattern=[[1, 128]], compare_op=ALU.is_ge, fill=0.0, base=0, channel_multiplier=-1), rd=[mF], wr=[mF])
        P.add("pool", lambda e: e.memset(mB[:], 1.0), wr=[mB])
        P.add("pool", lambda e: e.affine_select(out=mB[:], in_=mB[:], pattern=[[-1, 128]], compare_op=ALU.is_ge, fill=0.0, base=0, channel_multiplier=1), rd=[mB], wr=[mB])
        P.add("sp", lambda e: e.dma_start(out=self.pcol[:], in_=self.pcol_d[:, :]), wr=[self.pcol], dma=True)
        P.add("pool", lambda e: e.dma_start(out=self.w2g[:], in_=self.w2g_d[:, :]), wr=[self.w2g], dma=True)
        wsrc = self.win.rearrange("(k p) c -> p k c", p=128)
        P.add("pool", lambda e: e.dma_start(out=self.wsm[:, :, 0:32], in_=wsrc[:, :, O_LR:O_LR + 32]), wr=[self.wsm], dma=True)
        P.add("pool", lambda e: e.dma_start(out=self.wsm[:, :, 32:48], in_=wsrc[:, :, O_MIF:O_MIF + 16]), wr=[self.wsm], dma=True)
        P.add("dve", lambda e: e.tensor_scalar(out=self.nb2[:], in0=self.pcol[:, 24:32], scalar1=-1.0, scalar2=None, op0=ALU.mult), rd=[self.pcol], wr=[self.nb2])
        P.add("dve", lambda e: e.tensor_scalar(out=self.nbif[:], in0=self.pcol[0:4, 64:68], scalar1=-1.0, scalar2=None, op0=ALU.mult), rd=[self.pcol], wr=[self.nbif])
        rm = self.rm
        P.add("dve", lambda e: e.memset(rm[:], 1.0), wr=[rm])
        P.add("dve", lambda e: e.memset(rm[:].rearrange("p (c t) -> p c t", c=4)[:, :, 0:1], 0.0), rd=[rm], wr=[rm])
        P.add("dve", lambda e: e.memset(self.ones1[:], 1.0), wr=[self.ones1])
        for h in range(4):
            P.add("dve", lambda e, h=h: e.tensor_copy(out=self.sel[h][:], in_=idf[0:4, h:h + 1].to_broadcast([4, 128])), rd=[idf], wr=[self.sel[h]])

    def sequence(self, s):
        P = self.P
        for i in range(NT):
            r0 = s * L + i * 128
            P.add("sp", lambda e, i=i, r0=r0: e.dma_start(out=self.X[i][:], in_=self.x_d[r0:r0 + 128, :]), wr=[self.X[i]], dma=True)
        if "ffn1" in self.stages:
            self.ffn(s, self.w1i, self.w1o, 0, final=False)
        if "mix" in self.stages:
            P.barrier()
            self.mixer(s)
            P.barrier()
        self.ffn(s, self.w2i, self.w2o, 2, final=True, skip_ffn=("ffn2" not in self.stages))

    def load_gvec(self, dst, row):
        self.P.add("sp", lambda e: e.dma_start(out=dst[:], in_=self.gvec_d[row:row + 1, :].partition_broadcast(128)[:, 0, :]), wr=[dst], dma=True)

    def norm_T(self, tiles, gb, hb2, junk, dst_fn, dst_tt_fn):
        k = self.ssk
        self.ssk ^= 1
        for j in range(len(tiles)):
            self.norm_sq(tiles, j, junk, k)
        self.rsqrt(self.ss[k], self.rs[k], len(tiles), D * EPS)
        self.norm_B(tiles, gb, hb2, dst_fn, dst_tt_fn, k)

    def norm_sq(self, tiles, j, junk, k):
        ss = self.ss[k]
        i = tiles[j]
        self.P.add("act", lambda e: e.activation(out=junk[:], in_=self.X[i][:], func=AF.Square, accum_out=ss[:, j:j + 1]), rd=[self.X[i]], wr=[junk, ss])

    def norm_B(self, tiles, gb, hb2, dst_fn, dst_tt_fn, k):
        P = self.P
        rs = self.rs[k]
        for j, i in enumerate(tiles):
            hb = hb2[j % 2]
            pt = self.PT[j % 2]
            P.add("dve", lambda e, i=i, j=j, hb=hb: e.scalar_tensor_tensor(out=hb[:], in0=self.X[i][:], scalar=rs[:, j:j + 1], in1=gb[:], op0=ALU.mult, op1=ALU.mult), rd=[self.X[i], rs, gb], wr=[hb])
            for kc in range(8):
                P.add("pe", lambda e, kc=kc, hb=hb, pt=pt: e.transpose(out=pt[:, kc * 128:(kc + 1) * 128], in_=hb[:, kc * 128:(kc + 1) * 128], identity=self.ident[:]), rd=[hb, self.ident], wr=[pt])
            dst = dst_fn(j)
            P.add("act", lambda e, dst=dst, pt=pt: e.copy(out=dst, in_=pt[:].rearrange("p (k t) -> p k t", k=8)), rd=[pt], wr=[dst_tt_fn(j)])

    def ffn(self, s, wi_d, wo_d, grow, final, skip_ffn=False):
        P = self.P
        gb = self.f_gb
        if not skip_ffn:
            self.load_gvec(gb, grow)
            P.add("dve", lambda e: e.tensor_scalar(out=gb[:], in0=gb[:], scalar1=float(math.sqrt(D)), scalar2=None, op0=ALU.mult), rd=[gb], wr=[gb])
        if final:
            self.load_gvec(self.f_gfin, 3)
            P.add("dve", lambda e: e.tensor_scalar(out=self.f_gfin[:], in0=self.f_gfin[:], scalar1=float(math.sqrt(D)), scalar2=None, op0=ALU.mult), rd=[self.f_gfin], wr=[self.f_gfin])
        wi_src = wi_d.rearrange("(k p) (a c) -> p k a c", p=128, a=2)
        wslot = 0
        hTb = self.f_hTbuf
        dstf, dsttt = (lambda j: hTb[:, :, j * 128:(j + 1) * 128]), (lambda j: self.f_HT[j // 4])
        for p in range(2):
            tiles = list(range(p * 8, p * 8 + 8))
            ntiles = list(range(8, 16))
            if not skip_ffn:
                if p == 0:
                    self.norm_T(tiles, gb, self.f_hb, self.f_junk, dstf, dsttt)
                for hf in range(2):
                    hoist = (p == 0 and hf == 1)
                    if hoist:
                        kn = self.ssk
                        self.ssk ^= 1
                    wo = self.f_wo[hf]
                    P.add("pool", lambda e, wo=wo, hf=hf: e.dma_start(out=wo[:], in_=wo_d[hf * 1408:(hf + 1) * 1408, :].rearrange("(j p) d -> p j d", p=128)), wr=[wo], dma=True)
                    for jj in range(11):
                        j = hf * 11 + jj
                        wi = self.f_wi[wslot % 4]
                        wslot += 1
                        for a in range(2):
                            P.add("pool", lambda e, wi=wi, j=j, a=a: e.dma_start(out=wi[:, :, a, :], in_=wi_src[:, :, a, j * 128:(j + 1) * 128]), wr=[wi], dma=True)
                        for g in range(2):
                            pa, pg = self.PS[2 * g], self.PS[2 * g + 1]
                            ht = self.f_HT[g]
                            for a, pp in ((0, pa), (1, pg)):
                                for kc in range(8):
                                    P.add("pe", lambda e, a=a, pp=pp, kc=kc, wi=wi, ht=ht: e.matmul(pp[:], lhsT=wi[:, kc, a, :], rhs=ht[:, kc, :], start=(kc == 0), stop=(kc == 7)), rd=[wi, ht], wr=[pp])
                            sa = self.f_sa[g]
                            P.add("act", lambda e, sa=sa, pa=pa: e.activation(out=sa[:], in_=pa[:], func=AF.Silu), rd=[pa], wr=[sa])
                            at = self.f_ACT[g]
                            P.add("dve", lambda e, at=at, sa=sa, pg=pg, jj=jj: e.tensor_tensor(out=at[:, jj, :], in0=sa[:], in1=pg[:], op=ALU.mult), rd=[sa, pg], wr=[at])
                        if hoist and jj < 8:
                            self.norm_sq(ntiles, jj, self.f_junk, kn)
                        if hoist and jj == 8:
                            self.rsqrt(self.ss[kn], self.rs[kn], 8, D * EPS)
                    if hoist:
                        self.norm_B(ntiles, gb, self.f_hb, dstf, dsttt, kn)
                    for jt, i in enumerate(tiles):
                        at = self.f_ACT[jt // 4]
                        t0 = (jt % 4) * 128
                        for ch in range(2):
                            py = self.PS[4 + (jt * 2 + ch) % 2]
                            for jj in range(11):
                                P.add("pe", lambda e, py=py, at=at, t0=t0, jj=jj, ch=ch, wo=wo: e.matmul(py[:], lhsT=at[:, jj, t0:t0 + 128], rhs=wo[:, jj, ch * 512:(ch + 1) * 512], start=(jj == 0), stop=(jj == 10)), rd=[at, wo], wr=[py])
                            P.add("dve", lambda e, py=py, i=i, ch=ch: e.scalar_tensor_tensor(out=self.X[i][:, ch * 512:(ch + 1) * 512], in0=py[:], scalar=0.5, in1=self.X[i][:, ch * 512:(ch + 1) * 512], op0=ALU.mult, op1=ALU.add), rd=[py, self.X[i]], wr=[self.X[i]])
            if final:
                k = self.ssk
                self.ssk ^= 1
                ss, rs = self.ss[k], self.rs[k]
                for j, i in enumerate(tiles):
                    P.add("act", lambda e, i=i, j=j: e.activation(out=self.f_junk[:], in_=self.X[i][:], func=AF.Square, accum_out=ss[:, j:j + 1]), rd=[self.X[i]], wr=[self.f_junk, ss])
                self.rsqrt(ss, rs, 8, D * EPS)
                for j, i in enumerate(tiles):
                    yst = self.f_yst[j % 2]
                    r0 = s * L + i * 128
                    P.add("dve", lambda e, i=i, j=j, yst=yst: e.scalar_tensor_tensor(out=yst[:], in0=self.X[i][:], scalar=rs[:, j:j + 1], in1=self.f_gfin[:], op0=ALU.mult, op1=ALU.mult), rd=[self.X[i], rs, self.f_gfin], wr=[yst])
                    P.add("sp", lambda e, yst=yst, r0=r0: e.dma_start(out=self.y_d[r0:r0 + 128, :], in_=yst[:]), rd=[yst], dma=True)

    def mixer(self, s):
        P = self.P
        gb = self.m_gb
        self.load_gvec(gb, 1)
        P.add("dve", lambda e: e.tensor_scalar(out=gb[:], in0=gb[:], scalar1=float(math.sqrt(D)), scalar2=None, op0=ALU.mult), rd=[gb], wr=[gb])
        hTb = self.m_hTbuf
        need_ml = any(u[0] == "ml" for u in self.units)
        for half in range(2):
            tiles = list(range(half * 8, half * 8 + 8))
            self.norm_T(tiles, gb, self.m_hb, self.m_junk,
                        lambda j, half=half: hTb[:, :, (half * 8 + j) * 128:(half * 8 + j + 1) * 128],
                        lambda j, half=half: self.m_HT[(half * 8 + j) // 4])
            if need_ml:
                self.ml_gates_proj([2 * half, 2 * half + 1])
        if need_ml:
            self.ml_gates()
        P.barrier()
        prev = None
        self._preloaded = False
        for idx, (kind, h) in enumerate(self.units):
            if prev is not None and prev != kind:
                P.barrier()
            prev = kind
            self._unit, self._par = (kind, h), idx % 2
            self._next = self.units[idx + 1] if idx + 1 < len(self.units) else None
            self.set_w(idx % 2)
            if kind == "gla":
                self.gla_unit(h)
            else:
                self.ml_unit(h)
            self._preloaded = self._next is not None

    def proj_feat(self, pp, lhs_fn, g, M=128):
        P = self.P
        ht = self.m_HT[g]
        for kc in range(8):
            lhs, rd = lhs_fn(kc)
            P.add("pe", lambda e, kc=kc, lhs=lhs: e.matmul(pp[0:M, :], lhsT=lhs, rhs=ht[:, kc, :], start=(kc == 0), stop=(kc == 7)), rd=[rd, ht], wr=[pp])

    def proj_tok(self, out_ap, pp, w, i, c0, n):
        P = self.P
        ht = self.m_HT[i // 4]
        t0 = (i % 4) * 128
        for kc in range(8):
            P.add("pe", lambda e, kc=kc: e.matmul(out_ap, lhsT=ht[:, kc, t0:t0 + 128], rhs=w[:, kc, c0:c0 + n], start=(kc == 0), stop=(kc == 7)), rd=[ht, w], wr=[pp])

    def lr_proj(self):
        P = self.P
        for d in range(2):
            for g in range(4):
                pp = self.PS[(d * 4 + g) % 2]
                self.proj_feat(pp, lambda kc, d=d: (self.wsm[:, kc, d * 16:(d + 1) * 16], self.wsm), g, M=16)
                lr = self.m_lrT[d]
                P.add("act", lambda e, pp=pp, lr=lr, g=g: e.copy(out=lr[:, g * 512:(g + 1) * 512], in_=pp[0:16, :]), rd=[pp], wr=[lr])

    def set_w(self, par):
        self.m_W12, self.m_W1, self.m_W2 = self.m_W12s[par], self.m_W1s[par], self.m_W2s[par]

    def load_qkv(self, unit, par):
        P = self.P
        kind, h = unit
        if kind == "gla":
            cq, ck, cv = O_GQ + h * 128, O_GK + h * 128, O_GV + h * 256
        else:
            cq, ck, cv = O_MQK + h * 128, O_MQK + 512 + h * 128, O_MV + h * 256
        wsrc = self.win.rearrange("(k p) c -> p k c", p=128)
        W1, W2 = self.m_W1s[par], self.m_W2s[par]
        P.add("pool", lambda e: e.dma_start(out=W1[:, :, 0:128], in_=wsrc[:, :, cq:cq + 128]), wr=[W1], dma=True)
        P.add("pool", lambda e: e.dma_start(out=W1[:, :, 128:256], in_=wsrc[:, :, ck:ck + 128]), wr=[W1], dma=True)
        P.add("pool", lambda e: e.dma_start(out=W2[:], in_=wsrc[:, :, cv:cv + 256]), wr=[W2], dma=True)

    def load_unit_weights(self, cq, ck, cv, cr, cg, h, gnrow):
        P = self.P
        if not self._preloaded:
            self.load_qkv(self._unit, self._par)
        self._late = (cr, cg)
        P.add("pool", lambda e: e.dma_start(out=self.m_wo[:], in_=self.wout[h * 256:(h + 1) * 256, :].rearrange("(k p) d -> p k d", p=128)), wr=[self.m_wo], dma=True)
        gn = self.m_gn
        P.add("sp", lambda e: e.dma_start(out=gn[:], in_=self.gvec_d[gnrow:gnrow + 1, h * 256:(h + 1) * 256].partition_broadcast(128)[:, 0, :]), wr=[gn], dma=True)
        gsc = 8.0 if gnrow == 4 else 4.0
        P.add("dve", lambda e: e.tensor_scalar(out=gn[:], in0=gn[:], scalar1=gsc, scalar2=None, op0=ALU.mult), rd=[gn], wr=[gn])

    def load_late_weights(self):
        P = self.P
        wsrc = self.win.rearrange("(k p) c -> p k c", p=128)
        cr, cg = self._late
        W1, W2 = self.m_W1, self.m_W2
        P.add("pool", lambda e: e.dma_start(out=W1[:], in_=wsrc[:, :, cr:cr + 256]), wr=[W1], dma=True)
        P.add("pool", lambda e: e.dma_start(out=W2[:], in_=wsrc[:, :, cg:cg + 256]), wr=[W2], dma=True)
        if self._next is not None:
            self.load_qkv(self._next, 1 - self._par)

    def gla_unit(self, h):
        P = self.P
        self.load_unit_weights(O_GQ + h * 128, O_GK + h * 128, O_GV + h * 256, O_GR + h * 256, O_GA + h * 256, h, 4)
        lnsc = float(math.log(128.0 ** -0.5))
        dec = self.g_dec
        order = [0, 3, 1, 2]
        gpar = {g: n_ % 2 for n_, g in enumerate(order)}

        def stageApe(g):
            pq, pk = (self.PS[0], self.PS[1]) if gpar[g] == 0 else (self.PS[4], self.PS[5])
            self.proj_feat(pq, lambda kc: (self.m_W1[:, kc, 0:128], self.m_W1), g)
            self.proj_feat(pk, lambda kc: (self.m_W1[:, kc, 128:256], self.m_W1), g)
            for d in range(2):
                pz = self.PS[2 + d]
                lr = self.m_lrg[d]
                self.proj_feat(pz, lambda kc, d=d: (self.wsm[:, kc, d * 16:(d + 1) * 16], self.wsm), g, M=16)
                P.add("dve", lambda e, pz=pz, lr=lr: e.tensor_copy(out=lr[:], in_=pz[0:16, :]), rd=[pz], wr=[lr])
                P.add("pe", lambda e, pz=pz, lr=lr, d=d: e.matmul(pz[:], lhsT=self.w2g[:, d * 512 + h * 128:d * 512 + (h + 1) * 128], rhs=lr[:], start=True, stop=True), rd=[self.w2g, lr], wr=[pz])

        def stageAch(g):
            pq, pk = (self.PS[0], self.PS[1]) if gpar[g] == 0 else (self.PS[4], self.PS[5])
            for d in range(2):
                pz = self.PS[2 + d]
                Lin, Lc = self.g_Lin[d], self.g_Lc[d]
                col = d * 4 + h
                P.add("act", lambda e, pz=pz, Lin=Lin, col=col: e.activation(out=Lin[:], in_=pz[:], func=AF.Exp, scale=-1.0, bias=self.nb2[:, col:col + 1]), rd=[pz, self.nb2], wr=[Lin])
                P.add("act", lambda e, Lin=Lin: e.activation(out=Lin[:], in_=Lin[:], func=AF.Ln, bias=1.0), rd=[Lin], wr=[Lin])
                if d == 0:
                    P.add("dve", lambda e, Lin=Lin, Lc=Lc: e.tensor_tensor_scan(out=Lc[:], data0=self.rm[:], data1=Lin[:], initial=0.0, op0=ALU.mult, op1=ALU.add), rd=[self.rm, Lin], wr=[Lc])
                    last = Lc[:, 127::128]
                else:
                    P.add("dve", lambda e, Lin=Lin, Lc=Lc: e.tensor_tensor_scan(out=Lc[:][:, ::-1], data0=self.rm[:], data1=Lin[:][:, ::-1], initial=0.0, op0=ALU.mult, op1=ALU.add), rd=[self.rm, Lin], wr=[Lc])
                    last = Lc[:, 0::128]
                dsl = slice(d * 16 + g * 4, d * 16 + g * 4 + 4)
                P.add("act", lambda e, last=last, dsl=dsl: e.activation(out=dec[:, dsl], in_=last, func=AF.Exp, scale=-1.0 / TAU), rd=[Lc], wr=[dec])
                E1, E2, E3 = self.g_E[d]
                P.add("act", lambda e, E1=E1, Lc=Lc: e.activation(out=E1[:], in_=Lc[:], func=AF.Exp, scale=-1.0 / TAU, bias=lnsc), rd=[Lc], wr=[E1])
                P.add("act", lambda e, E2=E2, Lc=Lc: e.activation(out=E2[:], in_=Lc[:], func=AF.Exp, scale=1.0 / TAU), rd=[Lc], wr=[E2])
                qin, kin = self.g_qin[d][g], self.g_kin[d][g]
                ksT = self.g_ksT[d]
                P.add("dve", lambda e, qin=qin, E1=E1: e.tensor_tensor(out=qin[:], in0=pq[:], in1=E1[:], op=ALU.mult), rd=[pq, E1], wr=[qin])
                P.add("dve", lambda e, kin=kin, E2=E2: e.tensor_tensor(out=kin[:], in0=pk[:], in1=E2[:], op=ALU.mult), rd=[pk, E2], wr=[kin])
                P.add("dve", lambda e, E2=E2, E3=E3, dsl=dsl: e.tensor_tensor(out=E3[:].rearrange("p (c t) -> p c t", c=4), in0=E2[:].rearrange("p (c t) -> p c t", c=4), in1=dec[:, dsl].unsqueeze(2).to_broadcast([128, 4, 128]), op=ALU.mult), rd=[E2, dec], wr=[E3])
                P.add("dve", lambda e, ksT=ksT, E3=E3: e.tensor_tensor(out=ksT[:], in0=pk[:], in1=E3[:], op=ALU.mult), rd=[pk, E3], wr=[ksT])

        def stageT(g):
            for d in range(2):
                pt = self.PT[d]
                ksT = self.g_ksT[d]
                for c in range(4):
                    P.add("pe", lambda e, pt=pt, ksT=ksT, c=c: e.transpose(out=pt[:, c * 128:(c + 1) * 128], in_=ksT[:, c * 128:(c + 1) * 128], identity=self.ident[:]), rd=[ksT, self.ident], wr=[pt])
                ks = self.g_ks[d][g]
                P.add("dve", lambda e, ks=ks, pt=pt: e.tensor_copy(out=ks[:], in_=pt[:, 0:512].rearrange("p (t d) -> p t d", t=4)), rd=[pt], wr=[ks])

        def stageV(g):
            for pr in range(2):
                pv = self.PS[2 + pr]
                for k2 in range(2):
                    i = g * 4 + pr * 2 + k2
                    self.proj_tok(pv[:, k2 * 256:(k2 + 1) * 256], pv, self.m_W2, i, 0, 256)
                vt = self.g_v[g]
                P.add("act", lambda e, vt=vt, pv=pv, pr=pr: e.copy(out=vt[:, pr * 2:pr * 2 + 2, :], in_=pv[:].rearrange("p (t d) -> p t d", t=2)), rd=[pv], wr=[vt])

        stageApe(order[0])
        for n_, g in enumerate(order):
            stageAch(g)
            stageV(g)
            if n_ + 1 < 4:
                stageApe(order[n_ + 1])
            stageT(g)
        self.load_late_weights()
        uss, urs = self.ss[self.ssk], self.rs[self.ssk]
        self.ssk ^= 1
        for d in range(2):
            P.add("dve", lambda e, d=d: e.memset(self.g_S32[d][:], 0.0), wr=[self.g_S32[d]])
            P.add("dve", lambda e, d=d: e.memset(self.g_Sb[d][:], 0.0), wr=[self.g_Sb[d]])
        written = [False] * NT
        for step in range(NT):
            ctx = []
            for d in range(2):
                c = step if d == 0 else NT - 1 - step
                g, cc = c // 4, c % 4
                ctx.append(dict(d=d, c=c, cc=cc, pa=self.PS[3 * d], po=self.PS[3 * d + 1], pu=self.PS[3 * d + 2],
                                qin=self.g_qin[d][g], kin=self.g_kin[d][g], ks=self.g_ks[d][g], vt=self.g_v[g],
                                att=self.g_att[d], S32=self.g_S32[d], Sb=self.g_Sb[d],
                                mask=self.maskF if d == 0 else self.maskB, sl=slice(cc * 128, (cc + 1) * 128)))
            for k in ctx:
                pa, kin, qin, sl, pu, ks, vt, cc = k["pa"], k["kin"], k["qin"], k["sl"], k["pu"], k["ks"], k["vt"], k["cc"]
                P.add("pe", lambda e, pa=pa, kin=kin, qin=qin, sl=sl: e.matmul(pa[:, 0:128], lhsT=kin[:, sl], rhs=qin[:, sl], start=True, stop=True), rd=[kin, qin], wr=[pa])
                P.add("pe", lambda e, pu=pu, ks=ks, vt=vt, cc=cc: e.matmul(pu[:, 0:256], lhsT=ks[:, cc, :], rhs=vt[:, cc, :], start=True, stop=True), rd=[ks, vt], wr=[pu])
            for k in ctx:
                att, pa, mask = k["att"], k["pa"], k["mask"]
                P.add("dve", lambda e, att=att, pa=pa, mask=mask: e.tensor_tensor(out=att[:], in0=pa[:, 0:128], in1=mask[:], op=ALU.mult), rd=[pa, mask], wr=[att])
            for k in ctx:
                po, att, vt, cc, qin, Sb, sl = k["po"], k["att"], k["vt"], k["cc"], k["qin"], k["Sb"], k["sl"]
                P.add("pe", lambda e, po=po, att=att, vt=vt, cc=cc: e.matmul(po[:, 0:256], lhsT=att[:], rhs=vt[:, cc, :], start=True, stop=False), rd=[att, vt], wr=[po])
                P.add("pe", lambda e, po=po, qin=qin, Sb=Sb, sl=sl: e.matmul(po[:, 0:256], lhsT=qin[:, sl], rhs=Sb[:], start=False, stop=True), rd=[qin, Sb], wr=[po])
            for k in ctx:
                S32, pu = k["S32"], k["pu"]
                dcol = k["d"] * 16 + k["c"]
                P.add("dve", lambda e, S32=S32, pu=pu, dcol=dcol: e.scalar_tensor_tensor(out=S32[:], in0=S32[:], scalar=dec[:, dcol:dcol + 1], in1=pu[:, 0:256], op0=ALU.mult, op1=ALU.add), rd=[S32, pu, dec], wr=[S32])
            for k in ctx:
                S32, Sb = k["S32"], k["Sb"]
                P.add("act", lambda e, S32=S32, Sb=Sb: e.copy(out=Sb[:], in_=S32[:]), rd=[S32], wr=[Sb])
            for k in ctx:
                c, po = k["c"], k["po"]
                oc = self.g_o[c]
                if not written[c]:
                    written[c] = True
                    P.add("act", lambda e, oc=oc, po=po: e.copy(out=oc[:], in_=po[:, 0:256]), rd=[po], wr=[oc])
                else:
                    P.add("dve", lambda e, oc=oc, po=po: e.tensor_tensor(out=oc[:], in0=po[:, 0:256], in1=oc[:], op=ALU.add), rd=[po, oc], wr=[oc])
                    P.add("act", lambda e, oc=oc, c=c: e.activation(out=self.m_sq[:], in_=oc[:], func=AF.Square, accum_out=uss[:, c:c + 1]), rd=[oc], wr=[self.m_sq, uss])
        self.gating(self.g_o, silu_first=True, ssrs=(uss, urs))

    def gating(self, acc, silu_first, ssrs):
        P = self.P
        ss, rs = ssrs
        self.rsqrt(ss, rs, NT, 256 * EPS)
        gn = self.m_gn
        W12 = self.m_W12

        def head(i):
            b = i % 2
            pr = self.PS[3 * b]
            th, t12, mb = self.m_th[b], self.m_t12[b], self.m_mb[b]
            t1, t2 = t12[:, 0:256], t12[:, 256:512]
            ht = self.m_HT[i // 4]
            t0 = (i % 4) * 128
            for kc in range(8):
                P.add("pe", lambda e, kc=kc: e.matmul(pr[:], lhsT=ht[:, kc, t0:t0 + 128], rhs=W12[:, :, kc, :], start=(kc == 0), stop=(kc == 7)), rd=[ht, self.m_W1, self.m_W2], wr=[pr])
            if silu_first:
                P.add("act", lambda e: e.activation(out=th[:, 0:256], in_=pr[:, 0:256], func=AF.Silu), rd=[pr], wr=[th])
                P.add("act", lambda e: e.activation(out=th[:, 256:512], in_=pr[:, 256:512], func=AF.Tanh, scale=0.5), rd=[pr], wr=[th])
            else:
                P.add("act", lambda e: e.activation(out=th[:], in_=pr[:], func=AF.Tanh, scale=0.5), rd=[pr], wr=[th])
            P.add("dve", lambda e: e.scalar_tensor_tensor(out=t1, in0=acc[i][:], scalar=rs[:, i:i + 1], in1=gn[:], op0=ALU.mult, op1=ALU.mult), rd=[acc[i], rs, gn], wr=[t12])
            if silu_first:
                P.add("dve", lambda e: e.scalar_tensor_tensor(out=t2, in0=th[:, 256:512], scalar=1.0, in1=th[:, 0:256], op0=ALU.add, op1=ALU.mult), rd=[th, t12], wr=[t12])
                P.add("dve", lambda e: e.tensor_tensor(out=mb[:], in0=t1, in1=t2, op=ALU.mult), rd=[t12], wr=[mb])
            else:
                P.add("dve", lambda e: e.scalar_tensor_tensor(out=t2, in0=th[:, 0:256], scalar=1.0, in1=t1, op0=ALU.add, op1=ALU.mult), rd=[th, t12], wr=[t12])
                P.add("dve", lambda e: e.scalar_tensor_tensor(out=mb[:], in0=th[:, 256:512], scalar=1.0, in1=t2, op0=ALU.add, op1=ALU.mult), rd=[th, t12], wr=[mb])

        def tailA(i):
            b = i % 2
            pt = self.PT[b]
            mb, mT = self.m_mb[b], self.m_mT[b]
            for k2 in range(2):
                P.add("pe", lambda e, k2=k2: e.transpose(out=pt[:, k2 * 128:(k2 + 1) * 128], in_=mb[:, k2 * 128:(k2 + 1) * 128], identity=self.ident[:]), rd=[mb, self.ident], wr=[pt])
            P.add("act", lambda e: e.copy(out=mT[:], in_=pt[:, 0:256]), rd=[pt], wr=[mT])

        def tailB(i):
            b = i % 2
            py = [self.PS[3 * b + 1], self.PS[3 * b + 2]]
            mT = self.m_mT[b]
            for ch in range(2):
                pyc = py[ch]
                for k2 in range(2):
                    P.add("pe", lambda e, pyc=pyc, k2=k2, ch=ch: e.matmul(pyc[:], lhsT=mT[:, k2 * 128:(k2 + 1) * 128], rhs=self.m_wo[:, k2, ch * 512:(ch + 1) * 512], start=(k2 == 0), stop=(k2 == 1)), rd=[mT, self.m_wo], wr=[pyc])
                P.add("dve", lambda e, pyc=pyc, ch=ch: e.tensor_tensor(out=self.X[i][:, ch * 512:(ch + 1) * 512], in0=pyc[:], in1=self.X[i][:, ch * 512:(ch + 1) * 512], op=ALU.add), rd=[pyc, self.X[i]], wr=[self.X[i]])

        head(0)
        head(1)
        tailA(0)
        for j in range(1, NT):
            if j + 1 < NT:
                head(j + 1)
            tailB(j - 1)
            tailA(j)
        tailB(NT - 1)

    def ml_gates_proj(self, groups):
        P = self.P
        for d in range(2):
            A, B, G, sm = self.r_rows[d]
            for g in groups:
                pi, pf = self.PS[2 * d], self.PS[2 * d + 1]
                self.proj_feat(pi, lambda kc, d=d: (self.wsm[:, kc, 32 + d * 8:32 + d * 8 + 4], self.wsm), g, M=4)
                self.proj_feat(pf, lambda kc, d=d: (self.wsm[:, kc, 32 + d * 8 + 4:32 + d * 8 + 8], self.wsm), g, M=4)
                sl = slice(g * 512, (g + 1) * 512)
                P.add("act", lambda e, pi=pi, sl=sl, d=d: e.activation(out=A[:, sl], in_=pi[0:4, :], func=AF.Identity, bias=self.pcol[0:4, 64 + 2 * d:65 + 2 * d]), rd=[pi, self.pcol], wr=[A])
                P.add("act", lambda e, pf=pf, sl=sl, d=d: e.activation(out=B[:, sl], in_=pf[0:4, :], func=AF.Exp, scale=-1.0, bias=self.nbif[:, 2 * d + 1:2 * d + 2]), rd=[pf, self.nbif], wr=[B])

    def ml_gates(self):
        P = self.P
        lnf = float(math.log(math.sqrt(128.0)))
        for d in range(2):
            A, B, G, sm = self.r_rows[d]
            P.add("act", lambda e: e.activation(out=B[:], in_=B[:], func=AF.Ln, bias=1.0), rd=[B], wr=[B])
            rv = (lambda a: a) if d == 0 else (lambda a: a[:, ::-1])
            P.add("dve", lambda e, rv=rv: e.tensor_tensor_scan(out=rv(B[:]), data0=self.ones1[0:4, 0:1].to_broadcast([4, L]), data1=rv(B[:]), initial=0.0, op0=ALU.mult, op1=ALU.add), rd=[B, self.ones1], wr=[B])
            P.add("dve", lambda e: e.tensor_tensor(out=A[:], in0=A[:], in1=B[:], op=ALU.add), rd=[A, B], wr=[A])
            P.add("dve", lambda e, rv=rv: e.tensor_tensor_scan(out=rv(G[:]), data0=self.ones1[0:4, 0:1].to_broadcast([4, L]), data1=rv(A[:]), initial=-1e30, op0=ALU.mult, op1=ALU.max), rd=[A, self.ones1], wr=[G])
            gend = G[:, 127::128] if d == 0 else G[:, 0::128]
            P.add("dve", lambda e, gend=gend: e.tensor_copy(out=sm[:, 0:16], in_=gend), rd=[G], wr=[sm])
            P.add("dve", lambda e: e.memset(sm[:, 16:32], -1e30), rd=[sm], wr=[sm])
            if d == 0:
                P.add("dve", lambda e: e.tensor_copy(out=sm[:, 17:32], in_=sm[:, 0:15]), rd=[sm], wr=[sm])
            else:
                P.add("dve", lambda e: e.tensor_copy(out=sm[:, 16:31], in_=sm[:, 1:16]), rd=[sm], wr=[sm])
            P.add("dve", lambda e: e.tensor_tensor(out=sm[:, 32:48], in0=sm[:, 16:32], in1=sm[:, 0:16], op=ALU.subtract), rd=[sm], wr=[sm])
            P.add("act", lambda e: e.activation(out=sm[:, 32:48], in_=sm[:, 32:48], func=AF.Exp), rd=[sm], wr=[sm])
            gb_ = sm[:, 0:16].unsqueeze(2).to_broadcast([4, 16, 128])
            v3 = lambda t: t[:].rearrange("p (c t) -> p c t", c=16)
            P.add("dve", lambda e, gb_=gb_: e.tensor_tensor(out=v3(A), in0=v3(A), in1=gb_, op=ALU.subtract), rd=[A, sm], wr=[A])
            P.add("act", lambda e: e.activation(out=A[:], in_=A[:], func=AF.Exp), rd=[A], wr=[A])
            P.add("dve", lambda e, gb_=gb_: e.tensor_tensor(out=v3(B), in0=v3(B), in1=gb_, op=ALU.subtract), rd=[B, sm], wr=[B])
            P.add("act", lambda e: e.activation(out=B[:], in_=B[:], func=AF.Exp, bias=lnf), rd=[B], wr=[B])
        for d in range(2):
            A, B, G, sm = self.r_rows[d]
            pw, pe_ = self.PS[4], self.PS[5]
            for i in range(NT):
                P.add("pe", lambda e, i=i, pw=pw: e.matmul(pw[:, i * 4:(i + 1) * 4], lhsT=A[:, i * 128:(i + 1) * 128], rhs=self.identf[0:4, 0:4], start=True, stop=True), rd=[A, self.identf], wr=[pw])
                P.add("pe", lambda e, i=i, pe_=pe_: e.matmul(pe_[:, i * 4:(i + 1) * 4], lhsT=B[:, i * 128:(i + 1) * 128], rhs=self.identf[0:4, 0:4], start=True, stop=True), rd=[B, self.identf], wr=[pe_])
            P.add("dve", lambda e, d=d, pw=pw: e.tensor_copy(out=self.m_wst[d][:], in_=pw[:, 0:64]), rd=[pw], wr=[self.m_wst[d]])
            P.add("dve", lambda e, d=d, pe_=pe_: e.tensor_copy(out=self.m_en2[d][:], in_=pe_[:, 0:64]), rd=[pe_], wr=[self.m_en2[d]])
            pd = self.PS[d]
            for h in range(4):
                P.add("pe", lambda e, h=h, pd=pd: e.matmul(pd[:, h * 16:(h + 1) * 16], lhsT=self.sel[h][:], rhs=sm[:, 32:48], start=True, stop=True), rd=[self.sel[h], sm], wr=[pd])
            P.add("dve", lambda e, d=d, pd=pd: e.tensor_copy(out=self.m_decb[:, d * 64:(d + 1) * 64], in_=pd[:, 0:64]), rd=[pd], wr=[self.m_decb])

    def ml_unit(self, h):
        P = self.P
        self.load_unit_weights(O_MQK + h * 128, O_MQK + 512 + h * 128, O_MV + h * 256, O_MO + h * 256, O_GB + h * 256, h, 5)
        pre, cv = self.l_pre, self.l_cv
        for qk, dstT in enumerate((self.l_qTa, self.l_kTa)):
            P.add("dve", lambda e: e.memset(pre[:, 0:1], 0.0), rd=[pre], wr=[pre])
            P.add("dve", lambda e: e.memset(pre[:, L + 1:L + 2], 0.0), rd=[pre], wr=[pre])
            for g in range(4):
                pp = self.PS[g % 2]
                self.proj_feat(pp, lambda kc, qk=qk: (self.m_W1[:, kc, qk * 128:(qk + 1) * 128], self.m_W1), g)
                P.add("act", lambda e, pp=pp, g=g: e.copy(out=pre[:, 1 + g * 512:1 + (g + 1) * 512], in_=pp[:]), rd=[pp], wr=[pre])
            cb = 32 + (h * 2 + qk) * 4
            pc = self.pcol
            P.add("dve", lambda e, cb=cb: e.tensor_scalar(out=cv[:], in0=pre[:, 1:L + 1], scalar1=pc[:, cb + 1:cb + 2], scalar2=pc[:, cb + 3:cb + 4], op0=ALU.mult, op1=ALU.add), rd=[pre, pc], wr=[cv])
            P.add("dve", lambda e, cb=cb: e.scalar_tensor_tensor(out=cv[:], in0=pre[:, 0:L], scalar=pc[:, cb:cb + 1], in1=cv[:], op0=ALU.mult, op1=ALU.add), rd=[pre, pc, cv], wr=[cv])
            P.add("dve", lambda e, cb=cb: e.scalar_tensor_tensor(out=cv[:], in0=pre[:, 2:L + 2], scalar=pc[:, cb + 2:cb + 3], in1=cv[:], op0=ALU.mult, op1=ALU.add), rd=[pre, pc, cv], wr=[cv])
            P.add("act", lambda e, dstT=dstT: e.activation(out=dstT[:], in_=cv[:], func=AF.Silu), rd=[cv], wr=[dstT])
        for g in range(4):
            pt = self.PT[g % 2]
            for c in range(4):
                sl = slice((g * 4 + c) * 128, (g * 4 + c + 1) * 128)
                P.add("pe", lambda e, pt=pt, sl=sl, c=c: e.transpose(out=pt[:, c * 128:(c + 1) * 128], in_=self.l_kTa[:, sl], identity=self.ident[:]), rd=[self.l_kTa, self.ident], wr=[pt])
            P.add("act", lambda e, pt=pt, g=g: e.copy(out=self.l_ktok[:, g * 4:(g + 1) * 4, :], in_=pt[:, 0:512].rearrange("p (t d) -> p t d", t=4)), rd=[pt], wr=[self.l_ktok])
        for i in range(NT):
            pv = self.PS[2 + i % 2]
            self.proj_tok(pv[:, 0:256], pv, self.m_W2, i, 0, 256)
            for d in range(2):
                vw = self.l_vw[d]
                P.add("act", lambda e, vw=vw, pv=pv, i=i, d=d: e.activation(out=vw[:, i, 0:256], in_=pv[:, 0:256], func=AF.Copy, scale=self.m_wst[d][:, i * 4 + h:i * 4 + h + 1]), rd=[pv, self.m_wst[d]], wr=[vw])
        for d in range(2):
            vw = self.l_vw[d]
            wcol = self.m_wst[d][:].rearrange("p (t h) -> p t h", h=4)[:, :, h:h + 1]
            P.add("dve", lambda e, vw=vw, wcol=wcol: e.tensor_copy(out=vw[:, :, 256:257], in_=wcol), rd=[self.m_wst[d]], wr=[vw])
        self.load_late_weights()
        uss, urs = self.ss[self.ssk], self.rs[self.ssk]
        self.ssk ^= 1
        for d in range(2):
            P.add("dve", lambda e, d=d: e.memset(self.l_C32[d][:], 0.0), wr=[self.l_C32[d]])
        written = [False] * NT

        def mkctx(step):
            ctx = []
            for d in range(2):
                c = step if d == 0 else NT - 1 - step
                ctx.append(dict(d=d, c=c, sl=slice(c * 128, (c + 1) * 128), pa=self.PS[3 * d], pn=self.PS[3 * d + 1], pu=self.PS[3 * d + 2],
                                sT=self.l_sT[d], C32=self.l_C32[d], Cb=self.l_Cb[d], dn=self.l_dn[d], vw=self.l_vw[d],
                                mask=self.maskF if d == 0 else self.maskB, dcol=d * 64 + h * 16 + c, ecol=c * 4 + h))
            return ctx

        def front_pe(ctx):
            for k in ctx:
                pa, sl, pu, vw, c = k["pa"], k["sl"], k["pu"], k["vw"], k["c"]
                P.add("pe", lambda e, pa=pa, sl=sl: e.matmul(pa[:, 0:128], lhsT=self.l_kTa[:, sl], rhs=self.l_qTa[:, sl], start=True, stop=True), rd=[self.l_kTa, self.l_qTa], wr=[pa])
                P.add("pe", lambda e, pu=pu, vw=vw, c=c: e.matmul(pu[:, 0:257], lhsT=self.l_ktok[:, c, :], rhs=vw[:, c, 0:257], start=True, stop=True), rd=[self.l_ktok, vw], wr=[pu])

        def front_ev(ctx):
            for k in ctx:
                sT, pa, mask, Cb, C32, dcol = k["sT"], k["pa"], k["mask"], k["Cb"], k["C32"], k["dcol"]
                P.add("dve", lambda e, sT=sT, pa=pa, mask=mask: e.tensor_tensor(out=sT[:], in0=pa[:, 0:128], in1=mask[:], op=ALU.mult), rd=[pa, mask], wr=[sT])
                P.add("act", lambda e, Cb=Cb, C32=C32, dcol=dcol: e.activation(out=Cb[:, 0:257], in_=C32[:, 0:257], func=AF.Copy, scale=self.m_decb[:, dcol:dcol + 1]), rd=[C32, self.m_decb], wr=[Cb])

        cur = mkctx(0)
        front_pe(cur)
        front_ev(cur)
        for step in range(NT):
            ctx = cur
            for k in ctx:
                pn, sT, vw, c, Cb, sl = k["pn"], k["sT"], k["vw"], k["c"], k["Cb"], k["sl"]
                P.add("pe", lambda e, pn=pn, sT=sT, vw=vw, c=c: e.matmul(pn[:, 0:257], lhsT=sT[:], rhs=vw[:, c, 0:257], start=True, stop=False), rd=[sT, vw], wr=[pn])
                P.add("pe", lambda e, pn=pn, Cb=Cb, sl=sl: e.matmul(pn[:, 0:257], lhsT=self.l_qTa[:, sl], rhs=Cb[:, 0:257], start=False, stop=True), rd=[self.l_qTa, Cb], wr=[pn])
            for k in ctx:
                C32, pu, dcol = k["C32"], k["pu"], k["dcol"]
                P.add("dve", lambda e, C32=C32, pu=pu, dcol=dcol: e.scalar_tensor_tensor(out=C32[:, 0:257], in0=C32[:, 0:257], scalar=self.m_decb[:, dcol:dcol + 1], in1=pu[:, 0:257], op0=ALU.mult, op1=ALU.add), rd=[C32, pu, self.m_decb], wr=[C32])
            if step + 1 < NT:
                cur = mkctx(step + 1)
                front_pe(cur)
                front_ev(cur)
            for k in ctx:
                dn, pn, d, ecol, c = k["dn"], k["pn"], k["d"], k["ecol"], k["c"]
                P.add("dve", lambda e, dn=dn, pn=pn, d=d, ecol=ecol: e.tensor_scalar(out=dn[:, 2:3], in0=pn[:, 256:257], scalar1=self.m_en2[d][:, ecol:ecol + 1], scalar2=None, op0=ALU.max), rd=[pn, self.m_en2[d]], wr=[dn])
                P.add("dve", lambda e, dn=dn, pn=pn: e.scalar_tensor_tensor(out=dn[:, 0:1], in0=pn[:, 256:257], scalar=-1.0, in1=dn[:, 2:3], op0=ALU.mult, op1=ALU.max), rd=[pn, dn], wr=[dn])
                P.add("dve", lambda e, dn=dn: e.reciprocal(out=dn[:, 1:2], in_=dn[:, 0:1]), rd=[dn], wr=[dn])
                hc = self.l_h[c]
                if not written[c]:
                    written[c] = True
                    P.add("act", lambda e, hc=hc, pn=pn, dn=dn: e.activation(out=hc[:], in_=pn[:, 0:256], func=AF.Copy, scale=dn[:, 1:2]), rd=[pn, dn], wr=[hc])
                else:
                    P.add("dve", lambda e, hc=hc, pn=pn, dn=dn: e.scalar_tensor_tensor(out=hc[:], in0=pn[:, 0:256], scalar=dn[:, 1:2], in1=hc[:], op0=ALU.mult, op1=ALU.add), rd=[pn, dn, hc], wr=[hc])
                    P.add("act", lambda e, hc=hc, c=c: e.activation(out=self.m_sq[:], in_=hc[:], func=AF.Square, accum_out=uss[:, c:c + 1]), rd=[hc], wr=[self.m_sq, uss])
        self.gating(self.l_h, silu_first=False, ssrs=(uss, urs))


_CACHE = {}


def pack_small(inp):
    f = np.float32
    pcol = np.zeros((128, 72), f)
    pcol[:, 0:8] = np.asarray(inp["g_ffn1"], f).reshape(8, 128).T
    pcol[:, 8:16] = np.asarray(inp["g_mix"], f).reshape(8, 128).T
    pcol[:, 16:24] = np.asarray(inp["g_ffn2"], f).reshape(8, 128).T
    pcol[:, 24:28] = np.asarray(inp["gla_b2_fwd"], f).reshape(4, 128).T
    pcol[:, 28:32] = np.asarray(inp["gla_b2_bwd"], f).reshape(4, 128).T
    cw = np.asarray(inp["ml_conv_w"], f).reshape(3, 2, 4, 128)
    cb = np.asarray(inp["ml_conv_b"], f).reshape(2, 4, 128)
    for h in range(4):
        for qk in range(2):
            b = 32 + (h * 2 + qk) * 4
            for j in range(3):
                pcol[:, b + j] = cw[j, qk, h]
            pcol[:, b + 3] = cb[qk, h]
    pcol[0:4, 64:68] = np.asarray(inp["ml_b_if"], f).reshape(4, 4).T
    w2g = np.concatenate([np.asarray(inp["gla_w2_fwd"], f).reshape(16, 512), np.asarray(inp["gla_w2_bwd"], f).reshape(16, 512)], axis=1)
    gvec = np.stack([np.asarray(inp[k], f).reshape(1024) for k in ("g_ffn1", "g_mix", "g_ffn2", "g_final", "gla_norm", "ml_norm")], 0)
    return pcol, np.ascontiguousarray(w2g), np.ascontiguousarray(gvec)


def run(inputs, xs_per_core, nseq, stages=("ffn1", "mix", "ffn2"), units=None, ncores=8):
    key = (nseq, tuple(stages), None if units is None else tuple(units))
    if key not in _CACHE:
        _CACHE[key] = Builder(nseq, stages, units).build()
    nc = _CACHE[key]
    pcol, w2g, gvec = pack_small(inputs)
    f = np.float32
    shared = {
        "w_ffn1_in": np.ascontiguousarray(np.asarray(inputs["w_ffn1_in"], f).reshape(D, 2 * DFF)),
        "w_ffn1_out": np.ascontiguousarray(np.asarray(inputs["w_ffn1_out"], f).reshape(DFF, D)),
        "w_ffn2_in": np.ascontiguousarray(np.asarray(inputs["w_ffn2_in"], f).reshape(D, 2 * DFF)),
        "w_ffn2_out": np.ascontiguousarray(np.asarray(inputs["w_ffn2_out"], f).reshape(DFF, D)),
        "w_in": np.ascontiguousarray(np.asarray(inputs["w_in"], f).reshape(D, DIN)),
        "w_out": np.ascontiguousarray(np.asarray(inputs["w_out"], f).reshape(D, D)),
        "pcol": pcol, "w2g": w2g, "gvec": gvec,
    }
    in_maps = [dict(shared, x=np.ascontiguousarray(xs_per_core[c])) for c in range(ncores)]
    res = run_bass_kernel_spmd(nc, in_maps, core_ids=list(range(ncores)))
    return [res.results[c]["y"] for c in range(ncores)]


def kernel(**inputs):
    xp = np.asarray(inputs["x_prompt"], np.float32)
    xs = np.asarray(inputs["x_sample"], np.float32)
    per_core = []
    for c in range(8):
        per_core.append(np.concatenate([xp[4 * c:4 * c + 4].reshape(4 * L, D), xs[2 * c:2 * c + 2].reshape(2 * L, D)], axis=0))
    ys = run(inputs, per_core, 6)
    yp = np.stack([ys[c][0:4 * L].reshape(4, L, D) for c in range(8)], 0).reshape(32, L, D)
    ysm = np.stack([ys[c][4 * L:6 * L].reshape(2, L, D) for c in range(8)], 0).reshape(16, L, D)
    return (np.ascontiguousarray(yp, dtype=np.float32), np.ascontiguousarray(ysm, dtype=np.float32))
```

```python
import math
import types
import numpy as np
from contextlib import ExitStack
import concourse.bass as bass
import concourse.mybir as mybir
from concourse.bass_utils import run_bass_kernel_spmd

F32 = mybir.dt.float32
BF16 = mybir.dt.bfloat16
I32 = mybir.dt.int32
AF = mybir.ActivationFunctionType
ALU = mybir.AluOpType

ENGS = ("pe", "act", "dve", "pool", "sp")
SEM_EPOCH = 30000
DMA_RR = 6

D = 1024
L = 2048
NT = 16
DFF = 2816
NJ = 22
DIN = 8240
EPS = 1e-6
TAU = 16.0
O_GQ, O_GK, O_GV, O_GR, O_LR, O_MQK, O_MV, O_MO, O_MIF, O_GA, O_GB = 0, 512, 1024, 2048, 3072, 3104, 4128, 5152, 6176, 6192, 7216


class TT:
    __slots__ = ("ap", "w", "r")

    def __init__(self, ap):
        self.ap = ap
        self.w = None
        self.r = []

    def __getitem__(self, k):
        return self.ap[k]


class Ins:
    __slots__ = ("eng", "idx", "fn", "deps", "dma", "dma_idx", "signal", "cnt")

    def __init__(self, eng, idx, fn, deps, dma, dma_idx):
        self.eng, self.idx, self.fn, self.deps = eng, idx, fn, deps
        self.dma, self.dma_idx = dma, dma_idx
        self.signal = False
        self.cnt = 0


def _snap(fn):
    if fn is None or fn.__closure__ is None:
        return fn
    cells = []
    for c in fn.__closure__:
        try:
            cells.append(types.CellType(c.cell_contents))
        except ValueError:
            cells.append(c)
    g = types.FunctionType(fn.__code__, fn.__globals__, fn.__name__, fn.__defaults__, tuple(cells))
    g.__kwdefaults__ = fn.__kwdefaults__
    return g


class Prog:
    def __init__(self, nc):
        self.nc = nc
        self.q = {e: [] for e in ENGS}
        self.ndma = {e: 0 for e in ENGS}

    def add(self, eng, fn, rd=(), wr=(), dma=False):
        q = self.q[eng]
        idx = len(q)
        deps = set()
        for t in rd:
            if t.w is not None:
                deps.add(t.w)
        for t in wr:
            if t.w is not None:
                deps.add(t.w)
            deps.update(t.r)
        dma_idx = -1
        if dma:
            dma_idx = self.ndma[eng]
            self.ndma[eng] += 1
        ins = Ins(eng, idx, _snap(fn), deps, dma, dma_idx)
        q.append(ins)
        me = (eng, idx)
        for t in rd:
            t.r.append(me)
        for t in wr:
            t.w = me
            t.r = []
        return ins

    def barrier(self):
        last = []
        for e in ENGS:
            if self.q[e]:
                last.append((e, len(self.q[e]) - 1))
        dmas = []
        for e in ENGS:
            if self.ndma[e]:
                cnt = 0
                for ins in reversed(self.q[e]):
                    if ins.dma:
                        dmas.append((e, ins.idx))
                        cnt += 1
                        if cnt >= DMA_RR:
                            break
        for e in ENGS:
            ins = self.add(e, None)
            ins.deps.update(last)
            ins.deps.update(dmas)
            ins.deps.discard((e, ins.idx))

    def emit(self):
        with ExitStack() as st:
            self._emit(st)

    def _emit(self, st):
        nc = self.nc
        q = self.q
        for e in ENGS:
            for ins in q[e]:
                for (e2, i2) in ins.deps:
                    d = q[e2][i2]
                    if d.dma:
                        continue
                    if e2 == e and e == "pe":
                        continue
                    d.signal = True
        nsig = {}
        for e in ENGS:
            c = 0
            for ins in q[e]:
                if ins.signal and not ins.dma:
                    c += 1
                    ins.cnt = c
            nsig[e] = c
        sems = {}
        for e in ENGS:
            nep = nsig[e] // SEM_EPOCH + 1
            sems[e] = [st.enter_context(nc.semaphore(f"s_{e}_{k}")) for k in range(nep)]
        dsems = {}
        for e in ENGS:
            dsems[e] = [st.enter_context(nc.semaphore(f"d_{e}_{k}")) for k in range(DMA_RR)] if self.ndma[e] else []

        def target(d):
            if d.dma:
                return dsems[d.eng][d.dma_idx % DMA_RR], 16 * (d.dma_idx // DMA_RR + 1)
            c = d.cnt
            ep = (c - 1) // SEM_EPOCH
            return sems[d.eng][ep], c - ep * SEM_EPOCH

        stats = {e: [0, 0] for e in ENGS}

        def body(e):
            def run(eng):
                waited = {}
                for ins in q[e]:
                    need = {}
                    for (e2, i2) in ins.deps:
                        d = q[e2][i2]
                        if e2 == e and e == "pe" and not d.dma:
                            continue
                        s, v = target(d)
                        key = id(s)
                        if need.get(key, (None, 0))[1] < v:
                            need[key] = (s, v)
                    if ins.dma and ins.dma_idx >= DMA_RR:
                        s = dsems[e][ins.dma_idx % DMA_RR]
                        v = 16 * (ins.dma_idx // DMA_RR)
                        key = id(s)
                        if need.get(key, (None, 0))[1] < v:
                            need[key] = (s, v)
                    for key, (s, v) in need.items():
                        if waited.get(key, 0) >= v:
                            continue
                        eng.wait_ge(s, v)
                        waited[key] = v
                        stats[e][1] += 1
                    if ins.fn is None:
                        if ins.signal:
                            r = eng.nop()
                        else:
                            continue
                    else:
                        r = ins.fn(eng)
                    stats[e][0] += 1
                    if ins.dma:
                        s, _ = target(ins)
                        r.then_inc(s, 16)
                    elif ins.signal:
                        s, _ = target(ins)
                        r.then_inc(s, 1)
            return run

        block = st.enter_context(nc.Block())
        reg = {"pe": block.tensor, "act": block.scalar, "dve": block.vector,
               "pool": block.gpsimd, "sp": block.sync}
        for e in ENGS:
            if q[e]:
                reg[e](body(e))
        self.stats = stats


class Builder:
    def __init__(self, nseq, stages=("ffn1", "mix", "ffn2"), units=None):
        self.nseq = nseq
        self.stages = stages
        self.units = units if units is not None else [("gla", h) for h in range(4)] + [("ml", h) for h in range(4)]
        self.nc = bass.Bass("TRN2", target_bir_lowering=False)
        self.st = ExitStack()

    def dram(self, name, shape, kind="ExternalInput"):
        return self.nc.dram_tensor(name, shape, F32, kind=kind).ap()

    def sb(self, name, shape, dt):
        return self.st.enter_context(self.nc.sbuf_tensor(name, shape, dt))

    def carve(self, nelem, dt):
        n16 = nelem * (2 if dt == F32 else 1)
        n16 = (n16 + 15) // 16 * 16
        a = self.scr[:, self.sp:self.sp + n16]
        self.sp += n16
        assert self.sp <= self.SCR, (self.sp, self.SCR)
        if dt == F32:
            a = a.bitcast(F32)
        return a

    def build(self):
        nc = self.nc
        P = self.P = Prog(nc)
        ns = self.nseq
        self.x_d = self.dram("x", [ns * L, D])
        self.y_d = self.dram("y", [ns * L, D], kind="ExternalOutput")
        self.w1i = self.dram("w_ffn1_in", [D, 2 * DFF])
        self.w1o = self.dram("w_ffn1_out", [DFF, D])
        self.w2i = self.dram("w_ffn2_in", [D, 2 * DFF])
        self.w2o = self.dram("w_ffn2_out", [DFF, D])
        self.win = self.dram("w_in", [D, DIN])
        self.wout = self.dram("w_out", [D, D])
        self.pcol_d = self.dram("pcol", [128, 72])
        self.w2g_d = self.dram("w2g", [16, 1024])
        self.gvec_d = self.dram("gvec", [6, 1024])

        with self.st:
            self.alloc()
            self.setup_consts()
            for s in range(ns):
                self.sequence(s)
            P.barrier()
            P.emit()
        return nc

    def alloc(self):
        sb = self.sb
        self.xbuf = sb("xbuf", [128, NT, D], F32)
        self.X = [TT(self.xbuf[:, i, :]) for i in range(NT)]
        self.identf = TT(sb("identf", [128, 128], F32))
        self.ident = TT(sb("ident", [128, 128], BF16))
        self.maskF = TT(sb("maskF", [128, 128], F32))
        self.maskB = TT(sb("maskB", [128, 128], F32))
        self.pcol = TT(sb("pcol_s", [128, 72], F32))
        self.nb2 = TT(sb("nb2", [128, 8], F32))
        self.nbif = TT(sb("nbif", [4, 4], F32))
        self.w2g = TT(sb("w2g_s", [16, 1024], BF16))
        self.wsm = TT(sb("wsm", [128, 8, 48], BF16))
        self.rm = TT(sb("rm", [128, 512], F32))
        self.ones1 = TT(sb("ones1", [128, 1], F32))
        self.sel = [TT(sb(f"sel{h}", [4, 128], F32)) for h in range(4)]
        self.ss = [TT(sb(f"ss{k}", [128, 16], F32)) for k in range(2)]
        self.rs = [TT(sb(f"rs{k}", [128, 16], F32)) for k in range(2)]
        self.rtmp = TT(sb("rtmp", [128, 48], F32))
        self.ssk = 0
        self.SCR = 68032
        self.scr = sb("scr", [128, self.SCR], BF16)
        ps = lambda n, shape, dt: TT(self.st.enter_context(self.nc.psum_tensor(n, shape, dt)))
        self.PS = [ps(f"ps{k}", [128, 512], F32) for k in range(6)]
        self.PT = [ps(f"pt{k}", [128, 1024], BF16) for k in range(2)]
        self.layout_ffn()
        self.layout_mix()

    def layout_ffn(self):
        self.sp = 0
        c = self.carve
        self.f_gb = TT(c(1024, F32))
        self.f_gfin = TT(c(1024, F32))
        hT = c(8 * 1024, BF16).rearrange("p (k t) -> p k t", k=8)
        self.f_hTbuf = hT
        self.f_HT = [TT(hT[:, :, g * 512:(g + 1) * 512]) for g in range(2)]
        self.f_hb = [TT(c(1024, BF16)), TT(c(1024, BF16))]
        self.f_junk = TT(c(1024, BF16))
        act = c(11 * 1024, BF16).rearrange("p (j t) -> p j t", j=11)
        self.f_actbuf = act
        self.f_ACT = [TT(act[:, :, g * 512:(g + 1) * 512]) for g in range(2)]
        self.f_wo = [TT(c(11 * 1024, BF16).rearrange("p (j d) -> p j d", j=11)) for _ in range(2)]
        self.f_wi = [TT(c(8 * 2 * 128, BF16).rearrange("p (k a c) -> p k a c", k=8, a=2)) for _ in range(4)]
        self.f_sa = [TT(c(512, BF16)) for _ in range(2)]
        self.f_yst = [TT(c(1024, F32)) for _ in range(2)]
        self.f_end = self.sp
        print('ffn scratch units', self.sp)

    def layout_mix(self):
        self.sp = 0
        c = self.carve
        alt = c(4096, BF16)
        self.m_gb = TT(alt[:, 0:2048].bitcast(F32))
        hT = c(8 * L, BF16).rearrange("p (k t) -> p k t", k=8)
        self.m_hTbuf = hT
        self.m_HT = [TT(hT[:, :, g * 512:(g + 1) * 512]) for g in range(4)]
        _hb = TT(alt[:, 2048:3072])
        self.m_junk = TT(alt[:, 3072:4096])
        self.m_hb = [_hb, self.m_junk]
        w12 = c(2 * 8 * 256, BF16).rearrange("p (w k c) -> p w k c", w=2, k=8)
        w12b = alt.rearrange("p (w k c) -> p w k c", w=2, k=8)
        self.m_W12s = [w12, w12b]
        self.m_W1s = [TT(w12[:, 0]), TT(w12b[:, 0])]
        self.m_W2s = [TT(w12[:, 1]), TT(w12b[:, 1])]
        self.set_w(0)
        self.m_wo = TT(c(2 * 1024, BF16).rearrange("p (k d) -> p k d", k=2))
        self.m_gn = TT(c(256, F32))
        self.m_lrg = [TT(c(512, BF16)[0:16, :]) for _ in range(2)]
        self.m_wst = [TT(c(64, F32)) for _ in range(2)]
        self.m_en2 = [TT(c(64, F32)) for _ in range(2)]
        self.m_decb = TT(c(128, F32))
        B = [TT(c(512, F32)) for _ in range(5)]
        self.m_B = B
        self.m_th = [B[0], B[1]]
        self.m_t12 = [B[2], B[3]]
        self.m_mb = [TT(c(256, BF16)) for _ in range(2)]
        self.m_mT = [TT(c(256, BF16)) for _ in range(2)]
        self.m_sq = TT(c(256, BF16))
        base = self.sp
        self.g_qin = [[TT(a[:, g * 512:(g + 1) * 512]) for g in range(4)] for a in (c(L, BF16), c(L, BF16))]
        self.g_kin = [[TT(a[:, g * 512:(g + 1) * 512]) for g in range(4)] for a in (c(L, BF16), c(L, BF16))]
        ksb = [c(NT * 128, BF16).rearrange("p (t d) -> p t d", t=NT) for _ in range(2)]
        self.g_ksbuf = ksb
        self.g_ks = [[TT(a[:, g * 4:(g + 1) * 4, :]) for g in range(4)] for a in ksb]
        vb = c(NT * 256, BF16).rearrange("p (t d) -> p t d", t=NT)
        self.g_vbuf = vb
        self.g_v = [TT(vb[:, g * 4:(g + 1) * 4, :]) for g in range(4)]
        ob = c(NT * 256, F32).rearrange("p (t d) -> p t d", t=NT)
        self.g_obuf = ob
        self.g_o = [TT(ob[:, i, :]) for i in range(NT)]
        self.g_Lin = [TT(c(512, F32)), self.m_B[0]]
        self.g_Lc = [TT(c(512, F32)), self.m_B[1]]
        self.g_E = [[TT(c(512, F32)) for _ in range(3)], [self.m_B[2], self.m_B[3], self.m_B[4]]]
        self.g_ksT = [TT(c(512, BF16)) for _ in range(2)]
        self.g_nl = [TT(c(4, F32)) for _ in range(2)]
        self.g_dec = TT(c(32, F32))
        self.g_att = [TT(c(128, BF16)) for _ in range(2)]
        self.g_S32 = [TT(c(256, F32)) for _ in range(2)]
        self.g_Sb = [TT(c(256, BF16)) for _ in range(2)]
        gla_end = self.sp
        self.sp = base
        self.l_pre = TT(c(L + 2, F32))
        self.l_cv = TT(c(L, F32))
        qa, ka = c(L, BF16), c(L, BF16)
        self.l_qTa = TT(qa)
        self.l_kTa = TT(ka)
        kt = c(NT * 128, BF16).rearrange("p (t d) -> p t d", t=NT)
        self.l_ktok = TT(kt)
        vw = [c(NT * 258, BF16).rearrange("p (t d) -> p t d", t=NT) for _ in range(2)]
        self.l_vwbuf = vw
        self.l_vw = [TT(a) for a in vw]
        hb_ = c(NT * 256, F32).rearrange("p (t d) -> p t d", t=NT)
        self.l_h = [TT(hb_[:, i, :]) for i in range(NT)]
        self.l_sT = [TT(c(128, BF16)) for _ in range(2)]
        self.l_C32 = [TT(c(258, F32)) for _ in range(2)]
        self.l_Cb = [TT(c(258, BF16)) for _ in range(2)]
        self.l_dn = [TT(c(4, F32)) for _ in range(2)]
        ml_end = self.sp
        self.sp = base
        self.r_rows = [(TT(c(L, F32)[0:4, :]), TT(c(L, F32)[0:4, :]), TT(c(L, F32)[0:4, :]), TT(c(64, F32)[0:4, :])) for _ in range(2)]
        self.sp = max(gla_end, ml_end, self.sp)
        self.m_end = self.sp
        print('mixer scratch units', self.sp, 'gla_end', gla_end, 'ml_end', ml_end)

    def rsqrt(self, src, dst, n, addc):
        P = self.P
        t = self.rtmp
        a, y0 = t[:, 0:n], t[:, 16:16 + n]
        w = t[:, 32:32 + n]
        P.add("dve", lambda e: e.tensor_scalar(out=a, in0=src[:, 0:n], scalar1=addc, scalar2=None, op0=ALU.add), rd=[src], wr=[t])
        P.add("dve", lambda e: e.tensor_single_scalar(out=w.bitcast(I32), in_=a.bitcast(I32), scalar=1, op=ALU.arith_shift_right), rd=[t], wr=[t])
        P.add("dve", lambda e: e.tensor_scalar(out=y0.bitcast(I32), in0=w.bitcast(I32), scalar1=-1, scalar2=1597463007, op0=ALU.mult, op1=ALU.add), rd=[t], wr=[t])
        for it in range(3):
            P.add("dve", lambda e: e.tensor_tensor(out=w, in0=a, in1=y0, op=ALU.mult), rd=[t], wr=[t])
            P.add("dve", lambda e: e.tensor_tensor(out=w, in0=w, in1=y0, op=ALU.mult), rd=[t], wr=[t])
            P.add("dve", lambda e: e.tensor_scalar(out=w, in0=w, scalar1=-0.5, scalar2=1.5, op0=ALU.mult, op1=ALU.add), rd=[t], wr=[t])
            if it < 2:
                P.add("dve", lambda e: e.tensor_tensor(out=y0, in0=y0, in1=w, op=ALU.mult), rd=[t], wr=[t])
            else:
                P.add("dve", lambda e: e.tensor_tensor(out=dst[:, 0:n], in0=y0, in1=w, op=ALU.mult), rd=[t], wr=[dst])

    def setup_consts(self):
        P = self.P
        idf, idb = self.identf, self.ident
        P.add("pool", lambda e: e.memset(idf[:], 0.0), wr=[idf])
        P.add("pool", lambda e: e.affine_select(out=idf[:], in_=idf[:], pattern=[[-1, 128]], compare_op=ALU.not_equal, fill=1.0, base=0, channel_multiplier=1), rd=[idf], wr=[idf])
        P.add("dve", lambda e: e.tensor_copy(out=idb[:], in_=idf[:]), rd=[idf], wr=[idb])
        mF, mB = self.maskF, self.maskB
        P.add("pool", lambda e: e.memset(mF[:], 1.0), wr=[mF])
        P.add("pool", lambda e: e.affine_select(out=mF[:], in_=mF[:], pattern=[[1, 128]], compare_op=ALU.is_ge, fill=0.0, base=0, channel_multiplier=-1), rd=[mF], wr=[mF])
        P.add("pool", lambda e: e.memset(mB[:], 1.0), wr=[mB])
        P.add("pool", lambda e: e.affine_select(out=mB[:], in_=mB[:], pattern=[[-1, 128]], compare_op=ALU.is_ge, fill=0.0, base=0, channel_multiplier=1), rd=[mB], wr=[mB])
        P.add("sp", lambda e: e.dma_start(out=self.pcol[:], in_=self.pcol_d[:, :]), wr=[self.pcol], dma=True)
        P.add("pool", lambda e: e.dma_start(out=self.w2g[:], in_=self.w2g_d[:, :]), wr=[self.w2g], dma=True)
        wsrc = self.win.rearrange("(k p) c -> p k c", p=128)
        P.add("pool", lambda e: e.dma_start(out=self.wsm[:, :, 0:32], in_=wsrc[:, :, O_LR:O_LR + 32]), wr=[self.wsm], dma=True)
        P.add("pool", lambda e: e.dma_start(out=self.wsm[:, :, 32:48], in_=wsrc[:, :, O_MIF:O_MIF + 16]), wr=[self.wsm], dma=True)
        P.add("dve", lambda e: e.tensor_scalar(out=self.nb2[:], in0=self.pcol[:, 24:32], scalar1=-1.0, scalar2=None, op0=ALU.mult), rd=[self.pcol], wr=[self.nb2])
        P.add("dve", lambda e: e.tensor_scalar(out=self.nbif[:], in0=self.pcol[0:4, 64:68], scalar1=-1.0, scalar2=None, op0=ALU.mult), rd=[self.pcol], wr=[self.nbif])
        rm = self.rm
        P.add("dve", lambda e: e.memset(rm[:], 1.0), wr=[rm])
        P.add("dve", lambda e: e.memset(rm[:].rearrange("p (c t) -> p c t", c=4)[:, :, 0:1], 0.0), rd=[rm], wr=[rm])
        P.add("dve", lambda e: e.memset(self.ones1[:], 1.0), wr=[self.ones1])
        for h in range(4):
            P.add("dve", lambda e, h=h: e.tensor_copy(out=self.sel[h][:], in_=idf[0:4, h:h + 1].to_broadcast([4, 128])), rd=[idf], wr=[self.sel[h]])

    def sequence(self, s):
        P = self.P
        for i in range(NT):
            r0 = s * L + i * 128
            P.add("sp", lambda e, i=i, r0=r0: e.dma_start(out=self.X[i][:], in_=self.x_d[r0:r0 + 128, :]), wr=[self.X[i]], dma=True)
        if "ffn1" in self.stages:
            self.ffn(s, self.w1i, self.w1o, 0, final=False)
        if "mix" in self.stages:
            P.barrier()
            self.mixer(s)
            P.barrier()
        self.ffn(s, self.w2i, self.w2o, 2, final=True, skip_ffn=("ffn2" not in self.stages))

    def load_gvec(self, dst, row):
        self.P.add("sp", lambda e: e.dma_start(out=dst[:], in_=self.gvec_d[row:row + 1, :].partition_broadcast(128)[:, 0, :]), wr=[dst], dma=True)

    def norm_T(self, tiles, gb, hb2, junk, dst_fn, dst_tt_fn):
        k = self.ssk
        self.ssk ^= 1
        for j in range(len(tiles)):
            self.norm_sq(tiles, j, junk, k)
        self.rsqrt(self.ss[k], self.rs[k], len(tiles), D * EPS)
        self.norm_B(tiles, gb, hb2, dst_fn, dst_tt_fn, k)

    def norm_sq(self, tiles, j, junk, k):
        ss = self.ss[k]
        i = tiles[j]
        self.P.add("act", lambda e: e.activation(out=junk[:], in_=self.X[i][:], func=AF.Square, accum_out=ss[:, j:j + 1]), rd=[self.X[i]], wr=[junk, ss])

    def norm_B(self, tiles, gb, hb2, dst_fn, dst_tt_fn, k):
        P = self.P
        rs = self.rs[k]
        for j, i in enumerate(tiles):
            hb = hb2[j % 2]
            pt = self.PT[j % 2]
            P.add("dve", lambda e, i=i, j=j, hb=hb: e.scalar_tensor_tensor(out=hb[:], in0=self.X[i][:], scalar=rs[:, j:j + 1], in1=gb[:], op0=ALU.mult, op1=ALU.mult), rd=[self.X[i], rs, gb], wr=[hb])
            for kc in range(8):
                P.add("pe", lambda e, kc=kc, hb=hb, pt=pt: e.transpose(out=pt[:, kc * 128:(kc + 1) * 128], in_=hb[:, kc * 128:(kc + 1) * 128], identity=self.ident[:]), rd=[hb, self.ident], wr=[pt])
            dst = dst_fn(j)
            P.add("act", lambda e, dst=dst, pt=pt: e.copy(out=dst, in_=pt[:].rearrange("p (k t) -> p k t", k=8)), rd=[pt], wr=[dst_tt_fn(j)])

    def ffn(self, s, wi_d, wo_d, grow, final, skip_ffn=False):
        P = self.P
        gb = self.f_gb
        if not skip_ffn:
            self.load_gvec(gb, grow)
            P.add("dve", lambda e: e.tensor_scalar(out=gb[:], in0=gb[:], scalar1=float(math.sqrt(D)), scalar2=None, op0=ALU.mult), rd=[gb], wr=[gb])
        if final:
            self.load_gvec(self.f_gfin, 3)
            P.add("dve", lambda e: e.tensor_scalar(out=self.f_gfin[:], in0=self.f_gfin[:], scalar1=float(math.sqrt(D)), scalar2=None, op0=ALU.mult), rd=[self.f_gfin], wr=[self.f_gfin])
        wi_src = wi_d.rearrange("(k p) (a c) -> p k a c", p=128, a=2)
        wslot = 0
        hTb = self.f_hTbuf
        dstf, dsttt = (lambda j: hTb[:, :, j * 128:(j + 1) * 128]), (lambda j: self.f_HT[j // 4])
        for p in range(2):
            tiles = list(range(p * 8, p * 8 + 8))
            ntiles = list(range(8, 16))
            if not skip_ffn:
                if p == 0:
                    self.norm_T(tiles, gb, self.f_hb, self.f_junk, dstf, dsttt)
                for hf in range(2):
                    hoist = (p == 0 and hf == 1)
                    if hoist:
                        kn = self.ssk
                        self.ssk ^= 1
                    wo = self.f_wo[hf]
                    P.add("pool", lambda e, wo=wo, hf=hf: e.dma_start(out=wo[:], in_=wo_d[hf * 1408:(hf + 1) * 1408, :].rearrange("(j p) d -> p j d", p=128)), wr=[wo], dma=True)
                    for jj in range(11):
                        j = hf * 11 + jj
                        wi = self.f_wi[wslot % 4]
                        wslot += 1
                        for a in range(2):
                            P.add("pool", lambda e, wi=wi, j=j, a=a: e.dma_start(out=wi[:, :, a, :], in_=wi_src[:, :, a, j * 128:(j + 1) * 128]), wr=[wi], dma=True)
                        for g in range(2):
                            pa, pg = self.PS[2 * g], self.PS[2 * g + 1]
                            ht = self.f_HT[g]
                            for a, pp in ((0, pa), (1, pg)):
                                for kc in range(8):
                                    P.add("pe", lambda e, a=a, pp=pp, kc=kc, wi=wi, ht=ht: e.matmul(pp[:], lhsT=wi[:, kc, a, :], rhs=ht[:, kc, :], start=(kc == 0), stop=(kc == 7)), rd=[wi, ht], wr=[pp])
                            sa = self.f_sa[g]
                            P.add("act", lambda e, sa=sa, pa=pa: e.activation(out=sa[:], in_=pa[:], func=AF.Silu), rd=[pa], wr=[sa])
                            at = self.f_ACT[g]
                            P.add("dve", lambda e, at=at, sa=sa, pg=pg, jj=jj: e.tensor_tensor(out=at[:, jj, :], in0=sa[:], in1=pg[:], op=ALU.mult), rd=[sa, pg], wr=[at])
                        if hoist and jj < 8:
                            self.norm_sq(ntiles, jj, self.f_junk, kn)
                        if hoist and jj == 8:
                            self.rsqrt(self.ss[kn], self.rs[kn], 8, D * EPS)
                    if hoist:
                        self.norm_B(ntiles, gb, self.f_hb, dstf, dsttt, kn)
                    for jt, i in enumerate(tiles):
                        at = self.f_ACT[jt // 4]
                        t0 = (jt % 4) * 128
                        for ch in range(2):
                            py = self.PS[4 + (jt * 2 + ch) % 2]
                            for jj in range(11):
                                P.add("pe", lambda e, py=py, at=at, t0=t0, jj=jj, ch=ch, wo=wo: e.matmul(py[:], lhsT=at[:, jj, t0:t0 + 128], rhs=wo[:, jj, ch * 512:(ch + 1) * 512], start=(jj == 0), stop=(jj == 10)), rd=[at, wo], wr=[py])
                            P.add("dve", lambda e, py=py, i=i, ch=ch: e.scalar_tensor_tensor(out=self.X[i][:, ch * 512:(ch + 1) * 512], in0=py[:], scalar=0.5, in1=self.X[i][:, ch * 512:(ch + 1) * 512], op0=ALU.mult, op1=ALU.add), rd=[py, self.X[i]], wr=[self.X[i]])
            if final:
                k = self.ssk
                self.ssk ^= 1
                ss, rs = self.ss[k], self.rs[k]
                for j, i in enumerate(tiles):
                    P.add("act", lambda e, i=i, j=j: e.activation(out=self.f_junk[:], in_=self.X[i][:], func=AF.Square, accum_out=ss[:, j:j + 1]), rd=[self.X[i]], wr=[self.f_junk, ss])
                self.rsqrt(ss, rs, 8, D * EPS)
                for j, i in enumerate(tiles):
                    yst = self.f_yst[j % 2]
                    r0 = s * L + i * 128
                    P.add("dve", lambda e, i=i, j=j, yst=yst: e.scalar_tensor_tensor(out=yst[:], in0=self.X[i][:], scalar=rs[:, j:j + 1], in1=self.f_gfin[:], op0=ALU.mult, op1=ALU.mult), rd=[self.X[i], rs, self.f_gfin], wr=[yst])
                    P.add("sp", lambda e, yst=yst, r0=r0: e.dma_start(out=self.y_d[r0:r0 + 128, :], in_=yst[:]), rd=[yst], dma=True)

    def mixer(self, s):
        P = self.P
        gb = self.m_gb
        self.load_gvec(gb, 1)
        P.add("dve", lambda e: e.tensor_scalar(out=gb[:], in0=gb[:], scalar1=float(math.sqrt(D)), scalar2=None, op0=ALU.mult), rd=[gb], wr=[gb])
        hTb = self.m_hTbuf
        need_ml = any(u[0] == "ml" for u in self.units)
        self.load_qkv(self.units[0], 0)
        for half in range(2):
            tiles = list(range(half * 8, half * 8 + 8))
            self.norm_T(tiles, gb, self.m_hb, self.m_junk,
                        lambda j, half=half: hTb[:, :, (half * 8 + j) * 128:(half * 8 + j + 1) * 128],
                        lambda j, half=half: self.m_HT[(half * 8 + j) // 4])
            if need_ml:
                self.ml_gates_proj([2 * half, 2 * half + 1])
        if need_ml:
            self.ml_gates()
        P.barrier()
        prev = None
        self._preloaded = True
        for idx, (kind, h) in enumerate(self.units):
            if prev is not None and prev != kind:
                P.barrier()
            prev = kind
            self._unit, self._par = (kind, h), idx % 2
            self._next = self.units[idx + 1] if idx + 1 < len(self.units) else None
            self.set_w(idx % 2)
            if kind == "gla":
                self.gla_unit(h)
            else:
                self.ml_unit(h)
            self._preloaded = self._next is not None

    def proj_feat(self, pp, lhs_fn, g, M=128):
        P = self.P
        ht = self.m_HT[g]
        for kc in range(8):
            lhs, rd = lhs_fn(kc)
            P.add("pe", lambda e, kc=kc, lhs=lhs: e.matmul(pp[0:M, :], lhsT=lhs, rhs=ht[:, kc, :], start=(kc == 0), stop=(kc == 7)), rd=[rd, ht], wr=[pp])

    def proj_tok(self, out_ap, pp, w, i, c0, n):
        P = self.P
        ht = self.m_HT[i // 4]
        t0 = (i % 4) * 128
        for kc in range(8):
            P.add("pe", lambda e, kc=kc: e.matmul(out_ap, lhsT=ht[:, kc, t0:t0 + 128], rhs=w[:, kc, c0:c0 + n], start=(kc == 0), stop=(kc == 7)), rd=[ht, w], wr=[pp])

    def lr_proj(self):
        P = self.P
        for d in range(2):
            for g in range(4):
                pp = self.PS[(d * 4 + g) % 2]
                self.proj_feat(pp, lambda kc, d=d: (self.wsm[:, kc, d * 16:(d + 1) * 16], self.wsm), g, M=16)
                lr = self.m_lrT[d]
                P.add("act", lambda e, pp=pp, lr=lr, g=g: e.copy(out=lr[:, g * 512:(g + 1) * 512], in_=pp[0:16, :]), rd=[pp], wr=[lr])

    def set_w(self, par):
        self.m_W12, self.m_W1, self.m_W2 = self.m_W12s[par], self.m_W1s[par], self.m_W2s[par]

    def load_qkv(self, unit, par):
        P = self.P
        kind, h = unit
        if kind == "gla":
            cq, ck, cv = O_GQ + h * 128, O_GK + h * 128, O_GV + h * 256
        else:
            cq, ck, cv = O_MQK + h * 128, O_MQK + 512 + h * 128, O_MV + h * 256
        wsrc = self.win.rearrange("(k p) c -> p k c", p=128)
        W1, W2 = self.m_W1s[par], self.m_W2s[par]
        P.add("pool", lambda e: e.dma_start(out=W1[:, :, 0:128], in_=wsrc[:, :, cq:cq + 128]), wr=[W1], dma=True)
        P.add("pool", lambda e: e.dma_start(out=W1[:, :, 128:256], in_=wsrc[:, :, ck:ck + 128]), wr=[W1], dma=True)
        P.add("pool", lambda e: e.dma_start(out=W2[:], in_=wsrc[:, :, cv:cv + 256]), wr=[W2], dma=True)

    def load_unit_weights(self, cq, ck, cv, cr, cg, h, gnrow):
        P = self.P
        if not self._preloaded:
            self.load_qkv(self._unit, self._par)
        self._late = (cr, cg)
        P.add("pool", lambda e: e.dma_start(out=self.m_wo[:], in_=self.wout[h * 256:(h + 1) * 256, :].rearrange("(k p) d -> p k d", p=128)), wr=[self.m_wo], dma=True)
        gn = self.m_gn
        P.add("sp", lambda e: e.dma_start(out=gn[:], in_=self.gvec_d[gnrow:gnrow + 1, h * 256:(h + 1) * 256].partition_broadcast(128)[:, 0, :]), wr=[gn], dma=True)
        gsc = 8.0 if gnrow == 4 else 4.0
        P.add("dve", lambda e: e.tensor_scalar(out=gn[:], in0=gn[:], scalar1=gsc, scalar2=None, op0=ALU.mult), rd=[gn], wr=[gn])

    def load_late_weights(self):
        P = self.P
        wsrc = self.win.rearrange("(k p) c -> p k c", p=128)
        cr, cg = self._late
        W1, W2 = self.m_W1, self.m_W2
        P.add("pool", lambda e: e.dma_start(out=W1[:], in_=wsrc[:, :, cr:cr + 256]), wr=[W1], dma=True)
        P.add("pool", lambda e: e.dma_start(out=W2[:], in_=wsrc[:, :, cg:cg + 256]), wr=[W2], dma=True)
        if self._next is not None:
            self.load_qkv(self._next, 1 - self._par)

    def gla_unit(self, h):
        P = self.P
        self.load_unit_weights(O_GQ + h * 128, O_GK + h * 128, O_GV + h * 256, O_GR + h * 256, O_GA + h * 256, h, 4)
        lnsc = float(math.log(128.0 ** -0.5))
        dec = self.g_dec
        order = [0, 3, 1, 2]
        gpar = {g: n_ % 2 for n_, g in enumerate(order)}

        def stageApe(g):
            pq, pk = (self.PS[0], self.PS[1]) if gpar[g] == 0 else (self.PS[4], self.PS[5])
            self.proj_feat(pq, lambda kc: (self.m_W1[:, kc, 0:128], self.m_W1), g)
            self.proj_feat(pk, lambda kc: (self.m_W1[:, kc, 128:256], self.m_W1), g)
            for d in range(2):
                pz = self.PS[2 + d]
                lr = self.m_lrg[d]
                self.proj_feat(pz, lambda kc, d=d: (self.wsm[:, kc, d * 16:(d + 1) * 16], self.wsm), g, M=16)
                P.add("dve", lambda e, pz=pz, lr=lr: e.tensor_copy(out=lr[:], in_=pz[0:16, :]), rd=[pz], wr=[lr])
                P.add("pe", lambda e, pz=pz, lr=lr, d=d: e.matmul(pz[:], lhsT=self.w2g[:, d * 512 + h * 128:d * 512 + (h + 1) * 128], rhs=lr[:], start=True, stop=True), rd=[self.w2g, lr], wr=[pz])

        def stageAch(g):
            pq, pk = (self.PS[0], self.PS[1]) if gpar[g] == 0 else (self.PS[4], self.PS[5])
            for d in range(2):
                pz = self.PS[2 + d]
                Lin, Lc = self.g_Lin[d], self.g_Lc[d]
                col = d * 4 + h
                P.add("act", lambda e, pz=pz, Lin=Lin, col=col: e.activation(out=Lin[:], in_=pz[:], func=AF.Exp, scale=-1.0, bias=self.nb2[:, col:col + 1]), rd=[pz, self.nb2], wr=[Lin])
                P.add("act", lambda e, Lin=Lin: e.activation(out=Lin[:], in_=Lin[:], func=AF.Ln, bias=1.0), rd=[Lin], wr=[Lin])
                if d == 0:
                    P.add("dve", lambda e, Lin=Lin, Lc=Lc: e.tensor_tensor_scan(out=Lc[:], data0=self.rm[:], data1=Lin[:], initial=0.0, op0=ALU.mult, op1=ALU.add), rd=[self.rm, Lin], wr=[Lc])
                    last = Lc[:, 127::128]
                else:
                    P.add("dve", lambda e, Lin=Lin, Lc=Lc: e.tensor_tensor_scan(out=Lc[:][:, ::-1], data0=self.rm[:], data1=Lin[:][:, ::-1], initial=0.0, op0=ALU.mult, op1=ALU.add), rd=[self.rm, Lin], wr=[Lc])
                    last = Lc[:, 0::128]
                dsl = slice(d * 16 + g * 4, d * 16 + g * 4 + 4)
                P.add("act", lambda e, last=last, dsl=dsl: e.activation(out=dec[:, dsl], in_=last, func=AF.Exp, scale=-1.0 / TAU), rd=[Lc], wr=[dec])
                E1, E2, E3 = self.g_E[d]
                P.add("act", lambda e, E1=E1, Lc=Lc: e.activation(out=E1[:], in_=Lc[:], func=AF.Exp, scale=-1.0 / TAU, bias=lnsc), rd=[Lc], wr=[E1])
                P.add("act", lambda e, E2=E2, Lc=Lc: e.activation(out=E2[:], in_=Lc[:], func=AF.Exp, scale=1.0 / TAU), rd=[Lc], wr=[E2])
                qin, kin = self.g_qin[d][g], self.g_kin[d][g]
                ksT = self.g_ksT[d]
                P.add("dve", lambda e, qin=qin, E1=E1: e.tensor_tensor(out=qin[:], in0=pq[:], in1=E1[:], op=ALU.mult), rd=[pq, E1], wr=[qin])
                P.add("dve", lambda e, kin=kin, E2=E2: e.tensor_tensor(out=kin[:], in0=pk[:], in1=E2[:], op=ALU.mult), rd=[pk, E2], wr=[kin])
                P.add("dve", lambda e, E2=E2, E3=E3, dsl=dsl: e.tensor_tensor(out=E3[:].rearrange("p (c t) -> p c t", c=4), in0=E2[:].rearrange("p (c t) -> p c t", c=4), in1=dec[:, dsl].unsqueeze(2).to_broadcast([128, 4, 128]), op=ALU.mult), rd=[E2, dec], wr=[E3])
                P.add("dve", lambda e, ksT=ksT, E3=E3: e.tensor_tensor(out=ksT[:], in0=pk[:], in1=E3[:], op=ALU.mult), rd=[pk, E3], wr=[ksT])

        def stageT(g):
            for d in range(2):
                pt = self.PT[d]
                ksT = self.g_ksT[d]
                for c in range(4):
                    P.add("pe", lambda e, pt=pt, ksT=ksT, c=c: e.transpose(out=pt[:, c * 128:(c + 1) * 128], in_=ksT[:, c * 128:(c + 1) * 128], identity=self.ident[:]), rd=[ksT, self.ident], wr=[pt])
                ks = self.g_ks[d][g]
                P.add("dve", lambda e, ks=ks, pt=pt: e.tensor_copy(out=ks[:], in_=pt[:, 0:512].rearrange("p (t d) -> p t d", t=4)), rd=[pt], wr=[ks])

        def stageV(g):
            for pr in range(2):
                pv = self.PS[2 + pr]
                for k2 in range(2):
                    i = g * 4 + pr * 2 + k2
                    self.proj_tok(pv[:, k2 * 256:(k2 + 1) * 256], pv, self.m_W2, i, 0, 256)
                vt = self.g_v[g]
                P.add("act", lambda e, vt=vt, pv=pv, pr=pr: e.copy(out=vt[:, pr * 2:pr * 2 + 2, :], in_=pv[:].rearrange("p (t d) -> p t d", t=2)), rd=[pv], wr=[vt])

        stageApe(order[0])
        for n_, g in enumerate(order):
            stageAch(g)
            stageV(g)
            if n_ + 1 < 4:
                stageApe(order[n_ + 1])
            stageT(g)
        self.load_late_weights()
        uss, urs = self.ss[self.ssk], self.rs[self.ssk]
        self.ssk ^= 1
        for d in range(2):
            P.add("dve", lambda e, d=d: e.memset(self.g_S32[d][:], 0.0), wr=[self.g_S32[d]])
            P.add("dve", lambda e, d=d: e.memset(self.g_Sb[d][:], 0.0), wr=[self.g_Sb[d]])
        written = [False] * NT
        for step in range(NT):
            ctx = []
            for d in range(2):
                c = step if d == 0 else NT - 1 - step
                g, cc = c // 4, c % 4
                ctx.append(dict(d=d, c=c, cc=cc, pa=self.PS[3 * d], po=self.PS[3 * d + 1], pu=self.PS[3 * d + 2],
                                qin=self.g_qin[d][g], kin=self.g_kin[d][g], ks=self.g_ks[d][g], vt=self.g_v[g],
                                att=self.g_att[d], S32=self.g_S32[d], Sb=self.g_Sb[d],
                                mask=self.maskF if d == 0 else self.maskB, sl=slice(cc * 128, (cc + 1) * 128)))
            for k in ctx:
                pa, kin, qin, sl, pu, ks, vt, cc = k["pa"], k["kin"], k["qin"], k["sl"], k["pu"], k["ks"], k["vt"], k["cc"]
                P.add("pe", lambda e, pa=pa, kin=kin, qin=qin, sl=sl: e.matmul(pa[:, 0:128], lhsT=kin[:, sl], rhs=qin[:, sl], start=True, stop=True), rd=[kin, qin], wr=[pa])
                P.add("pe", lambda e, pu=pu, ks=ks, vt=vt, cc=cc: e.matmul(pu[:, 0:256], lhsT=ks[:, cc, :], rhs=vt[:, cc, :], start=True, stop=True), rd=[ks, vt], wr=[pu])
            for k in ctx:
                att, pa, mask = k["att"], k["pa"], k["mask"]
                P.add("dve", lambda e, att=att, pa=pa, mask=mask: e.tensor_tensor(out=att[:], in0=pa[:, 0:128], in1=mask[:], op=ALU.mult), rd=[pa, mask], wr=[att])
            for k in ctx:
                po, att, vt, cc, qin, Sb, sl = k["po"], k["att"], k["vt"], k["cc"], k["qin"], k["Sb"], k["sl"]
                P.add("pe", lambda e, po=po, att=att, vt=vt, cc=cc: e.matmul(po[:, 0:256], lhsT=att[:], rhs=vt[:, cc, :], start=True, stop=False), rd=[att, vt], wr=[po])
                P.add("pe", lambda e, po=po, qin=qin, Sb=Sb, sl=sl: e.matmul(po[:, 0:256], lhsT=qin[:, sl], rhs=Sb[:], start=False, stop=True), rd=[qin, Sb], wr=[po])
            for k in ctx:
                S32, pu = k["S32"], k["pu"]
                dcol = k["d"] * 16 + k["c"]
                P.add("dve", lambda e, S32=S32, pu=pu, dcol=dcol: e.scalar_tensor_tensor(out=S32[:], in0=S32[:], scalar=dec[:, dcol:dcol + 1], in1=pu[:, 0:256], op0=ALU.mult, op1=ALU.add), rd=[S32, pu, dec], wr=[S32])
            for k in ctx:
                S32, Sb = k["S32"], k["Sb"]
                P.add("act", lambda e, S32=S32, Sb=Sb: e.copy(out=Sb[:], in_=S32[:]), rd=[S32], wr=[Sb])
            for k in ctx:
                c, po = k["c"], k["po"]
                oc = self.g_o[c]
                if not written[c]:
                    written[c] = True
                    P.add("act", lambda e, oc=oc, po=po: e.copy(out=oc[:], in_=po[:, 0:256]), rd=[po], wr=[oc])
                else:
                    P.add("dve", lambda e, oc=oc, po=po: e.tensor_tensor(out=oc[:], in0=po[:, 0:256], in1=oc[:], op=ALU.add), rd=[po, oc], wr=[oc])
                    P.add("act", lambda e, oc=oc, c=c: e.activation(out=self.m_sq[:], in_=oc[:], func=AF.Square, accum_out=uss[:, c:c + 1]), rd=[oc], wr=[self.m_sq, uss])
        self.gating(self.g_o, silu_first=True, ssrs=(uss, urs))

    def gating(self, acc, silu_first, ssrs):
        P = self.P
        ss, rs = ssrs
        self.rsqrt(ss, rs, NT, 256 * EPS)
        gn = self.m_gn
        W12 = self.m_W12

        def head(i):
            b = i % 2
            pr = self.PS[3 * b]
            th, t12, mb = self.m_th[b], self.m_t12[b], self.m_mb[b]
            t1, t2 = t12[:, 0:256], t12[:, 256:512]
            ht = self.m_HT[i // 4]
            t0 = (i % 4) * 128
            for kc in range(8):
                P.add("pe", lambda e, kc=kc: e.matmul(pr[:], lhsT=ht[:, kc, t0:t0 + 128], rhs=W12[:, :, kc, :], start=(kc == 0), stop=(kc == 7)), rd=[ht, self.m_W1, self.m_W2], wr=[pr])
            if silu_first:
                P.add("act", lambda e: e.activation(out=th[:, 0:256], in_=pr[:, 0:256], func=AF.Silu), rd=[pr], wr=[th])
                P.add("act", lambda e: e.activation(out=th[:, 256:512], in_=pr[:, 256:512], func=AF.Tanh, scale=0.5), rd=[pr], wr=[th])
            else:
                P.add("act", lambda e: e.activation(out=th[:], in_=pr[:], func=AF.Tanh, scale=0.5), rd=[pr], wr=[th])
            P.add("dve", lambda e: e.scalar_tensor_tensor(out=t1, in0=acc[i][:], scalar=rs[:, i:i + 1], in1=gn[:], op0=ALU.mult, op1=ALU.mult), rd=[acc[i], rs, gn], wr=[t12])
            if silu_first:
                P.add("dve", lambda e: e.scalar_tensor_tensor(out=t2, in0=th[:, 256:512], scalar=1.0, in1=th[:, 0:256], op0=ALU.add, op1=ALU.mult), rd=[th, t12], wr=[t12])
                P.add("dve", lambda e: e.tensor_tensor(out=mb[:], in0=t1, in1=t2, op=ALU.mult), rd=[t12], wr=[mb])
            else:
                P.add("dve", lambda e: e.scalar_tensor_tensor(out=t2, in0=th[:, 0:256], scalar=1.0, in1=t1, op0=ALU.add, op1=ALU.mult), rd=[th, t12], wr=[t12])
                P.add("dve", lambda e: e.scalar_tensor_tensor(out=mb[:], in0=th[:, 256:512], scalar=1.0, in1=t2, op0=ALU.add, op1=ALU.mult), rd=[th, t12], wr=[mb])

        def tailA(i):
            b = i % 2
            pt = self.PT[b]
            mb, mT = self.m_mb[b], self.m_mT[b]
            for k2 in range(2):
                P.add("pe", lambda e, k2=k2: e.transpose(out=pt[:, k2 * 128:(k2 + 1) * 128], in_=mb[:, k2 * 128:(k2 + 1) * 128], identity=self.ident[:]), rd=[mb, self.ident], wr=[pt])
            P.add("act", lambda e: e.copy(out=mT[:], in_=pt[:, 0:256]), rd=[pt], wr=[mT])

        def tailB(i):
            b = i % 2
            py = [self.PS[3 * b + 1], self.PS[3 * b + 2]]
            mT = self.m_mT[b]
            for ch in range(2):
                pyc = py[ch]
                for k2 in range(2):
                    P.add("pe", lambda e, pyc=pyc, k2=k2, ch=ch: e.matmul(pyc[:], lhsT=mT[:, k2 * 128:(k2 + 1) * 128], rhs=self.m_wo[:, k2, ch * 512:(ch + 1) * 512], start=(k2 == 0), stop=(k2 == 1)), rd=[mT, self.m_wo], wr=[pyc])
                P.add("dve", lambda e, pyc=pyc, ch=ch: e.tensor_tensor(out=self.X[i][:, ch * 512:(ch + 1) * 512], in0=pyc[:], in1=self.X[i][:, ch * 512:(ch + 1) * 512], op=ALU.add), rd=[pyc, self.X[i]], wr=[self.X[i]])

        head(0)
        head(1)
        tailA(0)
        for j in range(1, NT):
            if j + 1 < NT:
                head(j + 1)
            tailB(j - 1)
            tailA(j)
        tailB(NT - 1)

    def ml_gates_proj(self, groups):
        P = self.P
        for d in range(2):
            A, B, G, sm = self.r_rows[d]
            for g in groups:
                pi, pf = self.PS[2 * d], self.PS[2 * d + 1]
                self.proj_feat(pi, lambda kc, d=d: (self.wsm[:, kc, 32 + d * 8:32 + d * 8 + 4], self.wsm), g, M=4)
                self.proj_feat(pf, lambda kc, d=d: (self.wsm[:, kc, 32 + d * 8 + 4:32 + d * 8 + 8], self.wsm), g, M=4)
                sl = slice(g * 512, (g + 1) * 512)
                P.add("act", lambda e, pi=pi, sl=sl, d=d: e.activation(out=A[:, sl], in_=pi[0:4, :], func=AF.Identity, bias=self.pcol[0:4, 64 + 2 * d:65 + 2 * d]), rd=[pi, self.pcol], wr=[A])
                P.add("act", lambda e, pf=pf, sl=sl, d=d: e.activation(out=B[:, sl], in_=pf[0:4, :], func=AF.Exp, scale=-1.0, bias=self.nbif[:, 2 * d + 1:2 * d + 2]), rd=[pf, self.nbif], wr=[B])

    def ml_gates(self):
        P = self.P
        lnf = float(math.log(math.sqrt(128.0)))
        for d in range(2):
            A, B, G, sm = self.r_rows[d]
            P.add("act", lambda e: e.activation(out=B[:], in_=B[:], func=AF.Ln, bias=1.0), rd=[B], wr=[B])
            rv = (lambda a: a) if d == 0 else (lambda a: a[:, ::-1])
            P.add("dve", lambda e, rv=rv: e.tensor_tensor_scan(out=rv(B[:]), data0=self.ones1[0:4, 0:1].to_broadcast([4, L]), data1=rv(B[:]), initial=0.0, op0=ALU.mult, op1=ALU.add), rd=[B, self.ones1], wr=[B])
            P.add("dve", lambda e: e.tensor_tensor(out=A[:], in0=A[:], in1=B[:], op=ALU.add), rd=[A, B], wr=[A])
            P.add("dve", lambda e, rv=rv: e.tensor_tensor_scan(out=rv(G[:]), data0=self.ones1[0:4, 0:1].to_broadcast([4, L]), data1=rv(A[:]), initial=-1e30, op0=ALU.mult, op1=ALU.max), rd=[A, self.ones1], wr=[G])
            gend = G[:, 127::128] if d == 0 else G[:, 0::128]
            P.add("dve", lambda e, gend=gend: e.tensor_copy(out=sm[:, 0:16], in_=gend), rd=[G], wr=[sm])
            P.add("dve", lambda e: e.memset(sm[:, 16:32], -1e30), rd=[sm], wr=[sm])
            if d == 0:
                P.add("dve", lambda e: e.tensor_copy(out=sm[:, 17:32], in_=sm[:, 0:15]), rd=[sm], wr=[sm])
            else:
                P.add("dve", lambda e: e.tensor_copy(out=sm[:, 16:31], in_=sm[:, 1:16]), rd=[sm], wr=[sm])
            P.add("dve", lambda e: e.tensor_tensor(out=sm[:, 32:48], in0=sm[:, 16:32], in1=sm[:, 0:16], op=ALU.subtract), rd=[sm], wr=[sm])
            P.add("act", lambda e: e.activation(out=sm[:, 32:48], in_=sm[:, 32:48], func=AF.Exp), rd=[sm], wr=[sm])
            gb_ = sm[:, 0:16].unsqueeze(2).to_broadcast([4, 16, 128])
            v3 = lambda t: t[:].rearrange("p (c t) -> p c t", c=16)
            P.add("dve", lambda e, gb_=gb_: e.tensor_tensor(out=v3(A), in0=v3(A), in1=gb_, op=ALU.subtract), rd=[A, sm], wr=[A])
            P.add("act", lambda e: e.activation(out=A[:], in_=A[:], func=AF.Exp), rd=[A], wr=[A])
            P.add("dve", lambda e, gb_=gb_: e.tensor_tensor(out=v3(B), in0=v3(B), in1=gb_, op=ALU.subtract), rd=[B, sm], wr=[B])
            P.add("act", lambda e: e.activation(out=B[:], in_=B[:], func=AF.Exp, bias=lnf), rd=[B], wr=[B])
        for d in range(2):
            A, B, G, sm = self.r_rows[d]
            pw, pe_ = self.PS[4], self.PS[5]
            for i in range(NT):
                P.add("pe", lambda e, i=i, pw=pw: e.matmul(pw[:, i * 4:(i + 1) * 4], lhsT=A[:, i * 128:(i + 1) * 128], rhs=self.identf[0:4, 0:4], start=True, stop=True), rd=[A, self.identf], wr=[pw])
                P.add("pe", lambda e, i=i, pe_=pe_: e.matmul(pe_[:, i * 4:(i + 1) * 4], lhsT=B[:, i * 128:(i + 1) * 128], rhs=self.identf[0:4, 0:4], start=True, stop=True), rd=[B, self.identf], wr=[pe_])
            P.add("dve", lambda e, d=d, pw=pw: e.tensor_copy(out=self.m_wst[d][:], in_=pw[:, 0:64]), rd=[pw], wr=[self.m_wst[d]])
            P.add("dve", lambda e, d=d, pe_=pe_: e.tensor_copy(out=self.m_en2[d][:], in_=pe_[:, 0:64]), rd=[pe_], wr=[self.m_en2[d]])
            pd = self.PS[d]
            for h in range(4):
                P.add("pe", lambda e, h=h, pd=pd: e.matmul(pd[:, h * 16:(h + 1) * 16], lhsT=self.sel[h][:], rhs=sm[:, 32:48], start=True, stop=True), rd=[self.sel[h], sm], wr=[pd])
            P.add("dve", lambda e, d=d, pd=pd: e.tensor_copy(out=self.m_decb[:, d * 64:(d + 1) * 64], in_=pd[:, 0:64]), rd=[pd], wr=[self.m_decb])

    def ml_unit(self, h):
        P = self.P
        self.load_unit_weights(O_MQK + h * 128, O_MQK + 512 + h * 128, O_MV + h * 256, O_MO + h * 256, O_GB + h * 256, h, 5)
        pre, cv = self.l_pre, self.l_cv
        for qk, dstT in enumerate((self.l_qTa, self.l_kTa)):
            P.add("dve", lambda e: e.memset(pre[:, 0:1], 0.0), rd=[pre], wr=[pre])
            P.add("dve", lambda e: e.memset(pre[:, L + 1:L + 2], 0.0), rd=[pre], wr=[pre])
            for g in range(4):
                pp = self.PS[g % 2]
                self.proj_feat(pp, lambda kc, qk=qk: (self.m_W1[:, kc, qk * 128:(qk + 1) * 128], self.m_W1), g)
                P.add("act", lambda e, pp=pp, g=g: e.copy(out=pre[:, 1 + g * 512:1 + (g + 1) * 512], in_=pp[:]), rd=[pp], wr=[pre])
            cb = 32 + (h * 2 + qk) * 4
            pc = self.pcol
            P.add("dve", lambda e, cb=cb: e.tensor_scalar(out=cv[:], in0=pre[:, 1:L + 1], scalar1=pc[:, cb + 1:cb + 2], scalar2=pc[:, cb + 3:cb + 4], op0=ALU.mult, op1=ALU.add), rd=[pre, pc], wr=[cv])
            P.add("dve", lambda e, cb=cb: e.scalar_tensor_tensor(out=cv[:], in0=pre[:, 0:L], scalar=pc[:, cb:cb + 1], in1=cv[:], op0=ALU.mult, op1=ALU.add), rd=[pre, pc, cv], wr=[cv])
            P.add("dve", lambda e, cb=cb: e.scalar_tensor_tensor(out=cv[:], in0=pre[:, 2:L + 2], scalar=pc[:, cb + 2:cb + 3], in1=cv[:], op0=ALU.mult, op1=ALU.add), rd=[pre, pc, cv], wr=[cv])
            P.add("act", lambda e, dstT=dstT: e.activation(out=dstT[:], in_=cv[:], func=AF.Silu), rd=[cv], wr=[dstT])
        for i in range(NT):
            pv = self.PS[2 + i % 2]
            self.proj_tok(pv[:, 0:256], pv, self.m_W2, i, 0, 256)
            for d in range(2):
                vw = self.l_vw[d]
                P.add("act", lambda e, vw=vw, pv=pv, i=i, d=d: e.activation(out=vw[:, i, 0:256], in_=pv[:, 0:256], func=AF.Copy, scale=self.m_wst[d][:, i * 4 + h:i * 4 + h + 1]), rd=[pv, self.m_wst[d]], wr=[vw])
        for g in range(4):
            pt = self.PT[g % 2]
            for c in range(4):
                sl = slice((g * 4 + c) * 128, (g * 4 + c + 1) * 128)
                P.add("pe", lambda e, pt=pt, sl=sl, c=c: e.transpose(out=pt[:, c * 128:(c + 1) * 128], in_=self.l_kTa[:, sl], identity=self.ident[:]), rd=[self.l_kTa, self.ident], wr=[pt])
            P.add("act", lambda e, pt=pt, g=g: e.copy(out=self.l_ktok[:, g * 4:(g + 1) * 4, :], in_=pt[:, 0:512].rearrange("p (t d) -> p t d", t=4)), rd=[pt], wr=[self.l_ktok])
        for d in range(2):
            vw = self.l_vw[d]
            wcol = self.m_wst[d][:].rearrange("p (t h) -> p t h", h=4)[:, :, h:h + 1]
            P.add("dve", lambda e, vw=vw, wcol=wcol: e.tensor_copy(out=vw[:, :, 256:257], in_=wcol), rd=[self.m_wst[d]], wr=[vw])
        self.load_late_weights()
        uss, urs = self.ss[self.ssk], self.rs[self.ssk]
        self.ssk ^= 1
        for d in range(2):
            P.add("dve", lambda e, d=d: e.memset(self.l_C32[d][:], 0.0), wr=[self.l_C32[d]])
        written = [False] * NT

        def mkctx(step):
            ctx = []
            for d in range(2):
                c = step if d == 0 else NT - 1 - step
                ctx.append(dict(d=d, c=c, sl=slice(c * 128, (c + 1) * 128), pa=self.PS[3 * d], pn=self.PS[3 * d + 1], pu=self.PS[3 * d + 2],
                                sT=self.l_sT[d], C32=self.l_C32[d], Cb=self.l_Cb[d], dn=self.l_dn[d], vw=self.l_vw[d],
                                mask=self.maskF if d == 0 else self.maskB, dcol=d * 64 + h * 16 + c, ecol=c * 4 + h))
            return ctx

        def front_pe(ctx):
            for k in ctx:
                pa, sl, pu, vw, c = k["pa"], k["sl"], k["pu"], k["vw"], k["c"]
                P.add("pe", lambda e, pa=pa, sl=sl: e.matmul(pa[:, 0:128], lhsT=self.l_kTa[:, sl], rhs=self.l_qTa[:, sl], start=True, stop=True), rd=[self.l_kTa, self.l_qTa], wr=[pa])
                P.add("pe", lambda e, pu=pu, vw=vw, c=c: e.matmul(pu[:, 0:257], lhsT=self.l_ktok[:, c, :], rhs=vw[:, c, 0:257], start=True, stop=True), rd=[self.l_ktok, vw], wr=[pu])

        def front_ev(ctx):
            for k in ctx:
                sT, pa, mask, Cb, C32, dcol = k["sT"], k["pa"], k["mask"], k["Cb"], k["C32"], k["dcol"]
                P.add("dve", lambda e, sT=sT, pa=pa, mask=mask: e.tensor_tensor(out=sT[:], in0=pa[:, 0:128], in1=mask[:], op=ALU.mult), rd=[pa, mask], wr=[sT])
                P.add("act", lambda e, Cb=Cb, C32=C32, dcol=dcol: e.activation(out=Cb[:, 0:257], in_=C32[:, 0:257], func=AF.Copy, scale=self.m_decb[:, dcol:dcol + 1]), rd=[C32, self.m_decb], wr=[Cb])

        cur = mkctx(0)
        front_pe(cur)
        front_ev(cur)
        for step in range(NT):
            ctx = cur
            for k in ctx:
                pn, sT, vw, c, Cb, sl = k["pn"], k["sT"], k["vw"], k["c"], k["Cb"], k["sl"]
                P.add("pe", lambda e, pn=pn, sT=sT, vw=vw, c=c: e.matmul(pn[:, 0:257], lhsT=sT[:], rhs=vw[:, c, 0:257], start=True, stop=False), rd=[sT, vw], wr=[pn])
                P.add("pe", lambda e, pn=pn, Cb=Cb, sl=sl: e.matmul(pn[:, 0:257], lhsT=self.l_qTa[:, sl], rhs=Cb[:, 0:257], start=False, stop=True), rd=[self.l_qTa, Cb], wr=[pn])
            for k in ctx:
                C32, pu, dcol = k["C32"], k["pu"], k["dcol"]
                P.add("dve", lambda e, C32=C32, pu=pu, dcol=dcol: e.scalar_tensor_tensor(out=C32[:, 0:257], in0=C32[:, 0:257], scalar=self.m_decb[:, dcol:dcol + 1], in1=pu[:, 0:257], op0=ALU.mult, op1=ALU.add), rd=[C32, pu, self.m_decb], wr=[C32])
            if step + 1 < NT:
                cur = mkctx(step + 1)
                front_pe(cur)
                front_ev(cur)
            for k in ctx:
                dn, pn, d, ecol, c = k["dn"], k["pn"], k["d"], k["ecol"], k["c"]
                P.add("dve", lambda e, dn=dn, pn=pn, d=d, ecol=ecol: e.tensor_scalar(out=dn[:, 2:3], in0=pn[:, 256:257], scalar1=self.m_en2[d][:, ecol:ecol + 1], scalar2=None, op0=ALU.max), rd=[pn, self.m_en2[d]], wr=[dn])
                P.add("dve", lambda e, dn=dn, pn=pn: e.scalar_tensor_tensor(out=dn[:, 0:1], in0=pn[:, 256:257], scalar=-1.0, in1=dn[:, 2:3], op0=ALU.mult, op1=ALU.max), rd=[pn, dn], wr=[dn])
                P.add("dve", lambda e, dn=dn: e.reciprocal(out=dn[:, 1:2], in_=dn[:, 0:1]), rd=[dn], wr=[dn])
                hc = self.l_h[c]
                if not written[c]:
                    written[c] = True
                    P.add("act", lambda e, hc=hc, pn=pn, dn=dn: e.activation(out=hc[:], in_=pn[:, 0:256], func=AF.Copy, scale=dn[:, 1:2]), rd=[pn, dn], wr=[hc])
                else:
                    P.add("dve", lambda e, hc=hc, pn=pn, dn=dn: e.scalar_tensor_tensor(out=hc[:], in0=pn[:, 0:256], scalar=dn[:, 1:2], in1=hc[:], op0=ALU.mult, op1=ALU.add), rd=[pn, dn, hc], wr=[hc])
                    P.add("act", lambda e, hc=hc, c=c: e.activation(out=self.m_sq[:], in_=hc[:], func=AF.Square, accum_out=uss[:, c:c + 1]), rd=[hc], wr=[self.m_sq, uss])
        self.gating(self.l_h, silu_first=False, ssrs=(uss, urs))


_CACHE = {}


def pack_small(inp):
    f = np.float32
    pcol = np.zeros((128, 72), f)
    pcol[:, 0:8] = np.asarray(inp["g_ffn1"], f).reshape(8, 128).T
    pcol[:, 8:16] = np.asarray(inp["g_mix"], f).reshape(8, 128).T
    pcol[:, 16:24] = np.asarray(inp["g_ffn2"], f).reshape(8, 128).T
    pcol[:, 24:28] = np.asarray(inp["gla_b2_fwd"], f).reshape(4, 128).T
    pcol[:, 28:32] = np.asarray(inp["gla_b2_bwd"], f).reshape(4, 128).T
    cw = np.asarray(inp["ml_conv_w"], f).reshape(3, 2, 4, 128)
    cb = np.asarray(inp["ml_conv_b"], f).reshape(2, 4, 128)
    for h in range(4):
        for qk in range(2):
            b = 32 + (h * 2 + qk) * 4
            for j in range(3):
                pcol[:, b + j] = cw[j, qk, h]
            pcol[:, b + 3] = cb[qk, h]
    pcol[0:4, 64:68] = np.asarray(inp["ml_b_if"], f).reshape(4, 4).T
    w2g = np.concatenate([np.asarray(inp["gla_w2_fwd"], f).reshape(16, 512), np.asarray(inp["gla_w2_bwd"], f).reshape(16, 512)], axis=1)
    gvec = np.stack([np.asarray(inp[k], f).reshape(1024) for k in ("g_ffn1", "g_mix", "g_ffn2", "g_final", "gla_norm", "ml_norm")], 0)
    return pcol, np.ascontiguousarray(w2g), np.ascontiguousarray(gvec)


def run(inputs, xs_per_core, nseq, stages=("ffn1", "mix", "ffn2"), units=None, ncores=8):
    key = (nseq, tuple(stages), None if units is None else tuple(units))
    if key not in _CACHE:
        _CACHE[key] = Builder(nseq, stages, units).build()
    nc = _CACHE[key]
    pcol, w2g, gvec = pack_small(inputs)
    f = np.float32
    shared = {
        "w_ffn1_in": np.ascontiguousarray(np.asarray(inputs["w_ffn1_in"], f).reshape(D, 2 * DFF)),
        "w_ffn1_out": np.ascontiguousarray(np.asarray(inputs["w_ffn1_out"], f).reshape(DFF, D)),
        "w_ffn2_in": np.ascontiguousarray(np.asarray(inputs["w_ffn2_in"], f).reshape(D, 2 * DFF)),
        "w_ffn2_out": np.ascontiguousarray(np.asarray(inputs["w_ffn2_out"], f).reshape(DFF, D)),
        "w_in": np.ascontiguousarray(np.asarray(inputs["w_in"], f).reshape(D, DIN)),
        "w_out": np.ascontiguousarray(np.asarray(inputs["w_out"], f).reshape(D, D)),
        "pcol": pcol, "w2g": w2g, "gvec": gvec,
    }
    in_maps = [dict(shared, x=np.ascontiguousarray(xs_per_core[c])) for c in range(ncores)]
    res = run_bass_kernel_spmd(nc, in_maps, core_ids=list(range(ncores)))
    return [res.results[c]["y"] for c in range(ncores)]


def kernel(**inputs):
    xp = np.asarray(inputs["x_prompt"], np.float32)
    xs = np.asarray(inputs["x_sample"], np.float32)
    per_core = []
    for c in range(8):
        per_core.append(np.concatenate([xp[4 * c:4 * c + 4].reshape(4 * L, D), xs[2 * c:2 * c + 2].reshape(2 * L, D)], axis=0))
    ys = run(inputs, per_core, 6)
    yp = np.stack([ys[c][0:4 * L].reshape(4, L, D) for c in range(8)], 0).reshape(32, L, D)
    ysm = np.stack([ys[c][4 * L:6 * L].reshape(2, L, D) for c in range(8)], 0).reshape(16, L, D)
    return (np.ascontiguousarray(yp, dtype=np.float32), np.ascontiguousarray(ysm, dtype=np.float32))
```
